# Optimizing a Trainium2 kernel written in Bass

```python
import math
import jax
import jax.numpy as jnp
from jax import lax
import numpy as np

D_MODEL = 2048
BATCH = 4
SEQ = 2048
DEPTH = 2
DEC_BATCH = 128
DEC_SEQ = 1
PAST_LEN = 16384
PAGE_SIZE = 128

D_MIX = D_MODEL
D_GROUP = D_MIX // 4
RWKV_D = D_GROUP
RWKV_HD = 64
RWKV_H = RWKV_D // RWKV_HD
RWKV_R_DECAY = 96
RWKV_R_A = 96
RWKV_R_GATE = 256
RWKV_PROJ = 3 * RWKV_D + RWKV_R_DECAY + RWKV_R_A + RWKV_R_GATE
RWKV_SPLITS = (RWKV_D, 2 * RWKV_D, 3 * RWKV_D, 3 * RWKV_D + RWKV_R_DECAY, 3 * RWKV_D + RWKV_R_DECAY + RWKV_R_A)
RWKV_LN_EPS = 64e-5
S5_D = D_GROUP
S5_CH = 16
S5_G = S5_D // S5_CH
S5_N = 64
GDN_D = D_GROUP
GDN_HD = 128
GDN_H = GDN_D // GDN_HD
GDN_CONV = 4
GDN_CHUNK = 64
GDN_PROJ = 4 * GDN_D + 2 * GDN_H
LRU_D = D_MIX - RWKV_D - S5_D - GDN_D
LRU_BLOCKS = 8
LRU_BS = LRU_D // LRU_BLOCKS
LRU_CONV = 4
LRU_C = 8.0
LRU_PROJ = 2 * LRU_D
IN_SPLITS = (RWKV_PROJ, RWKV_PROJ + S5_D, RWKV_PROJ + S5_D + GDN_PROJ)
N_IN = RWKV_PROJ + S5_D + GDN_PROJ + LRU_PROJ
D_FF = 5632
FFN_CONV = 3
NORM_EPS = 1e-6
STATE_NAMES = ('rwkv_wkv', 'rwkv_shift', 's5_re', 's5_im', 'gdn', 'gdn_conv', 'lru_h', 'lru_conv', 'ffn_conv')
F32 = jnp.float32

kernel_name = 'hybrid_parallel_heads_decoder_step'


def rms_norm(x, g, eps=NORM_EPS):
    xf = x.astype(F32)
    return xf * lax.rsqrt(jnp.mean(xf * xf, axis=-1, keepdims=True) + eps) * g.astype(F32)


def l2_normalize(x, eps=1e-12):
    xf = x.astype(F32)
    return xf * lax.rsqrt(jnp.sum(xf * xf, axis=-1, keepdims=True) + eps)


def causal_dwconv(x, buf, w):
    L = x.shape[1]
    xp = jnp.concatenate([buf.astype(F32), x.astype(F32)], axis=1)
    y = lax.conv_general_dilated(xp, w.astype(F32)[:, None, :], window_strides=(1,), padding='VALID',
                                 dimension_numbers=('NWC', 'WIO', 'NWC'), feature_group_count=x.shape[-1])
    return y, xp[:, L:]


def real_linear_scan(a, b, h0):
    b = b.at[:, 0].add(a[:, 0] * h0)

    def comb(e1, e2):
        a1, b1 = e1
        a2, b2 = e2
        return a1 * a2, a2 * b1 + b2

    _, h = lax.associative_scan(comb, (a, b), axis=1)
    return h


def complex_linear_scan(a_re, a_im, b_re, b_im, h0_re, h0_im):
    b_re = b_re.at[:, 0].add(a_re * h0_re - a_im * h0_im)
    b_im = b_im.at[:, 0].add(a_re * h0_im + a_im * h0_re)
    a_re = jnp.broadcast_to(a_re, b_re.shape)
    a_im = jnp.broadcast_to(a_im, b_im.shape)

    def comb(e1, e2):
        ar1, ai1, br1, bi1 = e1
        ar2, ai2, br2, bi2 = e2
        return (ar2 * ar1 - ai2 * ai1, ar2 * ai1 + ai2 * ar1,
                ar2 * br1 - ai2 * bi1 + br2, ar2 * bi1 + ai2 * br1 + bi2)

    _, _, h_re, h_im = lax.associative_scan(comb, (a_re, a_im, b_re, b_im), axis=1)
    return h_re, h_im


def rwkv7_mixer(p_in, s0, shift0, p):
    bsz, L, _ = p_in.shape
    p_in = p_in.astype(F32)
    prev = jnp.concatenate([shift0[:, None].astype(F32), p_in[:, :-1]], axis=1)
    xm = p_in + p['rwkv_mu'] * (prev - p_in)
    r, k, v, xw, xa, xg = jnp.split(xm, RWKV_SPLITS, axis=-1)
    log_w = -jnp.exp(-jax.nn.softplus(-(p['rwkv_w0'] + jnp.tanh(xw) @ p['rwkv_w_up'])) - 0.5)
    a = jax.nn.sigmoid(p['rwkv_a0'] + xa @ p['rwkv_a_up'])
    g = jax.nn.sigmoid(xg) @ p['rwkv_g_up']

    def heads(t):
        return t.reshape(bsz, L, RWKV_H, RWKV_HD)

    r, k, v, a, w = heads(r), heads(k), heads(v), heads(a), jnp.exp(heads(log_w))
    kk = l2_normalize(k * p['rwkv_k_k'].reshape(RWKV_H, RWKV_HD))
    k = k * (1.0 + (a - 1.0) * p['rwkv_k_a'].reshape(RWKV_H, RWKV_HD))

    def step(S, inp):
        r_t, w_t, k_t, v_t, kk_t, a_t = inp
        S = (S * w_t[..., None]
             - (kk_t * a_t)[..., None] * jnp.einsum('bhk,bhkv->bhv', kk_t, S)[:, :, None, :]
             + k_t[..., None] * v_t[:, :, None, :])
        return S, jnp.einsum('bhk,bhkv->bhv', r_t, S)

    xs = tuple(jnp.swapaxes(t, 0, 1) for t in (r, w, k, v, kk, a))
    S, y = lax.scan(step, s0.astype(F32), xs)
    y = jnp.swapaxes(y, 0, 1)
    mean = jnp.mean(y, axis=-1, keepdims=True)
    var = jnp.mean(jnp.square(y - mean), axis=-1, keepdims=True)
    y = ((y - mean) * lax.rsqrt(var + RWKV_LN_EPS)).reshape(bsz, L, RWKV_D) * p['rwkv_ln_w'] + p['rwkv_ln_b']
    bonus = jnp.sum(r * k * p['rwkv_r_k'], axis=-1, keepdims=True) * v
    y = (y + bonus.reshape(bsz, L, RWKV_D)) * g
    return y, S, p_in[:, -1]


def s5_mixer(u, h0_re, h0_im, p):
    bsz, L, _ = u.shape
    u = u.astype(F32)
    ug = u.reshape(bsz, L, S5_G, S5_CH)
    lr = p['s5_lambda_re'].astype(F32)
    li = p['s5_lambda_im'].astype(F32)
    dt = jnp.exp(p['s5_log_dt'].astype(F32))[:, None]
    mag = jnp.exp(lr * dt)
    ab_re, ab_im = mag * jnp.cos(li * dt), mag * jnp.sin(li * dt)
    den = lr * lr + li * li
    nr = ab_re - 1.0
    f_re = (nr * lr + ab_im * li) / den
    f_im = (ab_im * lr - nr * li) / den
    b_re, b_im = p['s5_b_re'], p['s5_b_im']
    bb_re = f_re[..., None] * b_re - f_im[..., None] * b_im
    bb_im = f_re[..., None] * b_im + f_im[..., None] * b_re
    bu_re = jnp.einsum('gnc,blgc->blgn', bb_re, ug)
    bu_im = jnp.einsum('gnc,blgc->blgn', bb_im, ug)
    h_re, h_im = complex_linear_scan(ab_re, ab_im, bu_re, bu_im, h0_re.astype(F32), h0_im.astype(F32))
    y = (jnp.einsum('gcn,blgn->blgc', p['s5_c_re'], h_re)
         - jnp.einsum('gcn,blgn->blgc', p['s5_c_im'], h_im)).reshape(bsz, L, S5_D)
    y = y + p['s5_d'] * u
    z = jax.nn.gelu(y)
    return z * jax.nn.sigmoid(z @ p['s5_glu_w'] + p['s5_glu_b']), h_re[:, -1], h_im[:, -1]


def gated_delta_rule(q, k, v, g, beta, s0):
    bsz, L, H, K = q.shape
    V = v.shape[-1]
    C = GDN_CHUNK
    n = -(-L // C)
    pad = n * C - L

    def blocks(t):
        t = jnp.pad(t.astype(F32), [(0, 0), (0, pad)] + [(0, 0)] * (t.ndim - 2))
        t = t.reshape((bsz, n, C) + t.shape[2:])
        return jnp.moveaxis(t, 3, 1)

    q, k, v, g, beta = blocks(q), blocks(k), blocks(v), blocks(g), blocks(beta)
    q = q * (K ** -0.5)
    kb = k * beta[..., None]
    vb = v * beta[..., None]
    gc = jnp.cumsum(g, axis=-1)
    causal = jnp.tril(jnp.ones((C, C), dtype=bool))
    strict = jnp.tril(jnp.ones((C, C), dtype=bool), -1)
    diff = gc[..., :, None] - gc[..., None, :]
    decay = jnp.where(causal, jnp.exp(jnp.where(causal, diff, 0.0)), 0.0)
    lmat = jnp.where(strict, jnp.einsum('bhnck,bhndk->bhncd', kb, k) * decay, 0.0)
    eye = jnp.eye(C, dtype=F32)
    rhs = jnp.concatenate([vb, kb * jnp.exp(gc)[..., None]], axis=-1)
    sol = lax.linalg.triangular_solve(lmat + eye, rhs, left_side=True, lower=True, unit_diagonal=True)
    u, wk = sol[..., :V], sol[..., V:]
    attn = jnp.where(causal, jnp.einsum('bhnck,bhndk->bhncd', q, k) * decay, 0.0)
    q_dec = q * jnp.exp(gc)[..., None]
    k_tail = k * jnp.exp(gc[..., -1:] - gc)[..., None]
    g_tot = jnp.exp(gc[..., -1])
    xs = tuple(jnp.moveaxis(t, 2, 0) for t in (u, wk, attn, q_dec, k_tail, g_tot))

    def step(S, inp):
        u_i, wk_i, attn_i, qd_i, kt_i, gt_i = inp
        v_new = u_i - jnp.einsum('bhck,bhkv->bhcv', wk_i, S)
        o_i = jnp.einsum('bhck,bhkv->bhcv', qd_i, S) + jnp.einsum('bhcd,bhdv->bhcv', attn_i, v_new)
        S = S * gt_i[..., None, None] + jnp.einsum('bhck,bhcv->bhkv', kt_i, v_new)
        return S, o_i

    S, o = lax.scan(step, s0.astype(F32), xs)
    o = jnp.moveaxis(o, 0, 2).reshape(bsz, H, n * C, V)[:, :, :L]
    return jnp.swapaxes(o, 1, 2), S


def gdn_mixer(pc, s0, conv0, p):
    bsz, L, _ = pc.shape
    qkv, z, a_in, b_in = jnp.split(pc.astype(F32), [3 * GDN_D, 4 * GDN_D, 4 * GDN_D + GDN_H], axis=-1)
    qkv, conv_new = causal_dwconv(qkv, conv0, p['gdn_conv_w'])
    q, k, v = jnp.split(jax.nn.silu(qkv), 3, axis=-1)

    def heads(t):
        return t.reshape(bsz, L, GDN_H, GDN_HD)

    q, k, v = l2_normalize(heads(q), 1e-6), l2_normalize(heads(k), 1e-6), heads(v)
    g = -jnp.exp(p['gdn_a_log']) * jax.nn.softplus(a_in + p['gdn_dt_bias'])
    beta = jax.nn.sigmoid(b_in)
    o, s_new = gated_delta_rule(q, k, v, g, beta, s0)
    o = rms_norm(o, p['gdn_norm_w']) * jax.nn.silu(heads(z))
    return o.reshape(bsz, L, GDN_D), s_new, conv_new


def rglru_mixer(pd, h0, conv0, p):
    bsz, L, _ = pd.shape
    xr, gate = jnp.split(pd.astype(F32), 2, axis=-1)
    xc, conv_new = causal_dwconv(xr, conv0, p['lru_conv_w'])
    xc = xc + p['lru_conv_b']
    xb = xc.reshape(bsz, L, LRU_BLOCKS, LRU_BS)

    def blockdiag(w, b):
        return jnp.einsum('blhi,hij->blhj', xb, w).reshape(bsz, L, LRU_D) + b

    r = jax.nn.sigmoid(blockdiag(p['lru_wr'], p['lru_br']))
    i = jax.nn.sigmoid(blockdiag(p['lru_wi'], p['lru_bi']))
    log_a = -LRU_C * r * jax.nn.softplus(-p['lru_lambda'])
    a = jnp.exp(log_a)
    b = jnp.sqrt(-jnp.expm1(2.0 * log_a)) * (i * xc)
    h = real_linear_scan(a, b, h0.astype(F32))
    return h * jax.nn.gelu(gate), h[:, -1], conv_new


def conv_ffn(h, buf, p):
    up = h @ p['ffn_w_up']
    up, buf_new = causal_dwconv(up, buf, p['ffn_conv_w'])
    gate, val = jnp.split(up + p['ffn_conv_b'], 2, axis=-1)
    return (jax.nn.silu(gate) * val) @ p['ffn_w_down'], buf_new


def trunk_layer(x, c, st, p):
    mod = jax.nn.silu(c.astype(F32)) @ p['ada_w'] + p['ada_b']
    sh1, sc1, g1, sh2, sc2, g2 = (m[:, None, :] for m in jnp.split(mod, 6, axis=-1))
    h = rms_norm(x, p['norm1_g']) * (1.0 + sc1) + sh1
    proj = h @ p['w_in']
    pa, pb, pc, pd = jnp.split(proj, IN_SPLITS, axis=-1)
    ya, s_wkv, s_shift = rwkv7_mixer(pa, st['rwkv_wkv'], st['rwkv_shift'], p)
    yb, s_re, s_im = s5_mixer(pb, st['s5_re'], st['s5_im'], p)
    yc, s_gdn, s_gdn_conv = gdn_mixer(pc, st['gdn'], st['gdn_conv'], p)
    yd, s_lru, s_lru_conv = rglru_mixer(pd, st['lru_h'], st['lru_conv'], p)
    x = x + g1 * (jnp.concatenate([ya, yb, yc, yd], axis=-1) @ p['w_out'])
    h = rms_norm(x, p['norm2_g']) * (1.0 + sc2) + sh2
    f, s_ffn = conv_ffn(h, st['ffn_conv'], p)
    x = x + g2 * f
    new = {'rwkv_wkv': s_wkv, 'rwkv_shift': s_shift, 's5_re': s_re, 's5_im': s_im, 'gdn': s_gdn,
           'gdn_conv': s_gdn_conv, 'lru_h': s_lru, 'lru_conv': s_lru_conv, 'ffn_conv': s_ffn}
    return x, new


def setup_inputs(seed: int = 0) -> dict:
    key = jax.random.key(seed)
    keys = iter(jax.random.split(key, 96))

    def nrm(shape, scale):
        return scale * jax.random.normal(next(keys), shape, F32)

    def unif(shape, lo, hi):
        return jax.random.uniform(next(keys), shape, F32, lo, hi)

    L, D = DEPTH, D_MODEL
    s5_lam_im = jnp.pi * jnp.arange(S5_N, dtype=F32) + nrm((L, S5_G, S5_N), 0.01)
    gdn_dt = jnp.exp(unif((L, GDN_H), math.log(1e-3), math.log(1e-1)))
    lru_a = unif((L, LRU_D), 0.9, 0.999) ** (1.0 / LRU_C)
    return {
        'x_prompt': nrm((BATCH, SEQ, D), 1.0),
        'x_sample': nrm((DEC_BATCH, DEC_SEQ, D), 1.0),
        'c_prompt': nrm((BATCH, D), 1.0),
        'c_sample': nrm((DEC_BATCH, D), 1.0),
        'state_rwkv_wkv': nrm((L, DEC_BATCH, RWKV_H, RWKV_HD, RWKV_HD), 0.5),
        'state_rwkv_shift': nrm((L, DEC_BATCH, RWKV_PROJ), 1.0),
        'state_s5_re': nrm((L, DEC_BATCH, S5_G, S5_N), 0.3),
        'state_s5_im': nrm((L, DEC_BATCH, S5_G, S5_N), 0.3),
        'state_gdn': nrm((L, DEC_BATCH, GDN_H, GDN_HD, GDN_HD), 0.1),
        'state_gdn_conv': nrm((L, DEC_BATCH, GDN_CONV - 1, 3 * GDN_D), 1.0),
        'state_lru_h': nrm((L, DEC_BATCH, LRU_D), 0.5),
        'state_lru_conv': nrm((L, DEC_BATCH, LRU_CONV - 1, LRU_D), 1.0),
        'state_ffn_conv': nrm((L, DEC_BATCH, FFN_CONV - 1, 2 * D_FF), 1.0),
        'ada_w': nrm((L, D, 6 * D), 0.3 * D ** -0.5),
        'ada_b': nrm((L, 6 * D), 0.02),
        'norm1_g': 1.0 + nrm((L, D), 0.02),
        'norm2_g': 1.0 + nrm((L, D), 0.02),
        'final_g': 1.0 + nrm((D,), 0.02),
        'w_in': nrm((L, D, N_IN), D ** -0.5),
        'w_out': nrm((L, D_MIX, D), D_MIX ** -0.5),
        'rwkv_mu': unif((L, RWKV_PROJ), 0.0, 1.0),
        'rwkv_w0': unif((L, RWKV_D), -6.0, 1.0),
        'rwkv_w_up': nrm((L, RWKV_R_DECAY, RWKV_D), 0.5 * RWKV_R_DECAY ** -0.5),
        'rwkv_a0': nrm((L, RWKV_D), 0.1),
        'rwkv_a_up': nrm((L, RWKV_R_A, RWKV_D), 0.5 * RWKV_R_A ** -0.5),
        'rwkv_g_up': nrm((L, RWKV_R_GATE, RWKV_D), RWKV_R_GATE ** -0.5),
        'rwkv_k_k': 0.85 + nrm((L, RWKV_D), 0.02),
        'rwkv_k_a': 1.0 + nrm((L, RWKV_D), 0.02),
        'rwkv_r_k': nrm((L, RWKV_H, RWKV_HD), 0.1),
        'rwkv_ln_w': 1.0 + nrm((L, RWKV_D), 0.02),
        'rwkv_ln_b': nrm((L, RWKV_D), 0.02),
        's5_lambda_re': -0.5 + nrm((L, S5_G, S5_N), 0.01),
        's5_lambda_im': s5_lam_im,
        's5_log_dt': unif((L, S5_G), math.log(1e-3), math.log(1e-1)),
        's5_b_re': nrm((L, S5_G, S5_N, S5_CH), (2 * S5_CH) ** -0.5),
        's5_b_im': nrm((L, S5_G, S5_N, S5_CH), (2 * S5_CH) ** -0.5),
        's5_c_re': nrm((L, S5_G, S5_CH, S5_N), 0.5),
        's5_c_im': nrm((L, S5_G, S5_CH, S5_N), 0.5),
        's5_d': nrm((L, S5_D), 0.5),
        's5_glu_w': nrm((L, S5_D, S5_D), S5_D ** -0.5),
        's5_glu_b': nrm((L, S5_D), 0.02),
        'gdn_conv_w': nrm((L, GDN_CONV, 3 * GDN_D), 0.5),
        'gdn_a_log': jnp.log(unif((L, GDN_H), 1.0, 16.0)),
        'gdn_dt_bias': gdn_dt + jnp.log(-jnp.expm1(-gdn_dt)),
        'gdn_norm_w': 1.0 + nrm((L, GDN_HD), 0.02),
        'lru_conv_w': nrm((L, LRU_CONV, LRU_D), 0.5),
        'lru_conv_b': nrm((L, LRU_D), 0.02),
        'lru_wr': nrm((L, LRU_BLOCKS, LRU_BS, LRU_BS), LRU_BS ** -0.5),
        'lru_br': nrm((L, LRU_D), 0.02),
        'lru_wi': nrm((L, LRU_BLOCKS, LRU_BS, LRU_BS), LRU_BS ** -0.5),
        'lru_bi': nrm((L, LRU_D), 0.02),
        'lru_lambda': jnp.log(lru_a) - jnp.log1p(-lru_a),
        'ffn_w_up': nrm((L, D, 2 * D_FF), D ** -0.5),
        'ffn_conv_w': nrm((L, FFN_CONV, 2 * D_FF), 3 ** -0.5),
        'ffn_conv_b': nrm((L, 2 * D_FF), 0.02),
        'ffn_w_down': nrm((L, D_FF, D), D_FF ** -0.5),
    }


def reference(x_prompt, x_sample, c_prompt, c_sample,
              state_rwkv_wkv, state_rwkv_shift, state_s5_re, state_s5_im, state_gdn, state_gdn_conv,
              state_lru_h, state_lru_conv, state_ffn_conv,
              ada_w, ada_b, norm1_g, norm2_g, final_g, w_in, w_out,
              rwkv_mu, rwkv_w0, rwkv_w_up, rwkv_a0, rwkv_a_up, rwkv_g_up, rwkv_k_k, rwkv_k_a, rwkv_r_k,
              rwkv_ln_w, rwkv_ln_b,
              s5_lambda_re, s5_lambda_im, s5_log_dt, s5_b_re, s5_b_im, s5_c_re, s5_c_im, s5_d,
              s5_glu_w, s5_glu_b,
              gdn_conv_w, gdn_a_log, gdn_dt_bias, gdn_norm_w,
              lru_conv_w, lru_conv_b, lru_wr, lru_br, lru_wi, lru_bi, lru_lambda,
              ffn_w_up, ffn_conv_w, ffn_conv_b, ffn_w_down):
    weights = {
        'ada_w': ada_w, 'ada_b': ada_b, 'norm1_g': norm1_g, 'norm2_g': norm2_g, 'w_in': w_in, 'w_out': w_out,
        'rwkv_mu': rwkv_mu, 'rwkv_w0': rwkv_w0, 'rwkv_w_up': rwkv_w_up, 'rwkv_a0': rwkv_a0,
        'rwkv_a_up': rwkv_a_up, 'rwkv_g_up': rwkv_g_up, 'rwkv_k_k': rwkv_k_k, 'rwkv_k_a': rwkv_k_a,
        'rwkv_r_k': rwkv_r_k, 'rwkv_ln_w': rwkv_ln_w, 'rwkv_ln_b': rwkv_ln_b,
        's5_lambda_re': s5_lambda_re, 's5_lambda_im': s5_lambda_im, 's5_log_dt': s5_log_dt,
        's5_b_re': s5_b_re, 's5_b_im': s5_b_im, 's5_c_re': s5_c_re, 's5_c_im': s5_c_im, 's5_d': s5_d,
        's5_glu_w': s5_glu_w, 's5_glu_b': s5_glu_b,
        'gdn_conv_w': gdn_conv_w, 'gdn_a_log': gdn_a_log, 'gdn_dt_bias': gdn_dt_bias, 'gdn_norm_w': gdn_norm_w,
        'lru_conv_w': lru_conv_w, 'lru_conv_b': lru_conv_b, 'lru_wr': lru_wr, 'lru_br': lru_br,
        'lru_wi': lru_wi, 'lru_bi': lru_bi, 'lru_lambda': lru_lambda,
        'ffn_w_up': ffn_w_up, 'ffn_conv_w': ffn_conv_w, 'ffn_conv_b': ffn_conv_b, 'ffn_w_down': ffn_w_down,
    }
    cache = {'rwkv_wkv': state_rwkv_wkv, 'rwkv_shift': state_rwkv_shift, 's5_re': state_s5_re,
             's5_im': state_s5_im, 'gdn': state_gdn, 'gdn_conv': state_gdn_conv, 'lru_h': state_lru_h,
             'lru_conv': state_lru_conv, 'ffn_conv': state_ffn_conv}
    bp = x_prompt.shape[0]
    zero_state = {n: jnp.zeros((bp,) + cache[n].shape[2:], F32) for n in STATE_NAMES}
    xp, xs = x_prompt, x_sample
    new_p = {n: [] for n in STATE_NAMES}
    new_s = {n: [] for n in STATE_NAMES}
    for l in range(DEPTH):
        p = {name: arr[l] for name, arr in weights.items()}
        xp, sp = trunk_layer(xp, c_prompt, zero_state, p)
        xs, ss = trunk_layer(xs, c_sample, {n: cache[n][l] for n in STATE_NAMES}, p)
        for n in STATE_NAMES:
            new_p[n].append(sp[n])
            new_s[n].append(ss[n])
    y_prompt = rms_norm(xp, final_g)
    y_sample = rms_norm(xs, final_g)
    P = {n: jnp.stack(new_p[n], axis=0) for n in STATE_NAMES}
    S = {n: jnp.stack(new_s[n], axis=0) for n in STATE_NAMES}
    return (y_prompt, y_sample,
            P['rwkv_wkv'], P['rwkv_shift'], P['s5_re'], P['s5_im'], P['gdn'], P['gdn_conv'],
            P['lru_h'], P['lru_conv'], P['ffn_conv'],
            S['rwkv_wkv'], S['rwkv_shift'], S['s5_re'], S['s5_im'], S['gdn'], S['gdn_conv'],
            S['lru_h'], S['lru_conv'], S['ffn_conv'])
```

```python
import numpy as np
from contextlib import ExitStack
import concourse.bass as bass
import concourse.mybir as mybir
from concourse.bass_utils import run_bass_kernel_spmd

F32 = mybir.dt.float32
BF16 = mybir.dt.bfloat16
AF = mybir.ActivationFunctionType
ALU = mybir.AluOpType
AX = mybir.AxisListType

EPOCH = 4000
DMA_EPOCH = 1000


class Op:
    __slots__ = ("eng", "fn", "deps", "is_dma", "signal", "sem", "val", "idx", "dkey")

    def __init__(self, eng, fn, is_dma=False, dkey=None):
        self.eng = eng
        self.fn = fn
        self.deps = []
        self.is_dma = is_dma
        self.signal = is_dma
        self.sem = None
        self.val = 0
        self.dkey = dkey


class Prog:
    ENGS = ("pe", "act", "dve", "pool", "sp")

    def __init__(self):
        self.nc = bass.Bass("TRN2", target_bir_lowering=False)
        self.st = ExitStack()
        self.ops = []
        self.state = {}
        self.last_eng = {}
        self.dmas_since = []

    def full_barrier(self):
        c = Op("pool", lambda e: e.memset(self._bar[:, 0:1], 0.0))
        c.deps = [o for o in self.last_eng.values()] + list(self.dmas_since)
        c.idx = len(self.ops)
        self.ops.append(c)
        self.dmas_since = []
        self.last_eng = {"pool": c}
        for en in ("pe", "act", "dve"):
            o = Op(en, None)
            o.deps = [c]
            o.idx = len(self.ops)
            self.ops.append(o)

    def sb(self, name, shape, dt=F32):
        return self.st.enter_context(self.nc.sbuf_tensor(name, list(shape), dt))

    def ps(self, name, shape, dt=F32):
        return self.st.enter_context(self.nc.psum_tensor(name, list(shape), dt))

    def dram(self, name, shape, dt=F32, kind="Internal"):
        return self.nc.dram_tensor(name, list(shape), dt, kind=kind)

    @staticmethod
    def _norm(k):
        if isinstance(k, tuple):
            return k[0], tuple(k[1:])
        return k, ()

    def _entries(self, name, sub):
        d = self.state.setdefault(name, {})
        out = []
        n = len(sub)
        for s2, e in d.items():
            m = min(n, len(s2))
            if s2[:m] == sub[:m]:
                out.append(e)
        return out

    def _add(self, op, reads, writes):
        deps = set()
        for k in reads:
            name, sub = self._norm(k)
            for e in self._entries(name, sub):
                if e[0] is not None:
                    deps.add(e[0])
                if name == "ps":
                    for r in e[1]:
                        if r.eng != op.eng:
                            deps.add(r)
        for k in writes:
            name, sub = self._norm(k)
            for e in self._entries(name, sub):
                if e[0] is not None:
                    deps.add(e[0])
                for r in e[1]:
                    deps.add(r)
        for k in reads:
            name, sub = self._norm(k)
            d = self.state.setdefault(name, {})
            e = d.setdefault(sub, [None, []])
            if not op.is_dma:
                e[1] = [x for x in e[1] if x.is_dma or x.eng != op.eng]
            e[1].append(op)
        for k in writes:
            name, sub = self._norm(k)
            d = self.state.setdefault(name, {})
            n = len(sub)
            for s2 in [s2 for s2 in d if len(s2) >= n and s2[:n] == sub]:
                del d[s2]
            d[sub] = [op, []]
        deps.discard(op)
        op.deps = list(deps)
        op.idx = len(self.ops)
        self.ops.append(op)
        if op.is_dma:
            self.dmas_since.append(op)
        else:
            self.last_eng[op.eng] = op
        return op

    def op(self, eng, fn, r=(), w=()):
        return self._add(Op(eng, fn), r, w)

    def dma(self, q, out, in_, r=(), w=(), dkey=None, prefetch=False, **kw):
        if q == "sp" and not prefetch:
            q = "act"
        if dkey is None:
            dkey = (w[0] if len(w) else r[0])
        dkey = ("dma",) + (dkey if isinstance(dkey, tuple) else (dkey,))
        o = Op(q, lambda e: e.dma_start(out=out, in_=in_, **kw), is_dma=True, dkey=dkey)
        return self._add(o, r, w)

    def finish(self):
        nc = self.nc
        ops = self.ops
        for o in ops:
            for d in o.deps:
                if d.eng == o.eng and o.eng == "pe" and not d.is_dma and not o.is_dma:
                    continue
                d.signal = True
        sems = {}

        def getsem(key):
            if key not in sems:
                sems[key] = self.st.enter_context(nc.semaphore("s%d" % len(sems)))
            return sems[key]

        ecount = {e: 0 for e in self.ENGS}
        dcount = {}
        for o in ops:
            if o.is_dma:
                c = dcount.get(o.dkey, 0)
                dcount[o.dkey] = c + 1
                o.sem = getsem((o.dkey, c // DMA_EPOCH))
                o.val = 16 * (c % DMA_EPOCH + 1)
            elif o.signal:
                c = ecount[o.eng]
                ecount[o.eng] = c + 1
                o.sem = getsem((o.eng, c // EPOCH))
                o.val = c % EPOCH + 1
        last = {}
        for o in ops:
            if o.is_dma:
                last[id(o.sem)] = o
        per = {e: [] for e in self.ENGS}
        for o in ops:
            per[o.eng].append(o)
        self.n_sems = len(sems)
        self.counts = {e: len(per[e]) for e in self.ENGS}
        known = {e: {} for e in self.ENGS}
        blk = self.st.enter_context(nc.Block())

        def emit(engname):
            def body(eng):
                kn = known[engname]
                for o in per[engname]:
                    need = {}
                    for d in o.deps:
                        if d.eng == engname and engname == "pe" and not d.is_dma:
                            continue
                        k = id(d.sem)
                        if kn.get(k, 0) >= d.val:
                            continue
                        if k not in need or need[k][1] < d.val:
                            need[k] = (d.sem, d.val)
                    for k, (s, v) in need.items():
                        eng.wait_ge(s, v)
                        kn[k] = v
                    if o.fn is None:
                        continue
                    ins = o.fn(eng)
                    if o.signal:
                        ins.then_inc(o.sem, 16 if o.is_dma else 1)
                if engname == "sp":
                    for o in last.values():
                        eng.wait_ge(o.sem, o.val)
            return body

        blk.tensor(emit("pe"))
        blk.scalar(emit("act"))
        blk.vector(emit("dve"))
        blk.gpsimd(emit("pool"))
        blk.sync(emit("sp"))
        self.st.close()
        return nc


D = 2048
DEPTH = 2
NSEQ_P = 4
LSEQ = 2048
NSAMP = 128
RW_D, RW_H, RW_HD = 512, 8, 64
RW_PROJ = 1984
S5_D, S5_G, S5_N, S5_CH = 512, 32, 64, 16
GD_D, GD_H, GD_HD = 512, 4, 128
GD_PROJ = 2056
LR_D = 512
N_IN = 5576
OFF_S5 = 1984
OFF_GD = 2496
OFF_LR = 4552
D_FF = 5632
NORM_EPS = 1e-6

W_NAMES = ['ada_w', 'ada_b', 'norm1_g', 'norm2_g', 'final_g', 'w_in', 'w_out',
           'rwkv_mu', 'rwkv_w0', 'rwkv_w_up', 'rwkv_a0', 'rwkv_a_up', 'rwkv_g_up', 'rwkv_k_k', 'rwkv_k_a',
           'rwkv_r_k', 'rwkv_ln_w', 'rwkv_ln_b',
           's5_lambda_re', 's5_lambda_im', 's5_log_dt', 's5_b_re', 's5_b_im', 's5_c_re', 's5_c_im', 's5_d',
           's5_glu_w', 's5_glu_b',
           'gdn_conv_w', 'gdn_a_log', 'gdn_dt_bias', 'gdn_norm_w',
           'lru_conv_w', 'lru_conv_b', 'lru_wr', 'lru_br', 'lru_wi', 'lru_bi', 'lru_lambda',
           'ffn_w_up', 'ffn_conv_w', 'ffn_conv_b', 'ffn_w_down']
W_SHAPES = {
    'ada_w': (2, 2048, 12288), 'ada_b': (2, 12288), 'norm1_g': (2, 2048), 'norm2_g': (2, 2048), 'final_g': (2048,),
    'w_in': (2, 2048, 5576), 'w_out': (2, 2048, 2048), 'rwkv_mu': (2, 1984), 'rwkv_w0': (2, 512),
    'rwkv_w_up': (2, 96, 512), 'rwkv_a0': (2, 512), 'rwkv_a_up': (2, 96, 512), 'rwkv_g_up': (2, 256, 512),
    'rwkv_k_k': (2, 512), 'rwkv_k_a': (2, 512), 'rwkv_r_k': (2, 8, 64), 'rwkv_ln_w': (2, 512), 'rwkv_ln_b': (2, 512),
    's5_lambda_re': (2, 32, 64), 's5_lambda_im': (2, 32, 64), 's5_log_dt': (2, 32), 's5_b_re': (2, 32, 64, 16),
    's5_b_im': (2, 32, 64, 16), 's5_c_re': (2, 32, 16, 64), 's5_c_im': (2, 32, 16, 64), 's5_d': (2, 512),
    's5_glu_w': (2, 512, 512), 's5_glu_b': (2, 512), 'gdn_conv_w': (2, 4, 1536), 'gdn_a_log': (2, 4),
    'gdn_dt_bias': (2, 4), 'gdn_norm_w': (2, 128), 'lru_conv_w': (2, 4, 512), 'lru_conv_b': (2, 512),
    'lru_wr': (2, 8, 64, 64), 'lru_br': (2, 512), 'lru_wi': (2, 8, 64, 64), 'lru_bi': (2, 512), 'lru_lambda': (2, 512),
    'ffn_w_up': (2, 2048, 11264), 'ffn_conv_w': (2, 3, 11264), 'ffn_conv_b': (2, 11264), 'ffn_w_down': (2, 5632, 2048),
}
STATE_SHAPES = {
    'rwkv_wkv': (8, 64, 64), 'rwkv_shift': (1984,), 's5_re': (32, 64), 's5_im': (32, 64), 'gdn': (4, 128, 128),
    'gdn_conv': (3, 1536), 'lru_h': (512,), 'lru_conv': (3, 512), 'ffn_conv': (2, 11264),
}
STATE_NAMES = ('rwkv_wkv', 'rwkv_shift', 's5_re', 's5_im', 'gdn', 'gdn_conv', 'lru_h', 'lru_conv', 'ffn_conv')


def build(LP=2048, NS=16, TT=512, mixers=("lru", "s5", "gdn", "rwkv"), dbg=0):
    P = Prog()
    nc = P.nc
    NT = LP // TT
    xp = P.dram("xp", [LP, D], F32, kind="ExternalInput").ap()
    xs_in = P.dram("xs", [NS, D], F32, kind="ExternalInput").ap()
    cc_in = P.dram("cc", [1 + NS, D], F32, kind="ExternalInput").ap()
    Wd = {n: P.dram(n, list(W_SHAPES[n]), F32, kind="ExternalInput").ap() for n in W_NAMES}
    Sin = {n: P.dram("si_" + n, [DEPTH, NS] + list(STATE_SHAPES[n]), F32, kind="ExternalInput").ap() for n in STATE_NAMES}
    yp = P.dram("yp", [LP, D], F32, kind="ExternalOutput").ap()
    ys = P.dram("ys", [NS, D], F32, kind="ExternalOutput").ap()
    SPo = {n: P.dram("sp_" + n, [DEPTH] + list(STATE_SHAPES[n]), F32, kind="ExternalOutput").ap() for n in STATE_NAMES}
    SSo = {n: P.dram("ss_" + n, [DEPTH, NS] + list(STATE_SHAPES[n]), F32, kind="ExternalOutput").ap() for n in STATE_NAMES}

    xT = P.sb("xT", [128, 16, TT], F32)
    hT = P.sb("hT", [128, 16, TT], BF16)
    ycat = P.sb("ycat", [128, 16, TT], BF16)
    NWB = 2
    WBE = 6144
    wbuf = [P.sb("wbuf%d" % i, [128, WBE], BF16) for i in range(NWB)]
    ARW = 19456
    arena = P.sb("arena", [128, ARW], F32)
    ident = P.sb("ident", [128, 128], F32)
    ones = P.sb("ones", [128, 128], F32)
    adab = P.sb("adab", [128, DEPTH, 96], F32)
    ng = P.sb("ng", [128, DEPTH, 2, 16], F32)
    fg = P.sb("fg", [128, 16], F32)
    coef = P.sb("coef", [128, DEPTH, 6, 16, 1 + NS], F32)
    rstd = P.sb("rstd", [128, TT], F32)
    sqb = [P.sb("sqb%d" % i, [128, TT], F32) for i in range(2)]
    ffh = P.sb("ffh", [128, DEPTH, 88, 2], F32)
    ffw = P.sb("ffw", [128, DEPTH, 88, 4], F32)
    psum = P.ps("psum", [128, 8, 512], F32)

    def PSB(b):
        return ("ps", b)

    ABASE = 0

    def A_(off, n):
        assert ABASE + off + n <= ARW, (off, n)
        return arena[:, ABASE + off:ABASE + off + n]

    cnt = {"ps": 0, "wb": 0, "sq": 0, "ev": 0, "vs": 0}

    def next_bank(lo=0, hi=4):
        b = lo + cnt["ps"] % (hi - lo)
        cnt["ps"] += 1
        return b

    def ev_eng():
        cnt["ev"] += 1
        return "dve" if cnt["ev"] % 2 else "act"

    def copy(eng, out, in_, r, w):
        if eng == "act":
            P.op("act", lambda e: e.copy(out, in_), r, w)
        else:
            P.op(eng, lambda e: e.tensor_copy(out, in_), r, w)

    def tt(eng, out, a, b, op, r, w):
        eng = "dve" if eng == "pool" else eng
        P.op(eng, lambda e: e.tensor_tensor(out, a, b, op), r, w)

    def ts(eng, out, a, s1, s2, op0, op1, r, w):
        eng = "dve" if eng == "pool" else eng
        if op1 is None:
            P.op(eng, lambda e: e.tensor_scalar(out, a, s1, None, op0), r, w)
        else:
            P.op(eng, lambda e: e.tensor_scalar(out, a, s1, s2, op0, op1), r, w)

    def stt(eng, out, in0, sc, in1, op0, op1, r, w):
        P.op(eng, lambda e: e.scalar_tensor_tensor(out, in0, sc, in1, op0, op1), r, w)

    def act(out, in_, func, r, w, bias=None, scale=None):
        kw = {}
        if bias is not None:
            kw["bias"] = bias
        if scale is not None:
            kw["scale"] = scale
        P.op("act", lambda e: e.activation(out, in_, func, **kw), r, w)

    def mm(out, lhsT, rhs, start, stop, r, w):
        P.op("pe", lambda e: e.matmul(out, lhsT, rhs, start=start, stop=stop), r, w)

    def transp(out, in_, n, r, w):
        P.op("pe", lambda e: e.transpose(out, in_, ident[:n, :n]), list(r) + ["ident"], w)

    P._bar = P.sb("barbuf", [128, 8], F32)

    def dma_rows(q, out, in_, ncols, r=(), w=(), dkey=None):
        base = dkey if dkey is not None else (w[0] if len(w) else r[0])
        base = base if isinstance(base, tuple) else (base,)
        for ci, c0 in enumerate(range(0, ncols, 512)):
            c1 = min(ncols, c0 + 512)
            P.dma(q, out[:, c0:c1], in_[:, c0:c1], r=r, w=w, dkey=base + ("c%d" % (ci % 4),))

    def barrier(name):
        P.full_barrier()

    P.op("pool", lambda e: e.memset(ident[:], 1.0), w=["ident"])
    P.op("pool", lambda e: e.affine_select(ident[:], ident[:], pattern=[[-1, 128]], compare_op=ALU.is_equal,
                                           fill=0.0, base=0, channel_multiplier=1), r=["ident"], w=["ident"])
    P.op("pool", lambda e: e.memset(ones[:], 1.0), w=["ones"])
    P.op("dve", lambda e: e.memset(ffh[:], 0.0), w=["ffh"])

    vst = [P.sb("vst%d" % i, [128, 128], F32) for i in range(2)]

    def load_fm(dst, src1d, nchunk, key, q="sp"):
        i = cnt["vs"] % 2
        cnt["vs"] += 1
        P.dma(q, vst[i][:nchunk, :], src1d.rearrange("(c p) -> c p", p=128), w=[("vst", i)])
        b = next_bank(4, 8)
        transp(psum[:, b, :nchunk], vst[i][:nchunk, :], nchunk, [("vst", i)], [PSB(b)])
        copy("dve", dst, psum[:, b, :nchunk], [PSB(b)], [key])

    def store_fm(dst1d, src, nchunk, rkeys):
        i = cnt["vs"] % 2
        cnt["vs"] += 1
        b = next_bank(4, 8)
        P.op("pe", lambda e: e.transpose(psum[:nchunk, b, :128], src, ident[:, :]), list(rkeys) + ["ident"], [PSB(b)])
        copy("dve", vst[i][:nchunk, :], psum[:nchunk, b, :128], [PSB(b)], [("vst", i)])
        P.dma("sp", dst1d.rearrange("(c p) -> c p", p=128), vst[i][:nchunk, :], r=[("vst", i)])

    for l in range(DEPTH):
        load_fm(adab[:, l, :], Wd['ada_b'][l], 96, ("adab", l))
        load_fm(ng[:, l, 0, :], Wd['norm1_g'][l], 16, ("ng", l, 0))
        load_fm(ng[:, l, 1, :], Wd['norm2_g'][l], 16, ("ng", l, 1))
        for j in range(3):
            load_fm(ffw[:, l, :, j], Wd['ffn_conv_w'][l, j], 88, ("ffw", l, j))
        load_fm(ffw[:, l, :, 3], Wd['ffn_conv_b'][l], 88, ("ffw", l, 3))
    load_fm(fg[:], Wd['final_g'], 16, "fg")

    def to_fm(dst_fn, src_sb, n, F, rkeys, wkey_fn, bank_lo=4, bank_hi=8):
        nblk = (F + 127) // 128
        per = max(1, 512 // n)
        j = 0
        while j < nblk:
            b = next_bank(bank_lo, bank_hi)
            g = min(per, nblk - j)
            for i in range(g):
                wdt = min(128, F - (j + i) * 128)
                transp(psum[:wdt, b, i * n:(i + 1) * n], src_sb[:, (j + i) * 128:(j + i) * 128 + wdt], n,
                       rkeys, [PSB(b)])
            for i in range(g):
                wdt = min(128, F - (j + i) * 128)
                copy(ev_eng(), dst_fn(j + i, wdt), psum[:wdt, b, i * n:(i + 1) * n], [PSB(b)], [wkey_fn(j + i)])
            j += g

    WSC_TOTAL = 800000
    wsc = P.dram("wsc", [128, WSC_TOTAL], BF16).ap()
    wcache = {}
    wsc_off = [0]

    def dense(W2d, KC, rhs_fn, mtiles, consume, N, tag, ckey=None):
        maxc = WBE // KC
        blocks = []
        cur = []
        for i, (c0, ncol) in enumerate(mtiles):
            if cur and (c0 + ncol - mtiles[cur[0]][0] > maxc or mtiles[cur[-1]][0] + mtiles[cur[-1]][1] != c0):
                blocks.append(cur)
                cur = []
            cur.append(i)
        if cur:
            blocks.append(cur)
        Wv = W2d.rearrange("(kc p) m -> p kc m", p=128)
        for bi_, blk in enumerate(blocks):
            c0 = mtiles[blk[0]][0]
            c1 = mtiles[blk[-1]][0] + mtiles[blk[-1]][1]
            bw = c1 - c0
            wi = cnt["wb"] % NWB
            cnt["wb"] += 1
            wv = wbuf[wi][:, 0:KC * bw].rearrange("p (kc m) -> p kc m", kc=KC)
            ck = (ckey + (bi_,)) if ckey is not None else None
            if ck is not None and ck in wcache:
                P.dma("sp", wbuf[wi][:, 0:KC * bw], wcache[ck], r=[("wsc",) + ck], w=[("wbuf", wi)], prefetch=True)
            else:
                P.dma("pool", wv, Wv[:, :, c0:c1], w=[("wbuf", wi)])
                if ck is not None:
                    off = wsc_off[0]
                    wsc_off[0] += KC * bw
                    assert wsc_off[0] <= WSC_TOTAL
                    wcache[ck] = wsc[:, off:off + KC * bw]
                    P.dma("act", wcache[ck], wbuf[wi][:, 0:KC * bw], r=[("wbuf", wi)], w=[("wsc",) + ck],
                          dkey=("wsc_o", wi))
            for i in blk:
                m0, ncol = mtiles[i]
                b = next_bank(0, 4)
                for kc in range(KC):
                    rap, rkey = rhs_fn(kc)
                    mm(psum[:ncol, b, 0:N], wv[:, kc, m0 - c0:m0 - c0 + ncol], rap, kc == 0, kc == KC - 1,
                       [("wbuf", wi), rkey], [PSB(b)])
                consume(i, psum[:ncol, b, 0:N], PSB(b))

    NSQ = 1 + NS
    ccs = A_(0, D)[:NSQ, :]
    scT = A_(D, 16 * NSQ).rearrange("p (c n) -> p c n", c=16)
    scTb = hT[:, :, 0:NSQ]
    modT = A_(4096, DEPTH * 96 * NSQ).rearrange("p (l c n) -> p l c n", l=DEPTH, c=96)
    dma_rows("sp", ccs, cc_in, D, w=["ar_cc"])
    to_fm(lambda j, w: scT[:w, j, :], ccs, NSQ, D, ["ar_cc"], lambda j: ("ar_scT", j))
    act(scTb, scT, AF.Silu, ["ar_scT"], ["hT"])
    for l in range(DEPTH):
        def cons_mod(i, ps, pk, l=l):
            ts("dve", modT[:, l, i, :], ps, adab[:, l, i:i + 1], None, ALU.add, None,
               [pk, ("adab", l)], [("modT", l, i)])
        dense(Wd['ada_w'][l], 16, lambda kc: (scTb[:, kc, :], "hT"), [(i * 128, 128) for i in range(96)],
              cons_mod, NSQ, "ada")
        for which, (sh_i, sc_i, g_i) in enumerate([(0, 1, 2), (3, 4, 5)]):
            base = which * 3
            ts("dve", coef[:, l, base + 0], modT[:, l, sc_i * 16:(sc_i + 1) * 16, :], 1.0, None, ALU.add, None,
               [("modT", l)], [("coef", l, base + 0)])
            tt("dve", coef[:, l, base + 0], coef[:, l, base + 0],
               ng[:, l, which, :].unsqueeze(2).to_broadcast([128, 16, NSQ]), ALU.mult,
               [("coef", l, base + 0), ("ng", l, which)], [("coef", l, base + 0)])
            copy("dve", coef[:, l, base + 1], modT[:, l, sh_i * 16:(sh_i + 1) * 16, :], [("modT", l)],
                 [("coef", l, base + 1)])
            copy("dve", coef[:, l, base + 2], modT[:, l, g_i * 16:(g_i + 1) * 16, :], [("modT", l)],
                 [("coef", l, base + 2)])
    barrier("arena_all")

    def cf(l, which, c, mode, N):
        if mode == "p":
            return coef[:, l, which, c, 0:1].to_broadcast([128, N])
        return coef[:, l, which, c, 1:1 + NS]

    def rmsnorm_mod(l, which, N, mode):
        b = next_bank(4, 8)
        for c in range(16):
            s = sqb[cnt["sq"] % 2]
            sk = ("sqb", cnt["sq"] % 2)
            cnt["sq"] += 1
            act(s[:, :N], xT[:, c, :N], AF.Square, [("xT", c)], [sk])
            mm(psum[:, b, :N], ones[:], s[:, :N], c == 0, c == 15, [sk, "ones"], [PSB(b)])
        ts("dve", rstd[:, :N], psum[:, b, :N], 1.0 / D, NORM_EPS, ALU.mult, ALU.add, [PSB(b)], ["rstd"])
        act(rstd[:, :N], rstd[:, :N], AF.Sqrt, ["rstd"], ["rstd"])
        P.op("dve", lambda e: e.reciprocal(rstd[:, :N], rstd[:, :N]), ["rstd"], ["rstd"])
        for c in range(16):
            s = sqb[cnt["sq"] % 2]
            sk = ("sqb", cnt["sq"] % 2)
            cnt["sq"] += 1
            if which is None:
                raise RuntimeError
            tt("dve", s[:, :N], xT[:, c, :N], rstd[:, :N], ALU.mult, [("xT", c), "rstd"], [sk])
            tt("pool", s[:, :N], s[:, :N], cf(l, which * 3 + 0, c, mode, N), ALU.mult, [sk, ("coef", l)], [sk])
            tt("dve", hT[:, c, :N], s[:, :N], cf(l, which * 3 + 1, c, mode, N), ALU.add, [sk, ("coef", l)],
               [("hT", c)])

    def resid_add(l, which, c, ps, pk, N, mode):
        s = sqb[cnt["sq"] % 2]
        sk = ("sqb", cnt["sq"] % 2)
        cnt["sq"] += 1
        tt("dve", s[:, :N], ps, cf(l, which * 3 + 2, c, mode, N), ALU.mult, [pk, ("coef", l)], [sk])
        tt("pool", xT[:, c, :N], xT[:, c, :N], s[:, :N], ALU.add, [sk, ("xT", c)], [("xT", c)])

    def zero_ycat(N):
        P.op("pool", lambda e: e.memset(ycat[:, :, :N], 0.0), w=["ycat"])

    from_mix = {}

    def layer(l, N, mode, ti):
        last_tile = (ti == NT - 1)
        rmsnorm_mod(l, 0, N, mode)
        barrier("arena_all")
        zero_ycat(N)
        for name in mixers:
            from_mix[name](l, N, mode, ti)
        dense(Wd['w_out'][l], 16, lambda kc: (ycat[:, kc, :N], ("ycat", kc)), [(i * 128, 128) for i in range(16)],
              lambda i, ps, pk: resid_add(l, 0, i, ps, pk, N, mode), N, "wout", ckey=("wout", l))
        rmsnorm_mod(l, 1, N, mode)
        barrier("arena_all")
        ffn(l, N, mode, ti)

    def ffn(l, N, mode, ti):
        NN = TT if mode == "p" else NS
        actT = A_(0, 44 * NN // 2).bitcast(BF16).rearrange("p (j n) -> p j n", j=44)
        o0 = 44 * NN // 2
        upx = [A_(o0 + i * (NN + 2), NN + 2) for i in range(4)]
        o1 = o0 + 4 * (NN + 2)
        cv = [A_(o1 + i * NN, NN) for i in range(4)]
        o2 = o1 + 4 * NN
        if mode == "s":
            hist = A_(o2, 88 * 2 * NS).rearrange("p (j w n) -> p j w n", j=88, w=2)
            stage = A_(o2 + 88 * 2 * NS, 11264)
            newrow = A_(o2 + 88 * 2 * NS + 11264, 88 * NS).rearrange("p (j n) -> p j n", j=88)
            for wi_ in range(2):
                dma_rows("sp", stage[:NS, :], Sin['ffn_conv'][l, :, wi_, :], 11264, w=["ar_stage"])
                to_fm(lambda j, w, wi_=wi_: hist[:w, j, wi_, :], stage[:NS, :], NS, 11264, ["ar_stage"],
                      lambda j, wi_=wi_: ("ar_hist", j, wi_))
            dma_rows("sp", SSo['ffn_conv'][l, :, 0, :], Sin['ffn_conv'][l, :, 1, :], 11264, r=[], w=["o_ffnc0"])
        gate_done = {}

        def cons_up(i, ps, pk):
            u = upx[i % 4]
            uk = ("ar_upx", i % 4)
            c = cv[i % 4]
            ck = ("ar_cv", i % 4)
            w0, w1, w2, bb = (ffw[:, l, i, k:k + 1] for k in range(4))
            if mode == "p":
                copy("act", u[:, 2:2 + N], ps, [pk], [uk])
                copy("pool", u[:, 0:2], ffh[:, l, i, :], [("ffh", l, i)], [uk])
                copy("pool", ffh[:, l, i, :], u[:, N:N + 2], [uk], [("ffh", l, i)])
                ts("dve", c[:, :N], u[:, 2:2 + N], w2, bb, ALU.mult, ALU.add, [uk, ("ffw", l)], [ck])
                stt("dve", c[:, :N], u[:, 1:1 + N], w1, c[:, :N], ALU.mult, ALU.add, [uk, ck, ("ffw", l)], [ck])
                stt("dve", c[:, :N], u[:, 0:N], w0, c[:, :N], ALU.mult, ALU.add, [uk, ck, ("ffw", l)], [ck])
            else:
                copy("act", u[:, 0:N], ps, [pk], [uk])
                copy("pool", newrow[:, i, :], u[:, 0:N], [uk], [("ar_newrow", i)])
                ts("dve", c[:, :N], u[:, 0:N], w2, bb, ALU.mult, ALU.add, [uk, ("ffw", l)], [ck])
                stt("dve", c[:, :N], hist[:, i, 1, :], w1, c[:, :N], ALU.mult, ALU.add,
                    [("ar_hist", i, 1), ck, ("ffw", l)], [ck])
                stt("dve", c[:, :N], hist[:, i, 0, :], w0, c[:, :N], ALU.mult, ALU.add,
                    [("ar_hist", i, 0), ck, ("ffw", l)], [ck])
            if i < 44:
                act(actT[:, i, :N], c[:, :N], AF.Silu, [ck], [("ar_actT", i)])
            else:
                tt("dve", actT[:, i - 44, :N], actT[:, i - 44, :N], c[:, :N], ALU.mult, [("ar_actT", i - 44), ck],
                   [("ar_actT", i - 44)])

        order = []
        for j0 in range(0, 44, 4):
            order += [(j, j * 128, 128) for j in range(j0, j0 + 4)]
            order += [(44 + j, (44 + j) * 128, 128) for j in range(j0, j0 + 4)]
        mt = [(c0, n) for (_, c0, n) in order]
        dense(Wd['ffn_w_up'][l], 16, lambda kc: (hT[:, kc, :N], ("hT", kc)), mt,
              lambda i, ps, pk: cons_up(order[i][0], ps, pk), N, "ffup", ckey=("ffup", l))
        if mode == "p" and ti == NT - 1:
            for w_ in range(2):
                store_fm(SPo['ffn_conv'][l, w_], ffh[:, l, :, w_], 88, [("ffh", l)])
        if mode == "s":
            to_tm(SSo['ffn_conv'][l, :, 1, :], lambda j, w: newrow[:w, j, :], NS, 11264,
                  lambda j: ("ar_newrow", j), stage, "ar_stage")
        dense(Wd['ffn_w_down'][l], 44, lambda kc: (actT[:, kc, :N], ("ar_actT", kc)),
              [(i * 128, 128) for i in range(16)],
              lambda i, ps, pk: resid_add(l, 1, i, ps, pk, N, mode), N, "ffdn", ckey=("ffdn", l))

    def to_tm(dst_hbm, src_fn, n, F, rkey_fn, stage, stage_key):
        nblk = (F + 127) // 128
        j = 0
        while j < nblk:
            b = next_bank(4, 8)
            g = min(4, nblk - j)
            tot = 0
            for i in range(g):
                wdt = min(128, F - (j + i) * 128)
                o_ap = psum[:n, b, i * 128:i * 128 + wdt]
                i_ap = src_fn(j + i, wdt)
                id_ap = ident[:wdt, :wdt]
                P.op("pe", lambda e, o_ap=o_ap, i_ap=i_ap, id_ap=id_ap: e.transpose(o_ap, i_ap, id_ap),
                     [rkey_fn(j + i), "ident"], [PSB(b)])
                tot += wdt
            copy(ev_eng(), stage[:n, j * 128:j * 128 + tot], psum[:n, b, 0:tot], [PSB(b)], [(stage_key, j)])
            j += g
        dma_rows("sp", dst_hbm, stage[:n, :F], F, r=[stage_key], dkey=stage_key)

    def mix_stub(l, N, mode, ti):
        pass

    for name in mixers:
        from_mix[name] = mix_stub

    def sample_in(dst3, src2d, F, key):
        stg = arena[:, ARW - 2048:ARW]
        for f0 in range(0, F, 2048):
            fw = min(2048, F - f0)
            dma_rows("sp", stg[:NS, :fw], src2d[:, f0:f0 + fw], fw, w=["ar_sin"])
            to_fm(lambda j, w, f0=f0: dst3[:w, f0 // 128 + j, :], stg[:NS, :fw], NS, fw, ["ar_sin"],
                  lambda j, f0=f0: (key, f0 // 128 + j))

    def sample_out(dst2d, src_fn, F, rkey_fn):
        stg = arena[:, ARW - 4096:ARW - 2048]
        for f0 in range(0, F, 2048):
            fw = min(2048, F - f0)
            to_tm(dst2d[:, f0:f0 + fw], lambda j, w, f0=f0: src_fn(f0 // 128 + j, w), NS, fw,
                  lambda j, f0=f0: rkey_fn(f0 // 128 + j), stg, "ar_sout")

    CH = 64
    if "gdn" in mixers or "rwkv" in mixers:
        msk = P.sb("msk", [64, 3, 4, 64], F32)
        P.op("pool", lambda e: e.memset(msk[:], 1.0), w=["msk"])
        for k_, (pat, cm_, base_) in enumerate([([[0, 4], [-1, 64]], 1, -1), ([[0, 4], [1, 64]], -1, -1),
                                                ([[0, 4], [1, 64]], -1, 0)]):
            P.op("pool", lambda e, k_=k_, pat=pat, cm_=cm_, base_=base_: e.affine_select(
                msk[:, k_], msk[:, k_], pattern=pat, compare_op=ALU.is_ge, fill=0.0, base=base_,
                channel_multiplier=cm_), r=["msk"], w=["msk"])

    def neumann(Q, QT, Q2, QT2, PT, H, C, kq):
        idb = ident[:C, :C].unsqueeze(1).to_broadcast([C, H, C])
        tt("dve", PT, Q, idb, ALU.add, [(kq, "Q"), "ident"], [(kq, "P")])
        L = 0
        while (1 << L) < C:
            L += 1
        if L <= 1:
            return
        cq, cqt, nq, nqt = Q, QT, Q2, QT2
        ckq, ckqt, nkq, nkqt = (kq, "Q"), (kq, "QT"), (kq, "Q2"), (kq, "QT2")
        pv = lambda bb: psum[:C, bb, 0:H * C].rearrange("p (h c) -> p h c", h=H)

        def square(bq, bqt):
            for h in range(H):
                mm(psum[:C, bq, h * C:(h + 1) * C], cqt[:, h, :], cq[:, h, :], True, True, [ckq, ckqt], [PSB(bq)])
            for h in range(H):
                mm(psum[:C, bqt, h * C:(h + 1) * C], cq[:, h, :], cqt[:, h, :], True, True, [ckq, ckqt], [PSB(bqt)])

        b1, b2 = next_bank(4, 8), next_bank(4, 8)
        square(b1, b2)
        copy("act", nq, pv(b1), [PSB(b1)], [nkq])
        copy("dve", nqt, pv(b2), [PSB(b2)], [nkqt])
        cq, cqt, nq, nqt = nq, nqt, cq, cqt
        ckq, ckqt, nkq, nkqt = nkq, nkqt, ckq, ckqt
        for j in range(1, L):
            last = (j == L - 1)
            b3 = next_bank(4, 8)
            for h in range(H):
                mm(psum[:C, b3, h * C:(h + 1) * C], cqt[:, h, :], PT[:, h, :], True, True, [ckqt, (kq, "P")], [PSB(b3)])
            if not last:
                b1, b2 = next_bank(4, 8), next_bank(4, 8)
                square(b1, b2)
            tt("dve", PT, PT, pv(b3), ALU.add, [(kq, "P"), PSB(b3)], [(kq, "P")])
            if not last:
                copy("act", nq, pv(b1), [PSB(b1)], [nkq])
                copy("act", nqt, pv(b2), [PSB(b2)], [nkqt])
                cq, cqt, nq, nqt = nq, nqt, cq, cqt
                ckq, ckqt, nkq, nkqt = nkq, nkqt, ckq, ckqt

    if "gdn" in mixers:
        gst = P.sb("gst", [128, DEPTH, 4, 128], F32)
        gch = P.sb("gch", [128, DEPTH, 12, 3], F32)
        gcw = P.sb("gcw", [128, DEPTH, 12, 4], F32)
        gnw = P.sb("gnw", [128, DEPTH], F32)
        gp8 = P.sb("gp8", [8, DEPTH, 4], F32)
        P.op("dve", lambda e: e.memset(gst[:], 0.0), w=["gst"])
        P.op("dve", lambda e: e.memset(gch[:], 0.0), w=["gch"])
        P.op("dve", lambda e: e.memset(gp8[:], 0.0), w=["gp8"])
        mA = A_(0, 2)[:8, :]
        P.op("pool", lambda e: e.memset(mA, 1.0), w=["ar_mA"])
        P.op("pool", lambda e: e.affine_select(mA[:, 0:1], mA[:, 0:1], pattern=[[0, 1]], compare_op=ALU.is_ge, fill=0.0,
                                               base=3, channel_multiplier=-1), r=["ar_mA"], w=["ar_mA"])
        for l in range(DEPTH):
            for k in range(4):
                load_fm(gcw[:, l, :, k], Wd['gdn_conv_w'][l, k], 12, ("gcw", l, k))
            P.dma("sp", gnw[:, l:l + 1], Wd['gdn_norm_w'][l].rearrange("(p o) -> p o", o=1), w=[("gnw", l)])
            P.dma("sp", gp8[0:4, l, 0:1], Wd['gdn_dt_bias'][l].rearrange("(p o) -> p o", o=1), w=[("gp8", l, 0)])
            P.dma("sp", gp8[0:4, l, 1:2], Wd['gdn_a_log'][l].rearrange("(p o) -> p o", o=1), w=[("gp8", l, 1)])
            act(gp8[:, l, 1:2], gp8[:, l, 1:2], AF.Exp, [("gp8", l, 1)], [("gp8", l, 1)])
            ts("dve", gp8[:, l, 1:2], gp8[:, l, 1:2], mA[:, 0:1], -1.0, ALU.mult, ALU.mult, [("gp8", l, 1), "ar_mA"],
               [("gp8", l, 1)])
            ts("dve", gp8[:, l, 2:3], mA[:, 0:1], -1.0, 1.0, ALU.mult, ALU.add, ["ar_mA"], [("gp8", l, 2)])
        barrier("x")

        def mix_gdn(l, N, mode, ti):
            H = 4
            C = CH if mode == "p" else 1
            o = 0

            def AL(n, shape=None):
                nonlocal o
                a = A_(o, n)
                o += n
                return a

            QKV = AL(12 * N).rearrange("p (i n) -> p i n", i=12)
            ZS = AL(4 * N).rearrange("p (i n) -> p i n", i=4)
            xe = [AL(N + 3) for _ in range(2)]
            GB = AL(N)
            gt1 = AL(N)
            if mode == "s":
                hist = AL(36 * NS).rearrange("p (j n) -> p j n", j=36)
                xnew = AL(12 * NS).rearrange("p (j n) -> p j n", j=12)
                Ssb = [AL(512).rearrange("p (h v) -> p h v", h=4) for _ in range(2)]
                sample_in(hist, Sin['gdn_conv'][l].rearrange("b w f -> b (w f)"), 4608, "ar_ghist")
                for w_ in range(2):
                    dma_rows("sp", SSo['gdn_conv'][l, :, w_, :], Sin['gdn_conv'][l, :, w_ + 1, :], 1536, r=[],
                             w=[("o_gdnc", w_)])

            def cons(i, ps, pk):
                if i < 12:
                    u = xe[i % 2]
                    uk = ("ar_gxe", i % 2)
                    X = QKV[:, i, :N]
                    XK = ("ar_qkv", i)
                    w0, w1, w2, w3 = (gcw[:, l, i, k:k + 1] for k in range(4))
                    if mode == "p":
                        copy("act", u[:, 3:3 + N], ps, [pk], [uk])
                        copy("dve", u[:, 0:3], gch[:, l, i, :], [("gch", l, i)], [uk])
                        copy("dve", gch[:, l, i, :], u[:, N:N + 3], [uk], [("gch", l, i)])
                        ts("dve", X, u[:, 3:3 + N], w3, None, ALU.mult, None, [uk, ("gcw", l)], [XK])
                        stt("dve", X, u[:, 2:2 + N], w2, X, ALU.mult, ALU.add, [uk, XK, ("gcw", l)], [XK])
                        stt("dve", X, u[:, 1:1 + N], w1, X, ALU.mult, ALU.add, [uk, XK, ("gcw", l)], [XK])
                        stt("dve", X, u[:, 0:N], w0, X, ALU.mult, ALU.add, [uk, XK, ("gcw", l)], [XK])
                    else:
                        copy("act", xnew[:, i, :], ps, [pk], [("ar_gxn", i)])
                        ts("dve", X, xnew[:, i, :], w3, None, ALU.mult, None, [("ar_gxn", i), ("gcw", l)], [XK])
                        for w_, wv in ((2, w2), (1, w1), (0, w0)):
                            stt("dve", X, hist[:, w_ * 12 + i, :], wv, X, ALU.mult, ALU.add,
                                [("ar_ghist", w_ * 12 + i), XK, ("gcw", l)], [XK])
                    act(X, X, AF.Silu, [XK], [XK])
                elif i < 16:
                    act(ZS[:, i - 12, :N], ps, AF.Silu, [pk], [("ar_zs", i - 12)])
                else:
                    act(gt1[:8, :N], ps, AF.Exp, [pk, ("gp8", l)], ["ar_gt1"], bias=gp8[:, l, 0:1])
                    act(gt1[:8, :N], gt1[:8, :N], AF.Ln, ["ar_gt1"], ["ar_gt1"], bias=1.0)
                    act(GB[:8, :N], ps, AF.Sigmoid, [pk], ["ar_GB"])
                    ts("dve", GB[:8, :N], GB[:8, :N], gp8[:, l, 2:3], None, ALU.mult, None, ["ar_GB", ("gp8", l)], ["ar_GB"])
                    stt("dve", GB[:8, :N], gt1[:8, :N], gp8[:, l, 1:2], GB[:8, :N], ALU.mult, ALU.add,
                        ["ar_gt1", "ar_GB", ("gp8", l)], ["ar_GB"])

            mt = [(OFF_GD + c * 128, 128) for c in range(16)] + [(OFF_GD + 2048, 8)]
            dense(Wd['w_in'][l], 16, lambda kc: (hT[:, kc, :N], ("hT", kc)), mt, cons, N, "gdn", ckey=("gdn", l))
            for i in range(8):
                sq = sqb[i % 2]
                sk = ("sqb", i % 2)
                tt("dve", sq[:, :N], QKV[:, i, :N], QKV[:, i, :N], ALU.mult, [("ar_qkv", i)], [sk])
                b = next_bank(4, 8)
                mm(psum[:, b, :N], ones[:], sq[:, :N], True, True, [sk, "ones"], [PSB(b)])
                ts("dve", sq[:, :N], psum[:, b, :N], 1e-6, None, ALU.add, None, [PSB(b)], [sk])
                act(sq[:, :N], sq[:, :N], AF.Sqrt, [sk], [sk])
                P.op("dve", lambda e, sq=sq: e.reciprocal(sq[:, :N], sq[:, :N]), [sk], [sk])
                if i < 4:
                    stt("dve", QKV[:, i, :N], QKV[:, i, :N], float(GD_HD) ** -0.5, sq[:, :N], ALU.mult, ALU.mult,
                        [("ar_qkv", i), sk], [("ar_qkv", i)])
                else:
                    tt("dve", QKV[:, i, :N], QKV[:, i, :N], sq[:, :N], ALU.mult, [("ar_qkv", i), sk], [("ar_qkv", i)])
            def galloc():
                d = {}
                d["Gt"] = AL(8)
                d["gc"] = AL(4)
                d["sm"] = AL(16)
                d["GB3"] = AL(512).rearrange("p (h a) -> p h a", h=4)
                for nm in ("Dm", "E1", "E2", "EG"):
                    d[nm] = AL(256).rearrange("p (h c) -> p h c", h=4)
                d["gtot"] = AL(4)
                for nm in ("kbg", "ktl", "vb"):
                    d[nm] = AL(512).rearrange("p (h a) -> p h a", h=4)
                for nm in ("Qm", "QTm", "Q2m", "QT2m", "PTm", "ATm", "wkT", "qdT"):
                    d[nm] = AL(256).rearrange("p (h c) -> p h c", h=4)
                for nm in ("usb", "vnew", "osb", "osq"):
                    d[nm] = AL(512).rearrange("p (h a) -> p h a", h=4)
                d["ss"] = AL(8)
                return d

            gsets = [galloc() for _ in range(2 if mode == "s" else 1)]
            nchunks = N // C
            for ci in range(nchunks):
                c0 = ci * C
                si_ = ci % len(gsets)
                T_ = gsets[si_]
                Gt, gc, sm, GB3, Dm, E1, E2, EG, gtot, kbg, ktl, vb = (T_[k] for k in (
                    "Gt", "gc", "sm", "GB3", "Dm", "E1", "E2", "EG", "gtot", "kbg", "ktl", "vb"))
                Qm, QTm, Q2m, QT2m, PTm, ATm, wkT, qdT, usb, vnew, osb, osq, ss = (T_[k] for k in (
                    "Qm", "QTm", "Q2m", "QT2m", "PTm", "ATm", "wkT", "qdT", "usb", "vnew", "osb", "osq", "ss"))
                K_ = lambda nm, si_=si_: ("ar_g_" + nm, si_)
                GN = "ar_gn%d" % si_
                if mode == "p":
                    S = gst[:, l]
                    SK = ("gst", l)
                else:
                    S = Ssb[ci % 2]
                    SK = ("ar_gS", ci % 2)
                    P.dma("sp", S, Sin['gdn'][l, ci].rearrange("h k v -> k h v"), w=[SK])
                b = next_bank(4, 8)
                o_ap, i_ap, id_ap = psum[:C, b, 0:8], GB[:8, c0:c0 + C], ident[:8, :8]
                P.op("pe", lambda e, o_ap=o_ap, i_ap=i_ap, id_ap=id_ap: e.transpose(o_ap, i_ap, id_ap),
                     ["ar_GB", "ident"], [PSB(b)])
                copy("dve", Gt[:C, :], psum[:C, b, 0:8], [PSB(b)], [K_("Gt")])
                b = next_bank(4, 8)
                mm(psum[:C, b, 0:4], msk[:C, 2, 0, :C], Gt[:C, 0:4], True, True, ["msk", K_("Gt")], [PSB(b)])
                copy("dve", gc[:C, :], psum[:C, b, 0:4], [PSB(b)], [K_("gc")])
                tt("dve", GB3[:C], Gt[:C, 0:4].unsqueeze(2).to_broadcast([C, 4, 128]),
                   ones[:C, :].unsqueeze(1).to_broadcast([C, 4, 128]), ALU.mult, [K_("Gt"), "ones"], [K_("GB3")])
                bR = next_bank(4, 8)
                for h in range(H):
                    mm(psum[:, bR, h * C:(h + 1) * C], GB3[:C, h, :], msk[:C, 2, 0, :C], True, True,
                       [K_("GB3"), "msk"], [PSB(bR)])
                Rv = psum[:, bR, 0:H * C].rearrange("p (h c) -> p h c", h=H)
                act(EG[:, :, :C], Rv, AF.Exp, [PSB(bR)], [K_("EG")])
                act(gtot[:, :], Rv[:, :, C - 1], AF.Exp, [PSB(bR)], [K_("gtot")])
                tt("dve", Dm[:C, :, :C], gc[:C, :].unsqueeze(2).to_broadcast([C, 4, C]), Rv[:C], ALU.subtract,
                   [K_("gc"), PSB(bR)], [K_("D")])
                tt("dve", sm[:C, 8:12], Rv[:C, :, C - 1], gc[:C, :], ALU.subtract, [PSB(bR), K_("gc")], [K_("sm")])
                act(sm[:C, 8:12], sm[:C, 8:12], AF.Exp, [K_("sm")], [K_("sm")])
                act(sm[:C, 0:4], gc[:C, :], AF.Exp, [K_("gc")], [K_("sm")])
                tt("dve", sm[:C, 4:8], sm[:C, 0:4], Gt[:C, 4:8], ALU.mult, [K_("sm"), K_("Gt")], [K_("sm")])
                ts("dve", sm[:C, 12:16], Gt[:C, 4:8], -1.0, None, ALU.mult, None, [K_("Gt")], [K_("sm")])
                ts("dve", E1[:C, :, :C], Dm[:C, :, :C], 0.0, None, ALU.min, None, [K_("D")], [K_("E1")])
                act(E1[:C, :, :C], E1[:C, :, :C], AF.Exp, [K_("E1")], [K_("E1")])
                tt("dve", E1[:C, :, :C], E1[:C, :, :C], msk[:C, 0, 0:4, :C], ALU.mult, [K_("E1"), "msk"], [K_("E1")])
                ts("dve", E2[:C, :, :C], Dm[:C, :, :C], -1.0, 0.0, ALU.mult, ALU.min, [K_("D")], [K_("E2")])
                act(E2[:C, :, :C], E2[:C, :, :C], AF.Exp, [K_("E2")], [K_("E2")])
                tt("dve", E2[:C, :, :C], E2[:C, :, :C], msk[:C, 2, 0:4, :C], ALU.mult, [K_("E2"), "msk"], [K_("E2")])
                bK = next_bank(4, 8)
                for h in range(H):
                    o_ap, i_ap = psum[:C, bK, h * 128:(h + 1) * 128], QKV[:, 4 + h, c0:c0 + C]
                    P.op("pe", lambda e, o_ap=o_ap, i_ap=i_ap: e.transpose(o_ap, i_ap, ident[:, :]),
                         [("ar_qkv", 4 + h), "ident"], [PSB(bK)])
                Kt = psum[:C, bK, :].rearrange("p (h a) -> p h a", h=H)
                tt("dve", kbg[:C], Kt, sm[:C, 4:8].unsqueeze(2).to_broadcast([C, 4, 128]), ALU.mult,
                   [PSB(bK), K_("sm")], [K_("kbg")])
                tt("dve", ktl[:C], Kt, sm[:C, 8:12].unsqueeze(2).to_broadcast([C, 4, 128]), ALU.mult,
                   [PSB(bK), K_("sm")], [K_("ktl")])
                bV = next_bank(4, 8)
                for h in range(H):
                    o_ap, i_ap = psum[:C, bV, h * 128:(h + 1) * 128], QKV[:, 8 + h, c0:c0 + C]
                    P.op("pe", lambda e, o_ap=o_ap, i_ap=i_ap: e.transpose(o_ap, i_ap, ident[:, :]),
                         [("ar_qkv", 8 + h), "ident"], [PSB(bV)])
                tt("dve", vb[:C], psum[:C, bV, :].rearrange("p (h a) -> p h a", h=H),
                   Gt[:C, 4:8].unsqueeze(2).to_broadcast([C, 4, 128]), ALU.mult, [PSB(bV), K_("Gt")], [K_("vb")])
                bKK = next_bank(4, 8)
                for h in range(H):
                    mm(psum[:C, bKK, h * C:(h + 1) * C], QKV[:, 4 + h, c0:c0 + C], QKV[:, 4 + h, c0:c0 + C], True, True,
                       [("ar_qkv", 4 + h)], [PSB(bKK)])
                KKv = psum[:C, bKK, 0:H * C].rearrange("p (h c) -> p h c", h=H)
                tt("dve", QTm[:C, :, :C], KKv, E1[:C, :, :C], ALU.mult, [PSB(bKK), K_("E1")], [(GN, "QT")])
                tt("dve", QTm[:C, :, :C], QTm[:C, :, :C], sm[:C, 12:16].unsqueeze(2).to_broadcast([C, 4, C]), ALU.mult,
                   [(GN, "QT"), K_("sm")], [(GN, "QT")])
                bT = next_bank(4, 8)
                for h in range(H):
                    o_ap, i_ap, id_ap = psum[:C, bT, h * C:(h + 1) * C], QTm[:C, h, :C], ident[:C, :C]
                    P.op("pe", lambda e, o_ap=o_ap, i_ap=i_ap, id_ap=id_ap: e.transpose(o_ap, i_ap, id_ap),
                         [(GN, "QT"), "ident"], [PSB(bT)])
                copy("act", Qm[:C, :, :C], psum[:C, bT, 0:H * C].rearrange("p (h c) -> p h c", h=H), [PSB(bT)],
                     [(GN, "Q")])
                bQK = next_bank(4, 8)
                for h in range(H):
                    mm(psum[:C, bQK, h * C:(h + 1) * C], QKV[:, 4 + h, c0:c0 + C], QKV[:, h, c0:c0 + C], True, True,
                       [("ar_qkv", 4 + h), ("ar_qkv", h)], [PSB(bQK)])
                tt("dve", ATm[:C, :, :C], psum[:C, bQK, 0:H * C].rearrange("p (h c) -> p h c", h=H), E2[:C, :, :C],
                   ALU.mult, [PSB(bQK), K_("E2")], [K_("AT")])
                neumann(Qm[:C, :, :C], QTm[:C, :, :C], Q2m[:C, :, :C], QT2m[:C, :, :C], PTm[:C, :, :C], H, C, GN)
                bW = next_bank(4, 8)
                for h in range(H):
                    mm(psum[:, bW, h * C:(h + 1) * C], kbg[:C, h, :], PTm[:C, h, :C], True, True,
                       [K_("kbg"), (GN, "P")], [PSB(bW)])
                copy("act", wkT[:, :, :C], psum[:, bW, 0:H * C].rearrange("p (h c) -> p h c", h=H), [PSB(bW)], [K_("wkT")])
                bU = next_bank(4, 8)
                for h in range(H):
                    mm(psum[:C, bU, h * 128:(h + 1) * 128], PTm[:C, h, :C], vb[:C, h, :], True, True,
                       [(GN, "P"), K_("vb")], [PSB(bU)])
                copy("act", usb[:C], psum[:C, bU, :].rearrange("p (h a) -> p h a", h=H), [PSB(bU)], [K_("usb")])
                bWS = next_bank(4, 8)
                for h in range(H):
                    mm(psum[:C, bWS, h * 128:(h + 1) * 128], wkT[:, h, :C], S[:, h, :], True, True, [K_("wkT"), SK],
                       [PSB(bWS)])
                tt("dve", vnew[:C], usb[:C], psum[:C, bWS, :].rearrange("p (h a) -> p h a", h=H), ALU.subtract,
                   [K_("usb"), PSB(bWS)], [K_("vnew")])
                tt("dve", qdT[:, :, :C], QKV[:, 0:4, c0:c0 + C], EG[:, :, :C], ALU.mult, [("ar_qkv",), K_("EG")], [K_("qdT")])
                bO = next_bank(4, 8)
                for h in range(H):
                    mm(psum[:C, bO, h * 128:(h + 1) * 128], qdT[:, h, :C], S[:, h, :], True, False, [K_("qdT"), SK], [PSB(bO)])
                    mm(psum[:C, bO, h * 128:(h + 1) * 128], ATm[:C, h, :C], vnew[:C, h, :], False, True,
                       [K_("AT"), K_("vnew")], [PSB(bO)])
                copy("act", osb[:C], psum[:C, bO, :].rearrange("p (h a) -> p h a", h=H), [PSB(bO)], [K_("osb")])
                bS = next_bank(4, 8)
                for h in range(H):
                    mm(psum[:, bS, h * 128:(h + 1) * 128], ktl[:C, h, :], vnew[:C, h, :], True, True,
                       [K_("ktl"), K_("vnew")], [PSB(bS)])
                for h in range(H):
                    stt("dve", S[:, h, :], S[:, h, :], gtot[:, h:h + 1], psum[:, bS, h * 128:(h + 1) * 128], ALU.mult,
                        ALU.add, [SK, K_("gtot"), PSB(bS)], [SK])
                if mode == "s":
                    P.dma("sp", SSo['gdn'][l, ci].rearrange("h k v -> k h v"), S, r=[SK], dkey=("gS_o", ci % 2))
                tt("dve", osq[:C], osb[:C], osb[:C], ALU.mult, [K_("osb")], [K_("osq")])
                P.op("dve", lambda e, a=ss[:C, 0:4], b_=osq[:C]: e.tensor_reduce(a, b_, AX.X, ALU.add), [K_("osq")], [K_("ss")])
                ts("dve", ss[:C, 0:4], ss[:C, 0:4], 1.0 / GD_HD, NORM_EPS, ALU.mult, ALU.add, [K_("ss")], [K_("ss")])
                act(ss[:C, 0:4], ss[:C, 0:4], AF.Sqrt, [K_("ss")], [K_("ss")])
                P.op("dve", lambda e, a=ss[:C, 0:4]: e.reciprocal(a, a), [K_("ss")], [K_("ss")])
                tt("dve", osb[:C], osb[:C], ss[:C, 0:4].unsqueeze(2).to_broadcast([C, 4, 128]), ALU.mult,
                   [K_("osb"), K_("ss")], [K_("osb")])
                bF = next_bank(4, 8)
                for h in range(H):
                    o_ap, i_ap, id_ap = psum[:, bF, h * C:(h + 1) * C], osb[:C, h, :], ident[:C, :C]
                    P.op("pe", lambda e, o_ap=o_ap, i_ap=i_ap, id_ap=id_ap: e.transpose(o_ap, i_ap, id_ap),
                         [K_("osb"), "ident"], [PSB(bF)])
                for h in range(H):
                    stt("dve", ycat[:, 8 + h, c0:c0 + C], psum[:, bF, h * C:(h + 1) * C], gnw[:, l:l + 1],
                        ZS[:, h, c0:c0 + C], ALU.mult, ALU.mult, [PSB(bF), ("gnw", l), ("ar_zs", h)], [("ycat", 8 + h)])
            if mode == "p" and ti == NT - 1:
                P.dma("sp", SPo['gdn'][l].rearrange("h k v -> k h v"), gst[:, l], r=[("gst", l)], dkey=("gst_o",))
                for w_ in range(3):
                    store_fm(SPo['gdn_conv'][l, w_], gch[:, l, :, w_], 12, [("gch", l)])
            if mode == "s":
                sample_out(SSo['gdn_conv'][l, :, 2, :], lambda j, w: xnew[:w, j, :], 1536, lambda j: ("ar_gxn", j))
            barrier("x")

        from_mix["gdn"] = mix_gdn

    if "rwkv" in mixers:
        RSEG = [(i * 128, 128) for i in range(12)] + [(1536, 96), (1632, 96), (1728, 128), (1856, 128)]
        rst = P.sb("rst", [128, DEPTH, 4, 64], F32)
        rsh = P.sb("rsh", [128, DEPTH, 16], F32)
        rmu = P.sb("rmu", [128, DEPTH, 16], F32)
        rpp = P.sb("rpp", [128, DEPTH, 7, 4], F32)
        blk = P.sb("blk", [128, 128], F32)
        P.op("dve", lambda e: e.memset(rst[:], 0.0), w=["rst"])
        P.op("dve", lambda e: e.memset(rsh[:], 0.0), w=["rsh"])
        P.op("dve", lambda e: e.memset(rmu[:], 0.0), w=["rmu"])
        P.op("dve", lambda e: e.memset(blk[:], 0.0), w=["blk"])
        P.op("dve", lambda e: e.memset(blk[0:64, 0:64], 1.0), w=["blk"])
        P.op("dve", lambda e: e.memset(blk[64:128, 64:128], 1.0), w=["blk"])
        for l in range(DEPTH):
            for i, (f0, n) in enumerate(RSEG):
                P.dma("sp", rmu[:n, l, i:i + 1], Wd['rwkv_mu'][l, f0:f0 + n].rearrange("(p o) -> p o", o=1),
                      w=[("rmu", l, i)], dkey=("rmu", i % 4))
            for k, nm in enumerate(['rwkv_w0', 'rwkv_a0', 'rwkv_k_k', 'rwkv_k_a', None, 'rwkv_ln_w', 'rwkv_ln_b']):
                src = Wd['rwkv_r_k'][l].rearrange("h d -> (h d)") if nm is None else Wd[nm][l]
                load_fm(rpp[:, l, k, :], src, 4, ("rpp", l, k))
        barrier("x")

        def mix_rwkv(l, N, mode, ti):
            C = CH if mode == "p" else 1
            o = 0

            def AL(n):
                nonlocal o
                a = A_(o, n)
                o += n
                return a

            V4 = lambda a, k: a.rearrange("p (i n) -> p i n", i=k)
            Rr, Kx, Vv = V4(AL(4 * N), 4), V4(AL(4 * N), 4), V4(AL(4 * N), 4)
            TW, XA = AL(N), AL(N)
            SG = V4(AL(2 * N), 2)
            WU = AL(512)
            AU = AL(512)
            GU = V4(AL(1024), 2)
            P.dma("sp", WU[:96, :], Wd['rwkv_w_up'][l], w=["ar_r_WU"])
            P.dma("sp", AU[:96, :], Wd['rwkv_a_up'][l], w=["ar_r_AU"])
            P.dma("sp", GU, Wd['rwkv_g_up'][l].rearrange("(kc p) m -> p kc m", p=128), w=["ar_r_GU"])
            if mode == "s":
                sh0 = V4(AL(16 * NS), 16)
                pnew = V4(AL(16 * NS), 16)
                Ssb = [V4(AL(256), 4) for _ in range(2)]
                stg = arena[:, ARW - 2048:ARW]
                dma_rows("sp", stg[:NS, :1984], Sin['rwkv_shift'][l], 1984, w=["ar_sin"])
                for i, (f0, n) in enumerate(RSEG):
                    to_fm(lambda j, w, i=i: sh0[:w, i, :], stg[:NS, f0:f0 + n], NS, n, ["ar_sin"],
                          lambda j, i=i: ("ar_r_sh0", i))

            def cons(i, ps, pk):
                n = RSEG[i][1]
                pb = sqb[i % 2]
                pbk = ("sqb", i % 2)
                if i < 4:
                    dst, dk = Rr[:, i, :N], ("ar_r_R", i)
                elif i < 8:
                    dst, dk = Kx[:, i - 4, :N], ("ar_r_K", i - 4)
                elif i < 12:
                    dst, dk = Vv[:, i - 8, :N], ("ar_r_V", i - 8)
                elif i == 12:
                    dst, dk = TW[:n, :N], ("ar_r_TW",)
                elif i == 13:
                    dst, dk = XA[:n, :N], ("ar_r_XA",)
                else:
                    dst, dk = SG[:, i - 14, :N], ("ar_r_SG", i - 14)
                mu = rmu[:n, l, i:i + 1]
                if mode == "p":
                    copy("act", pb[:n, 0:N], ps, [pk], [pbk])
                    tt("dve", dst[:, 1:N], pb[:n, 0:N - 1], pb[:n, 1:N], ALU.subtract, [pbk], [dk])
                    tt("dve", dst[:, 0:1], rsh[:n, l, i:i + 1], pb[:n, 0:1], ALU.subtract, [pbk, ("rsh", l, i)], [dk])
                    copy("dve", rsh[:n, l, i:i + 1], pb[:n, N - 1:N], [pbk, dk], [("rsh", l, i)])
                    stt("dve", dst, dst, mu, pb[:n, 0:N], ALU.mult, ALU.add, [dk, pbk, ("rmu", l, i)], [dk])
                else:
                    copy("act", pnew[:n, i, :], ps, [pk], [("ar_r_pn", i)])
                    tt("dve", dst, sh0[:n, i, :], pnew[:n, i, :], ALU.subtract, [("ar_r_sh0", i), ("ar_r_pn", i)], [dk])
                    stt("dve", dst, dst, mu, pnew[:n, i, :], ALU.mult, ALU.add, [dk, ("ar_r_pn", i), ("rmu", l, i)], [dk])
                if i == 12:
                    act(dst, dst, AF.Tanh, [dk], [dk])
                if i >= 14:
                    act(dst, dst, AF.Sigmoid, [dk], [dk])

            dense(Wd['w_in'][l], 16, lambda kc: (hT[:, kc, :N], ("hT", kc)), RSEG, cons, N, "rwkv", ckey=("rwkv", l))
            H = 4
            obase = o
            for half in range(2):
                o = obase
                V2 = lambda a: a.rearrange("p (i n) -> p i n", i=2)
                LW, Aa, KKb, LC, Eb = (V2(AL(2 * N)) for _ in range(5))
                BON = LW
                KH = ("ar_r_h",)
                for q in range(2):
                    hp = 2 * half + q
                    cols = slice(hp * 128, (hp + 1) * 128)
                    pw0, pa0, pkk, pka, prk, plw, plb = (rpp[:, l, k, hp:hp + 1] for k in range(7))
                    b = next_bank(0, 8)
                    mm(psum[:, b, :N], WU[:96, cols], TW[:96, :N], True, True, ["ar_r_WU", ("ar_r_TW",)], [PSB(b)])
                    act(LW[:, q, :N], psum[:, b, :N], AF.Sigmoid, [PSB(b), ("rpp", l)], [("ar_r_LW", q)], bias=pw0)
                    ts("dve", LW[:, q, :N], LW[:, q, :N], -0.6065306597126334, None, ALU.mult, None, [("ar_r_LW", q)],
                       [("ar_r_LW", q)])
                    b = next_bank(0, 8)
                    mm(psum[:, b, :N], AU[:96, cols], XA[:96, :N], True, True, ["ar_r_AU", ("ar_r_XA",)], [PSB(b)])
                    act(Aa[:, q, :N], psum[:, b, :N], AF.Sigmoid, [PSB(b), ("rpp", l)], [("ar_r_A", q)], bias=pa0)
                    ts("dve", KKb[:, q, :N], Kx[:, hp, :N], pkk, None, ALU.mult, None, [("ar_r_K", hp), ("rpp", l)],
                       [("ar_r_KK", q)])
                    sq = sqb[q]
                    sk = ("sqb", q)
                    tt("dve", sq[:, :N], KKb[:, q, :N], KKb[:, q, :N], ALU.mult, [("ar_r_KK", q)], [sk])
                    b = next_bank(0, 8)
                    mm(psum[:, b, :N], blk[:], sq[:, :N], True, True, [sk, "blk"], [PSB(b)])
                    ts("dve", sq[:, :N], psum[:, b, :N], 1e-12, None, ALU.add, None, [PSB(b)], [sk])
                    act(sq[:, :N], sq[:, :N], AF.Sqrt, [sk], [sk])
                    P.op("dve", lambda e, sq=sq: e.reciprocal(sq[:, :N], sq[:, :N]), [sk], [sk])
                    tt("dve", KKb[:, q, :N], KKb[:, q, :N], sq[:, :N], ALU.mult, [("ar_r_KK", q), sk], [("ar_r_KK", q)])
                    ts("dve", sq[:, :N], Aa[:, q, :N], -1.0, pka, ALU.add, ALU.mult, [("ar_r_A", q), ("rpp", l)], [sk])
                    stt("dve", Kx[:, hp, :N], sq[:, :N], 1.0, Kx[:, hp, :N], ALU.add, ALU.mult, [sk, ("ar_r_K", hp)],
                        [("ar_r_K", hp)])
                    tt("dve", Aa[:, q, :N], Aa[:, q, :N], KKb[:, q, :N], ALU.mult, [("ar_r_A", q), ("ar_r_KK", q)],
                       [("ar_r_A", q)])
                    if mode == "p":
                        for ci in range(N // C):
                            c0 = ci * C
                            o_ap, d1 = LC[:, q, c0:c0 + C], LW[:, q, c0:c0 + C]
                            P.op("dve", lambda e, o_ap=o_ap, d1=d1: e.tensor_tensor_scan(
                                o_ap, ones[:, 0:C], d1, 0.0, ALU.mult, ALU.add), [("ar_r_LW", q), "ones"], [("ar_r_LC", q)])
                    else:
                        copy("dve", LC[:, q, :N], LW[:, q, :N], [("ar_r_LW", q)], [("ar_r_LC", q)])
                    tt("dve", Eb[:, q, :N], LC[:, q, :N], LW[:, q, :N], ALU.subtract, [("ar_r_LC", q), ("ar_r_LW", q)],
                       [("ar_r_E", q)])
                    act(Eb[:, q, :N], Eb[:, q, :N], AF.Exp, [("ar_r_E", q)], [("ar_r_E", q)])
                    tt("dve", KKb[:, q, :N], KKb[:, q, :N], Eb[:, q, :N], ALU.mult, [("ar_r_KK", q), ("ar_r_E", q)],
                       [("ar_r_KK", q)])
                    tt("dve", sq[:, :N], Rr[:, hp, :N], Kx[:, hp, :N], ALU.mult, [("ar_r_R", hp), ("ar_r_K", hp)], [sk])
                    ts("dve", sq[:, :N], sq[:, :N], prk, None, ALU.mult, None, [sk, ("rpp", l)], [sk])
                    b = next_bank(0, 8)
                    mm(psum[:, b, :N], blk[:], sq[:, :N], True, True, [sk, "blk"], [PSB(b)])
                    tt("dve", BON[:, q, :N], psum[:, b, :N], Vv[:, hp, :N], ALU.mult, [PSB(b), ("ar_r_V", hp)],
                       [("ar_r_LW", q)])
                    act(Eb[:, q, :N], LC[:, q, :N], AF.Exp, [("ar_r_LC", q), ("ar_r_E", q)], [("ar_r_E", q)], scale=-1.0)
                    tt("dve", Aa[:, q, :N], Aa[:, q, :N], Eb[:, q, :N], ALU.mult, [("ar_r_A", q), ("ar_r_E", q)],
                       [("ar_r_A", q)])
                    tt("dve", Kx[:, hp, :N], Kx[:, hp, :N], Eb[:, q, :N], ALU.mult, [("ar_r_K", hp), ("ar_r_E", q)],
                       [("ar_r_K", hp)])
                    act(LC[:, q, :N], LC[:, q, :N], AF.Exp, [("ar_r_LC", q)], [("ar_r_LC", q)])
                    tt("dve", Rr[:, hp, :N], Rr[:, hp, :N], LC[:, q, :N], ALU.mult, [("ar_r_R", hp), ("ar_r_LC", q)],
                       [("ar_r_R", hp)])
                def ralloc():
                    d = {}
                    M4 = lambda: AL(H * C).rearrange("p (h c) -> p h c", h=H)
                    for nm in ("AakT", "BqkT", "BqaT", "Qm", "QTm", "Q2m", "QT2m", "PTm"):
                        d[nm] = M4()
                    T64 = lambda: AL(H * 64).rearrange("p (h c) -> p h c", h=H)
                    for nm in ("Vtok", "KTtok", "ATtok", "Rsb", "Usb"):
                        d[nm] = T64()
                    d["Ysq"] = d["Rsb"]
                    d["Ysb"] = d["AakT"] if C == 64 else T64()
                    d["MK"] = AL(8 * C).rearrange("p (k q h c) -> p k q h c", k=2, q=2, h=2)
                    d["st8"] = AL(8)
                    return d

                rsets = [ralloc() for _ in range(2 if mode == "s" else 1)]
                msl, msu, miu = msk[:C, 0, 0:H, :C], msk[:C, 1, 0:H, :C], msk[:C, 2, 0:H, :C]
                for ci in range(N // C):
                    c0 = ci * C
                    cs = slice(c0, c0 + C)
                    si_ = ci % len(rsets)
                    T_ = rsets[si_]
                    AakT, BqkT, BqaT, Qm, QTm, Q2m, QT2m, PTm = (T_[k] for k in (
                        "AakT", "BqkT", "BqaT", "Qm", "QTm", "Q2m", "QT2m", "PTm"))
                    Vtok, KTtok, ATtok, Rsb, Usb, Ysq, Ysb, MK, st8 = (T_[k] for k in (
                        "Vtok", "KTtok", "ATtok", "Rsb", "Usb", "Ysq", "Ysb", "MK", "st8"))
                    K_ = lambda nm, si_=si_: ("ar_r_c_" + nm, si_)
                    RN = "ar_rn%d" % si_
                    if mode == "p":
                        S = rst[:, l]
                        SK = ("rst", l)
                    else:
                        S = Ssb[ci % 2]
                        SK = ("ar_r_S", ci % 2, half)
                        if half == 0:
                            P.dma("sp", S, Sin['rwkv_wkv'][l, ci].rearrange("(hp h2) k v -> (h2 k) hp v", h2=2),
                                  w=[("ar_r_S", ci % 2)])
                            SK = ("ar_r_S", ci % 2)
                        else:
                            SK = ("ar_r_S", ci % 2)
                    if mode == "s" and half == 1:
                        P.dma("sp", S, Sin['rwkv_wkv'][l, ci].rearrange("(hp h2) k v -> (h2 k) hp v", h2=2),
                              w=[("ar_r_S", ci % 2)])
                    banks = [next_bank(0, 8) for _ in range(5)]
                    for q in range(2):
                        hp = 2 * half + q
                        for h2 in range(2):
                            hmask = blk[:, 64 * h2:64 * h2 + 1]
                            ts("dve", MK[:, 0, q, h2, :C], KKb[:, q, cs], hmask, None, ALU.mult, None,
                               [("ar_r_KK", q), "blk"], [K_("MK")])
                            ts("dve", MK[:, 1, q, h2, :C], Rr[:, hp, cs], hmask, None, ALU.mult, None,
                               [("ar_r_R", hp), "blk"], [K_("MK")])
                    for q in range(2):
                        hp = 2 * half + q
                        for h2 in range(2):
                            hl = q * 2 + h2
                            kt, at = Kx[:, hp, cs], Aa[:, q, cs]
                            kp, qh = MK[:, 0, q, h2, :C], MK[:, 1, q, h2, :C]
                            rk = [("ar_r_K", hp), ("ar_r_A", q), K_("MK")]
                            oc = slice(hl * C, (hl + 1) * C)
                            mm(psum[:C, banks[0], oc], kt, kp, True, True, rk, [PSB(banks[0])])
                            mm(psum[:C, banks[1], oc], kt, qh, True, True, rk, [PSB(banks[1])])
                            mm(psum[:C, banks[2], oc], at, kp, True, True, rk, [PSB(banks[2])])
                            mm(psum[:C, banks[3], oc], at, qh, True, True, rk, [PSB(banks[3])])
                            mm(psum[:C, banks[4], oc], kp, at, True, True, rk, [PSB(banks[4])])
                    pv = lambda bi: psum[:C, banks[bi], 0:H * C].rearrange("p (h c) -> p h c", h=H)
                    tt("dve", AakT[:C], pv(0), msu, ALU.mult, [PSB(banks[0]), "msk"], [K_("AakT")])
                    tt("dve", BqkT[:C], pv(1), miu, ALU.mult, [PSB(banks[1]), "msk"], [K_("BqkT")])
                    stt("dve", Qm[:C], pv(2), -1.0, msu, ALU.mult, ALU.mult, [PSB(banks[2]), "msk"], [(RN, "Q")])
                    tt("dve", BqaT[:C], pv(3), miu, ALU.mult, [PSB(banks[3]), "msk"], [K_("BqaT")])
                    stt("dve", QTm[:C], pv(4), -1.0, msl, ALU.mult, ALU.mult, [PSB(banks[4]), "msk"], [(RN, "QT")])
                    neumann(Qm[:C], QTm[:C], Q2m[:C], QT2m[:C], PTm[:C], H, C, RN)
                    for (srcfn, dstt, nm) in ((lambda q: Vv[:, 2 * half + q, cs], Vtok, "Vtok"),
                                              (lambda q: Kx[:, 2 * half + q, cs], KTtok, "KTtok"),
                                              (lambda q: Aa[:, q, cs], ATtok, "ATtok")):
                        b = next_bank(0, 8)
                        for q in range(2):
                            o_ap, i_ap = psum[:C, b, q * 128:(q + 1) * 128], srcfn(q)
                            P.op("pe", lambda e, o_ap=o_ap, i_ap=i_ap: e.transpose(o_ap, i_ap, ident[:, :]),
                                 [("ar_r_V",), ("ar_r_K",), ("ar_r_A",), "ident"], [PSB(b)])
                        copy("act", dstt[:C], psum[:C, b, 0:256].rearrange("p (h c) -> p h c", h=H), [PSB(b)], [K_(nm)])
                    b = next_bank(0, 8)
                    for q in range(2):
                        hp = 2 * half + q
                        for h2 in range(2):
                            hl = q * 2 + h2
                            rows = slice(64 * h2, 64 * h2 + 64)
                            oc = slice(hl * 64, (hl + 1) * 64)
                            mm(psum[:C, b, oc], MK[:, 0, q, h2, :C], S[:, hp, :], True, False, [K_("MK"), SK], [PSB(b)])
                            mm(psum[:C, b, oc], AakT[:C, hl, :], Vtok[:C, hl, :], False, True, [K_("AakT"), K_("Vtok")],
                               [PSB(b)])
                    ts("dve", Rsb[:C], psum[:C, b, 0:256].rearrange("p (h c) -> p h c", h=H), -1.0, None, ALU.mult, None,
                       [PSB(b)], [K_("Rsb")])
                    b = next_bank(0, 8)
                    for hl in range(H):
                        mm(psum[:C, b, hl * 64:(hl + 1) * 64], PTm[:C, hl, :], Rsb[:C, hl, :], True, True,
                           [(RN, "P"), K_("Rsb")], [PSB(b)])
                    copy("act", Usb[:C], psum[:C, b, 0:256].rearrange("p (h c) -> p h c", h=H), [PSB(b)], [K_("Usb")])
                    b = next_bank(0, 8)
                    for q in range(2):
                        hp = 2 * half + q
                        for h2 in range(2):
                            hl = q * 2 + h2
                            rows = slice(64 * h2, 64 * h2 + 64)
                            oc = slice(hl * 64, (hl + 1) * 64)
                            mm(psum[:C, b, oc], MK[:, 1, q, h2, :C], S[:, hp, :], True, False, [K_("MK"), SK], [PSB(b)])
                            mm(psum[:C, b, oc], BqkT[:C, hl, :], Vtok[:C, hl, :], False, False, [K_("BqkT"), K_("Vtok")],
                               [PSB(b)])
                            mm(psum[:C, b, oc], BqaT[:C, hl, :], Usb[:C, hl, :], False, True, [K_("BqaT"), K_("Usb")],
                               [PSB(b)])
                    copy("act", Ysb[:C], psum[:C, b, 0:256].rearrange("p (h c) -> p h c", h=H), [PSB(b)], [K_("AakT")])
                    b = next_bank(0, 8)
                    for q in range(2):
                        for h2 in range(2):
                            hl = q * 2 + h2
                            oc = slice(hl * 64, (hl + 1) * 64)
                            mm(psum[:, b, oc], KTtok[:C, 2 * q:2 * q + 2, :].rearrange("p h c -> p (h c)"), Vtok[:C, hl, :],
                               True, False, [K_("KTtok"), K_("Vtok")], [PSB(b)])
                            mm(psum[:, b, oc], ATtok[:C, 2 * q:2 * q + 2, :].rearrange("p h c -> p (h c)"), Usb[:C, hl, :],
                               False, True, [K_("ATtok"), K_("Usb")], [PSB(b)])
                    for q in range(2):
                        hp = 2 * half + q
                        for h2 in range(2):
                            hl = q * 2 + h2
                            rows = slice(64 * h2, 64 * h2 + 64)
                            tt("dve", S[rows, hp, :], S[rows, hp, :], psum[rows, b, hl * 64:(hl + 1) * 64], ALU.add,
                               [SK, PSB(b)], [SK])
                            ts("dve", S[rows, hp, :], S[rows, hp, :], LC[rows, q, c0 + C - 1:c0 + C], None, ALU.mult, None,
                               [SK, ("ar_r_LC", q)], [SK])
                    if mode == "s":
                        for q in range(2):
                            hp = 2 * half + q
                            P.dma("sp", SSo['rwkv_wkv'][l, ci, 2 * hp:2 * hp + 2].rearrange("h2 k v -> (h2 k) v"),
                                  S[:, hp, :], r=[SK], dkey=("rS_o", ci % 2, q))
                    P.op("dve", lambda e, a=st8[:C, 0:4], b_=Ysb[:C]: e.tensor_reduce(a, b_, AX.X, ALU.add), [K_("AakT")],
                         [K_("st")])
                    ts("dve", st8[:C, 0:4], st8[:C, 0:4], -1.0 / 64, None, ALU.mult, None, [K_("st")], [K_("st")])
                    tt("dve", Ysb[:C], Ysb[:C], st8[:C, 0:4].unsqueeze(2).to_broadcast([C, H, 64]), ALU.add,
                       [K_("AakT"), K_("st")], [K_("AakT")])
                    tt("dve", Ysq[:C], Ysb[:C], Ysb[:C], ALU.mult, [K_("AakT")], [K_("Rsb")])
                    P.op("dve", lambda e, a=st8[:C, 4:8], b_=Ysq[:C]: e.tensor_reduce(a, b_, AX.X, ALU.add), [K_("Rsb")],
                         [K_("st")])
                    ts("dve", st8[:C, 4:8], st8[:C, 4:8], 1.0 / 64, 64e-5, ALU.mult, ALU.add, [K_("st")], [K_("st")])
                    act(st8[:C, 4:8], st8[:C, 4:8], AF.Sqrt, [K_("st")], [K_("st")])
                    P.op("dve", lambda e, a=st8[:C, 4:8]: e.reciprocal(a, a), [K_("st")], [K_("st")])
                    tt("dve", Ysb[:C], Ysb[:C], st8[:C, 4:8].unsqueeze(2).to_broadcast([C, H, 64]), ALU.mult,
                       [K_("AakT"), K_("st")], [K_("AakT")])
                    bF = next_bank(0, 8)
                    for q in range(2):
                        o_ap, i_ap, id_ap = psum[:, bF, q * C:(q + 1) * C], Ysb[:C, 2 * q:2 * q + 2, :].rearrange(
                            "p h c -> p (h c)"), ident[:C, :C]
                        P.op("pe", lambda e, o_ap=o_ap, i_ap=i_ap, id_ap=id_ap: e.transpose(o_ap, i_ap, id_ap),
                             [K_("AakT"), "ident"], [PSB(bF)])
                    bG = next_bank(0, 8)
                    for q in range(2):
                        hp = 2 * half + q
                        for kc in range(2):
                            mm(psum[:, bG, q * C:(q + 1) * C], GU[:, kc, hp * 128:(hp + 1) * 128], SG[:, kc, cs], kc == 0,
                               kc == 1, ["ar_r_GU", ("ar_r_SG", kc)], [PSB(bG)])
                    for q in range(2):
                        hp = 2 * half + q
                        yf = Eb[:, q, cs]
                        ts("dve", yf, psum[:, bF, q * C:(q + 1) * C], rpp[:, l, 5, hp:hp + 1], rpp[:, l, 6, hp:hp + 1],
                           ALU.mult, ALU.add, [PSB(bF), ("rpp", l)], [("ar_r_E", q)])
                        tt("dve", yf, yf, BON[:, q, cs], ALU.add, [("ar_r_E", q), ("ar_r_LW", q)], [("ar_r_E", q)])
                        tt("dve", ycat[:, hp, cs], yf, psum[:, bG, q * C:(q + 1) * C], ALU.mult, [("ar_r_E", q), PSB(bG)],
                           [("ycat", hp)])
                barrier("x")
            if mode == "p" and ti == NT - 1:
                P.dma("sp", SPo['rwkv_wkv'][l].rearrange("(hp h2) k v -> (h2 k) hp v", h2=2), rst[:, l], r=[("rst", l)],
                      dkey=("rst_o",))
                for i, (f0, n) in enumerate(RSEG):
                    P.dma("sp", SPo['rwkv_shift'][l, f0:f0 + n].rearrange("(p o) -> p o", o=1), rsh[:n, l, i:i + 1],
                          r=[("rsh", l, i)], dkey=("rsh_o", i % 4))
            if mode == "s":
                stg2 = arena[:, ARW - 4096:ARW - 2048]
                for i, (f0, n) in enumerate(RSEG):
                    b = next_bank(0, 8)
                    o_ap, i_ap, id_ap = psum[:NS, b, 0:n], pnew[:n, i, :], ident[:n, :n]
                    P.op("pe", lambda e, o_ap=o_ap, i_ap=i_ap, id_ap=id_ap: e.transpose(o_ap, i_ap, id_ap),
                         [("ar_r_pn", i), "ident"], [PSB(b)])
                    copy("dve", stg2[:NS, f0:f0 + n], psum[:NS, b, 0:n], [PSB(b)], [("ar_r_stg2", i)])
                dma_rows("sp", SSo['rwkv_shift'][l], stg2[:NS, :1984], 1984, r=["ar_r_stg2"], dkey="ar_r_stg2")
            barrier("x")

        from_mix["rwkv"] = mix_rwkv

    if "s5" in mixers:
        import math
        PI = math.pi
        NLEV = 9
        s5m = P.dram("s5m", [DEPTH, 4, 128, 16 * 128], F32).ap()
        s5c = P.sb("s5c", [128, DEPTH, 12, 16], F32)
        s5pw = P.sb("s5pw", [128, DEPTH, 16, NLEV, 3], F32)
        s5h = P.sb("s5h", [128, DEPTH, 16, 2], F32)
        s5d = P.sb("s5d", [128, DEPTH, 2, 4], F32)
        par = P.sb("par", [128, 4], F32)
        P.op("dve", lambda e: e.memset(s5h[:], 0.0), w=["s5h"])
        G8 = A_(0, 8)
        P.op("pool", lambda e: e.memset(G8, 1.0), w=["ar_g8"])
        P.op("pool", lambda e: e.affine_select(G8, G8, pattern=[[-16, 8]], compare_op=ALU.is_ge, fill=0.0, base=0,
                                               channel_multiplier=1), r=["ar_g8"], w=["ar_g8"])
        P.op("pool", lambda e: e.affine_select(G8, G8, pattern=[[16, 8]], compare_op=ALU.is_ge, fill=0.0, base=15,
                                               channel_multiplier=-1), r=["ar_g8"], w=["ar_g8"])
        G8v = G8.rearrange("p (i two) -> p two i", two=2)
        P.op("dve", lambda e: e.tensor_reduce(par[:, 0:2], G8v, AX.X, ALU.add), r=["ar_g8"], w=["par"])
        ts("dve", par[:, 2:4], par[:, 0:2], -1.0, None, ALU.mult, None, ["par"], ["par"])
        XZ = [A_(64 + m * 128, 128) for m in range(4)]
        for m in range(4):
            P.op("dve", lambda e, m=m: e.memset(XZ[m], 0.0), w=[("ar_xz", m)])
        for l in range(DEPTH):
            C_ = lambda k: s5c[:, l, k, :]
            CK = lambda k: ("s5c", l, k)
            load_fm(C_(0), Wd['s5_lambda_re'][l].rearrange("g n -> (g n)"), 16, CK(0))
            load_fm(C_(1), Wd['s5_lambda_im'][l].rearrange("g n -> (g n)"), 16, CK(1))
            load_fm(s5d[:, l, 0, :], Wd['s5_d'][l], 4, ("s5d", l, 0))
            load_fm(s5d[:, l, 1, :], Wd['s5_glu_b'][l], 4, ("s5d", l, 1))
            ldt = A_(32, 32)
            P.dma("sp", ldt[0:1, :], Wd['s5_log_dt'][l:l + 1, :], w=["ar_ldt"])
            b = next_bank(4, 8)
            mm(psum[:, b, 0:32], ones[0:1, :], ldt[0:1, :], True, True, ["ar_ldt", "ones"], [PSB(b)])
            dtb = A_(576, 32)
            act(dtb, psum[:, b, 0:32], AF.Exp, [PSB(b)], ["ar_dtb"])
            dv = dtb.rearrange("p (j g) -> p j g", g=2)
            copy("dve", s5c[0:64, l, 2, :], dv[0:64, :, 0], ["ar_dtb"], [CK(2)])
            copy("dve", s5c[64:128, l, 2, :], dv[64:128, :, 1], ["ar_dtb"], [CK(2)])
            tt("dve", C_(7), C_(0), C_(2), ALU.mult, [CK(0), CK(2)], [CK(7)])
            act(C_(7), C_(7), AF.Exp, [CK(7)], [CK(7)])
            tt("dve", C_(8), C_(1), C_(2), ALU.mult, [CK(1), CK(2)], [CK(8)])
            for slot, shift in ((4, 0.0), (3, PI / 2)):
                ts("dve", C_(9), C_(8), shift, None, ALU.add, None, [CK(8)], [CK(9)])
                copy("dve", C_(10), C_(9), [CK(9)], [CK(10)])
                for k in range(1, 7):
                    ts("dve", C_(11), C_(9), (2 * k - 1) * PI, -2 * PI, ALU.is_ge, ALU.mult, [CK(9)], [CK(11)])
                    tt("dve", C_(10), C_(10), C_(11), ALU.add, [CK(10), CK(11)], [CK(10)])
                act(C_(slot), C_(10), AF.Sin, [CK(10)], [CK(slot)])
                tt("dve", C_(slot), C_(slot), C_(7), ALU.mult, [CK(slot), CK(7)], [CK(slot)])
            tt("dve", C_(9), C_(0), C_(0), ALU.mult, [CK(0)], [CK(9)])
            tt("dve", C_(10), C_(1), C_(1), ALU.mult, [CK(1)], [CK(10)])
            tt("dve", C_(9), C_(9), C_(10), ALU.add, [CK(9), CK(10)], [CK(9)])
            P.op("dve", lambda e, l=l: e.reciprocal(s5c[:, l, 9, :], s5c[:, l, 9, :]), [CK(9)], [CK(9)])
            ts("dve", C_(10), C_(3), -1.0, None, ALU.add, None, [CK(3)], [CK(10)])
            tt("dve", C_(5), C_(10), C_(0), ALU.mult, [CK(10), CK(0)], [CK(5)])
            tt("dve", C_(11), C_(4), C_(1), ALU.mult, [CK(4), CK(1)], [CK(11)])
            tt("dve", C_(5), C_(5), C_(11), ALU.add, [CK(5), CK(11)], [CK(5)])
            tt("dve", C_(5), C_(5), C_(9), ALU.mult, [CK(5), CK(9)], [CK(5)])
            tt("dve", C_(6), C_(4), C_(0), ALU.mult, [CK(4), CK(0)], [CK(6)])
            tt("dve", C_(11), C_(10), C_(1), ALU.mult, [CK(10), CK(1)], [CK(11)])
            tt("dve", C_(6), C_(6), C_(11), ALU.subtract, [CK(6), CK(11)], [CK(6)])
            tt("dve", C_(6), C_(6), C_(9), ALU.mult, [CK(6), CK(9)], [CK(6)])
            copy("dve", s5pw[:, l, :, 0, 0], C_(3), [CK(3)], [("s5pw", l)])
            copy("dve", s5pw[:, l, :, 0, 1], C_(4), [CK(4)], [("s5pw", l)])
            for lev in range(1, NLEV):
                pr, pi_ = s5pw[:, l, :, lev - 1, 0], s5pw[:, l, :, lev - 1, 1]
                tt("dve", C_(10), pr, pr, ALU.mult, [("s5pw", l)], [CK(10)])
                tt("dve", C_(11), pi_, pi_, ALU.mult, [("s5pw", l)], [CK(11)])
                tt("dve", s5pw[:, l, :, lev, 0], C_(10), C_(11), ALU.subtract, [CK(10), CK(11)], [("s5pw", l)])
                tt("dve", C_(10), pr, pi_, ALU.mult, [("s5pw", l)], [CK(10)])
                ts("dve", s5pw[:, l, :, lev, 1], C_(10), 2.0, None, ALU.mult, None, [CK(10)], [("s5pw", l)])
            ts("dve", s5pw[:, l, :, :, 2], s5pw[:, l, :, :, 1], -1.0, None, ALU.mult, None, [("s5pw", l)], [("s5pw", l)])
            braw = [A_(1024 + k * 256, 256).rearrange("p (j c) -> p j c", j=16) for k in range(4)]
            for k, nm in enumerate(['s5_b_re', 's5_b_im']):
                P.dma("sp", braw[k], Wd[nm][l].rearrange("(j gl) n c -> (gl n) j c", gl=2), w=[("ar_braw", k)])
            fre = s5c[:, l, 5, :].unsqueeze(2).to_broadcast([128, 16, 16])
            fim = s5c[:, l, 6, :].unsqueeze(2).to_broadcast([128, 16, 16])
            tmpb = A_(2048, 256).rearrange("p (j c) -> p j c", j=16)
            tt("dve", braw[2], braw[0], fre, ALU.mult, [("ar_braw", 0), CK(5)], [("ar_braw", 2)])
            tt("dve", tmpb, braw[1], fim, ALU.mult, [("ar_braw", 1), CK(6)], ["ar_tmpb"])
            tt("dve", braw[2], braw[2], tmpb, ALU.subtract, [("ar_braw", 2), "ar_tmpb"], [("ar_braw", 2)])
            tt("dve", braw[3], braw[1], fre, ALU.mult, [("ar_braw", 1), CK(5)], [("ar_braw", 3)])
            tt("dve", tmpb, braw[0], fim, ALU.mult, [("ar_braw", 0), CK(6)], ["ar_tmpb"])
            tt("dve", braw[3], braw[3], tmpb, ALU.add, [("ar_braw", 3), "ar_tmpb"], [("ar_braw", 3)])
            MT = A_(4096, 2048).rearrange("p (j m) -> p j m", j=16)
            for kind in range(2):
                for j in range(16):
                    m = j % 4
                    X = XZ[m]
                    copy("dve", X[0:64, 32 * m:32 * m + 16], braw[2 + kind][0:64, j, :], [("ar_braw", 2 + kind)],
                         [("ar_xz", m)])
                    copy("dve", X[64:128, 32 * m + 16:32 * m + 32], braw[2 + kind][64:128, j, :],
                         [("ar_braw", 2 + kind)], [("ar_xz", m)])
                    b = next_bank(4, 8)
                    transp(psum[:, b, 0:128], X, 128, [("ar_xz", m)], [PSB(b)])
                    copy("act", MT[:, j, :], psum[:, b, 0:128], [PSB(b)], [("ar_mt", j)])
                P.dma("sp", s5m[l, kind], MT.rearrange("p j m -> p (j m)"), r=["ar_mt"], w=[("s5m", l, kind)],
                      dkey=("s5m_o",))
            for kind, nm in enumerate(['s5_c_re', 's5_c_im']):
                P.op("dve", lambda e: e.memset(MT, 0.0), w=["ar_mt"])
                for ct in range(4):
                    ctile = A_(3072, 64)
                    Z = A_(3200, 128)
                    P.dma("sp", ctile, Wd[nm][l].rearrange("g c n -> (g c) n")[ct * 128:(ct + 1) * 128, :],
                          w=["ar_ctile"])
                    ts("dve", Z[:, 0:64], ctile, par[:, 2 * kind:2 * kind + 1], None, ALU.mult, None,
                       ["ar_ctile", "par"], ["ar_z"])
                    ts("dve", Z[:, 64:128], ctile, par[:, 2 * kind + 1:2 * kind + 2], None, ALU.mult, None,
                       ["ar_ctile", "par"], ["ar_z"])
                    b = next_bank(4, 8)
                    transp(psum[:, b, 0:128], Z, 128, ["ar_z"], [PSB(b)])
                    for m in range(4):
                        copy("act", MT[:, 4 * ct + m, 32 * m:32 * m + 32], psum[:, b, 32 * m:32 * m + 32], [PSB(b)],
                             [("ar_mt", 4 * ct + m)])
                P.dma("sp", s5m[l, 2 + kind], MT.rearrange("p j m -> p (j m)"), r=["ar_mt"], w=[("s5m", l, 2 + kind)],
                      dkey=("s5m_o",))
        barrier("x")

        def mix_s5(l, N, mode, ti):
            M = A_(0, 8192).rearrange("p (k j m) -> p k j m", k=4, j=16)
            for k in range(4):
                P.dma("sp", M[:, k].rearrange("p j m -> p (j m)"), s5m[l, k], r=[("s5m", l, k)], w=[("ar_M", k)])
            o = 8192
            U = A_(o, 4 * N).rearrange("p (c n) -> p c n", c=4)
            o += 4 * N
            HB = [[[A_(o + ((s_ * 2 + pp) * 2 + ri) * N, N) for ri in range(2)] for pp in range(2)] for s_ in range(2)]
            o += 8 * N
            ZF = A_(o, 4 * N).rearrange("p (c n) -> p c n", c=4)
            o += 4 * N
            ZB = A_(o, 2 * N).bitcast(BF16).rearrange("p (c n) -> p c n", c=4)
            o += 2 * N
            if mode == "s":
                H0 = A_(o, 32 * NS).rearrange("p (r j n) -> p r j n", r=2, j=16)
                o += 32 * NS
                HN = A_(o, 32 * NS).rearrange("p (r j n) -> p r j n", r=2, j=16)
                o += 32 * NS
                sample_in(H0[:, 0], Sin['s5_re'][l].rearrange("b g n -> b (g n)"), 2048, "ar_h0r")
                sample_in(H0[:, 1], Sin['s5_im'][l].rearrange("b g n -> b (g n)"), 2048, "ar_h0i")

            def cons_u(i, ps, pk):
                copy("act", U[:, i, :N], ps, [pk], [("ar_U", i)])

            dense(Wd['w_in'][l], 16, lambda kc: (hT[:, kc, :N], ("hT", kc)),
                  [(OFF_S5 + c * 128, 128) for c in range(4)], cons_u, N, "s5u", ckey=("s5u", l))
            for ct in range(4):
                by = 4 + ct % 2
                for m in range(4):
                    j = 4 * ct + m
                    set_ = HB[j % 2]
                    sk = ("ar_hb", j % 2)
                    cur = 0
                    for ri in range(2):
                        b = next_bank(6, 8)
                        mm(psum[:, b, :N], M[:, ri, j, :], U[:, ct, :N], True, True, [("ar_M", ri), ("ar_U", ct)], [PSB(b)])
                        copy("act" if ri else "dve", set_[0][ri], psum[:, b, :N], [PSB(b)], [sk])
                    ar, ai, nai = (s5pw[:, l, j, 0, k:k + 1] for k in range(3))
                    if mode == "p":
                        h0r, h0i = s5h[:, l, j, 0:1], s5h[:, l, j, 1:2]
                        br0, bi0 = set_[0][0][:, 0:1], set_[0][1][:, 0:1]
                        stt("dve", br0, h0r, ar, br0, ALU.mult, ALU.add, [("s5h", l, j), ("s5pw", l), sk], [sk])
                        stt("dve", br0, h0i, nai, br0, ALU.mult, ALU.add, [("s5h", l, j), ("s5pw", l), sk], [sk])
                        stt("dve", bi0, h0i, ar, bi0, ALU.mult, ALU.add, [("s5h", l, j), ("s5pw", l), sk], [sk])
                        stt("dve", bi0, h0r, ai, bi0, ALU.mult, ALU.add, [("s5h", l, j), ("s5pw", l), sk], [sk])
                        lev = 0
                        d = 1
                        while d < N:
                            src, dst = set_[cur], set_[1 - cur]
                            pr, pi_, npi = (s5pw[:, l, j, lev, k:k + 1] for k in range(3))
                            for ri in range(2):
                                copy("dve", dst[ri][:, 0:d], src[ri][:, 0:d], [sk], [sk])
                            stt("dve", dst[0][:, d:N], src[1][:, 0:N - d], npi, src[0][:, d:N], ALU.mult, ALU.add,
                                [sk, ("s5pw", l)], [sk])
                            stt("dve", dst[0][:, d:N], src[0][:, 0:N - d], pr, dst[0][:, d:N], ALU.mult, ALU.add,
                                [sk, ("s5pw", l)], [sk])
                            stt("dve", dst[1][:, d:N], src[0][:, 0:N - d], pi_, src[1][:, d:N], ALU.mult, ALU.add,
                                [sk, ("s5pw", l)], [sk])
                            stt("dve", dst[1][:, d:N], src[1][:, 0:N - d], pr, dst[1][:, d:N], ALU.mult, ALU.add,
                                [sk, ("s5pw", l)], [sk])
                            cur = 1 - cur
                            d *= 2
                            lev += 1
                        fin = set_[cur]
                        copy("dve", s5h[:, l, j, 0:1], fin[0][:, N - 1:N], [sk], [("s5h", l, j)])
                        copy("dve", s5h[:, l, j, 1:2], fin[1][:, N - 1:N], [sk], [("s5h", l, j)])
                    else:
                        fin = [HN[:, 0, j, :], HN[:, 1, j, :]]
                        fk = ("ar_hn", j)
                        h0r, h0i = H0[:, 0, j, :], H0[:, 1, j, :]
                        stt("dve", fin[0], h0r, ar, set_[0][0], ALU.mult, ALU.add, [("ar_h0r", j), sk, ("s5pw", l)], [fk])
                        stt("dve", fin[0], h0i, nai, fin[0], ALU.mult, ALU.add, [("ar_h0i", j), fk, ("s5pw", l)], [fk])
                        stt("dve", fin[1], h0i, ar, set_[0][1], ALU.mult, ALU.add, [("ar_h0i", j), sk, ("s5pw", l)], [fk])
                        stt("dve", fin[1], h0r, ai, fin[1], ALU.mult, ALU.add, [("ar_h0r", j), fk, ("s5pw", l)], [fk])
                        sk = fk
                    mm(psum[:, by, :N], M[:, 2, j, :], fin[0], m == 0, False, [("ar_M", 2), sk], [PSB(by)])
                    mm(psum[:, by, :N], M[:, 3, j, :], fin[1], False, m == 3, [("ar_M", 3), sk], [PSB(by)])
                stt("dve", ZF[:, ct, :N], U[:, ct, :N], s5d[:, l, 0, ct:ct + 1], psum[:, by, :N], ALU.mult, ALU.add,
                    [("ar_U", ct), ("s5d", l), PSB(by)], [("ar_zf", ct)])
                act(ZF[:, ct, :N], ZF[:, ct, :N], AF.Gelu_apprx_tanh, [("ar_zf", ct)], [("ar_zf", ct)])
                copy("dve", ZB[:, ct, :N], ZF[:, ct, :N], [("ar_zf", ct)], [("ar_zb", ct)])

            def cons_glu(i, ps, pk):
                sg = HB[0][0][0]
                act(sg[:, :N], ps, AF.Sigmoid, [pk, ("s5d", l)], [("ar_hb", 0)], bias=s5d[:, l, 1, i:i + 1])
                tt("dve", ycat[:, 4 + i, :N], ZF[:, i, :N], sg[:, :N], ALU.mult, [("ar_zf", i), ("ar_hb", 0)],
                   [("ycat", 4 + i)])

            dense(Wd['s5_glu_w'][l], 4, lambda kc: (ZB[:, kc, :N], ("ar_zb", kc)), [(c * 128, 128) for c in range(4)],
                  cons_glu, N, "s5glu", ckey=("s5glu", l))
            if mode == "p" and ti == NT - 1:
                store_fm(SPo['s5_re'][l].rearrange("g n -> (g n)"), s5h[:, l, :, 0], 16, [("s5h", l)])
                store_fm(SPo['s5_im'][l].rearrange("g n -> (g n)"), s5h[:, l, :, 1], 16, [("s5h", l)])
            if mode == "s":
                sample_out(SSo['s5_re'][l].rearrange("b g n -> b (g n)"), lambda j, w: HN[:w, 0, j, :], 2048,
                           lambda j: ("ar_hn", j))
                sample_out(SSo['s5_im'][l].rearrange("b g n -> b (g n)"), lambda j, w: HN[:w, 1, j, :], 2048,
                           lambda j: ("ar_hn", j))
            barrier("x")

        from_mix["s5"] = mix_s5

    if "lru" in mixers:
        lrh = P.sb("lrh", [128, DEPTH, 4, 3], F32)
        lrs = P.sb("lrs", [128, DEPTH, 4], F32)
        lrp = P.sb("lrp", [128, DEPTH, 4, 8], F32)
        P.op("dve", lambda e: e.memset(lrh[:], 0.0), w=["lrh"])
        P.op("dve", lambda e: e.memset(lrs[:], 0.0), w=["lrs"])
        for l in range(DEPTH):
            for k in range(4):
                load_fm(lrp[:, l, :, k], Wd['lru_conv_w'][l, k], 4, ("lrp", l, k))
            for k, nm in enumerate(['lru_conv_b', 'lru_br', 'lru_bi', 'lru_lambda']):
                load_fm(lrp[:, l, :, 4 + k], Wd[nm][l], 4, ("lrp", l, 4 + k))
            act(lrp[:, l, :, 7], lrp[:, l, :, 7], AF.Exp, [("lrp", l, 7)], [("lrp", l, 7)], scale=-1.0)
            act(lrp[:, l, :, 7], lrp[:, l, :, 7], AF.Ln, [("lrp", l, 7)], [("lrp", l, 7)], bias=1.0)
            ts("dve", lrp[:, l, :, 7], lrp[:, l, :, 7], -8.0, None, ALU.mult, None, [("lrp", l, 7)], [("lrp", l, 7)])

        def mix_lru(l, N, mode, ti):
            o = 0
            xe = [A_(o + c * (N + 3), N + 3) for c in range(4)]
            o += 4 * (N + 3)
            xc = [A_(o + c * N, N) for c in range(4)]
            o += 4 * N
            hb = [A_(o + c * N, N) for c in range(4)]
            o += 4 * N
            tmp = [A_(o + k * N, N) for k in range(8)]
            o += 8 * N
            lrwa = A_(o, 1024).rearrange("p (w c m) -> p w c m", w=2, c=4)
            o += 1024
            P.op("dve", lambda e: e.memset(lrwa, 0.0), w=["ar_lrw"])
            for wi_, nm in enumerate(['lru_wr', 'lru_wi']):
                for c in range(4):
                    for h2 in range(2):
                        P.dma("sp", lrwa[64 * h2:64 * h2 + 64, wi_, c, 64 * h2:64 * h2 + 64], Wd[nm][l, 2 * c + h2],
                              r=[], w=[("ar_lrw", wi_, c)], dkey=("lrw", h2))
            if mode == "s":
                hist = A_(o, 12 * NS).rearrange("p (j n) -> p j n", j=12)
                o += 12 * NS
                h0 = A_(o, 4 * NS).rearrange("p (j n) -> p j n", j=4)
                o += 4 * NS
                sample_in(hist, Sin['lru_conv'][l].rearrange("b w f -> b (w f)"), 1536, "ar_lhist")
                sample_in(h0, Sin['lru_h'][l], 512, "ar_lh0")
                for w_ in range(2):
                    P.dma("sp", SSo['lru_conv'][l, :, w_, :], Sin['lru_conv'][l, :, w_ + 1, :], r=[],
                          w=[("o_lruc", w_)])

            def cons(i, ps, pk):
                if i < 4:
                    c = i
                    w0, w1, w2, w3, cb, br, bi, cc_ = (lrp[:, l, c, k:k + 1] for k in range(8))
                    X, XK = xc[c], ("ar_lxc", c)
                    if mode == "p":
                        u, uk = xe[c], ("ar_lxe", c)
                        copy("act", u[:, 3:3 + N], ps, [pk], [uk])
                        copy("dve", u[:, 0:3], lrh[:, l, c, :], [("lrh", l, c)], [uk])
                        copy("dve", lrh[:, l, c, :], u[:, N:N + 3], [uk], [("lrh", l, c)])
                        ts("dve", X, u[:, 3:3 + N], w3, cb, ALU.mult, ALU.add, [uk, ("lrp", l)], [XK])
                        stt("dve", X, u[:, 2:2 + N], w2, X, ALU.mult, ALU.add, [uk, XK, ("lrp", l)], [XK])
                        stt("dve", X, u[:, 1:1 + N], w1, X, ALU.mult, ALU.add, [uk, XK, ("lrp", l)], [XK])
                        stt("dve", X, u[:, 0:N], w0, X, ALU.mult, ALU.add, [uk, XK, ("lrp", l)], [XK])
                    else:
                        u, uk = xe[c], ("ar_lxe", c)
                        copy("act", u[:, 0:N], ps, [pk], [uk])
                        ts("dve", X, u[:, 0:N], w3, cb, ALU.mult, ALU.add, [uk, ("lrp", l)], [XK])
                        stt("dve", X, hist[:, 8 + c, :], w2, X, ALU.mult, ALU.add, [("ar_lhist", 8 + c), XK, ("lrp", l)], [XK])
                        stt("dve", X, hist[:, 4 + c, :], w1, X, ALU.mult, ALU.add, [("ar_lhist", 4 + c), XK, ("lrp", l)], [XK])
                        stt("dve", X, hist[:, c, :], w0, X, ALU.mult, ALU.add, [("ar_lhist", c), XK, ("lrp", l)], [XK])
                    b1 = next_bank(4, 8)
                    mm(psum[:, b1, :N], lrwa[:, 0, c, :], X, True, True, [("ar_lrw", 0, c), XK], [PSB(b1)])
                    r_, i_, a_, q_ = tmp[0], tmp[1], tmp[2], tmp[3]
                    act(r_, psum[:, b1, :N], AF.Sigmoid, [PSB(b1)], [("ar_ltmp", 0)], bias=br)
                    b2 = next_bank(4, 8)
                    mm(psum[:, b2, :N], lrwa[:, 1, c, :], X, True, True, [("ar_lrw", 1, c), XK], [PSB(b2)])
                    act(i_, psum[:, b2, :N], AF.Sigmoid, [PSB(b2)], [("ar_ltmp", 1)], bias=bi)
                    act(a_, r_, AF.Exp, [("ar_ltmp", 0), ("lrp", l)], [("ar_ltmp", 2)], scale=cc_)
                    tt("dve", q_, a_, a_, ALU.mult, [("ar_ltmp", 2)], [("ar_ltmp", 3)])
                    ts("dve", q_, q_, -1.0, 1.0, ALU.mult, ALU.add, [("ar_ltmp", 3)], [("ar_ltmp", 3)])
                    act(q_, q_, AF.Sqrt, [("ar_ltmp", 3)], [("ar_ltmp", 3)])
                    tt("dve", i_, i_, X, ALU.mult, [("ar_ltmp", 1), XK], [("ar_ltmp", 1)])
                    tt("dve", q_, q_, i_, ALU.mult, [("ar_ltmp", 3), ("ar_ltmp", 1)], [("ar_ltmp", 3)])
                    H, HK = hb[c], ("ar_lh", c)
                    if mode == "p":
                        P.op("dve", lambda e: e.tensor_tensor_scan(H, a_, q_, lrs[:, l, c:c + 1], ALU.mult, ALU.add),
                             [("ar_ltmp", 2), ("ar_ltmp", 3), ("lrs", l, c)], [HK])
                        copy("dve", lrs[:, l, c:c + 1], H[:, N - 1:N], [HK], [("lrs", l, c)])
                    else:
                        tt("dve", H, a_, h0[:, c, :], ALU.mult, [("ar_ltmp", 2), ("ar_lh0", c)], [HK])
                        tt("dve", H, H, q_, ALU.add, [HK, ("ar_ltmp", 3)], [HK])
                else:
                    c = i - 4
                    g_ = tmp[4 + c % 2]
                    gk = ("ar_ltmp", 4 + c % 2)
                    act(g_, ps, AF.Gelu_apprx_tanh, [pk], [gk])
                    tt("dve", ycat[:, 12 + c, :N], hb[c], g_, ALU.mult, [("ar_lh", c), gk], [("ycat", 12 + c)])

            mt = [(OFF_LR + c * 128, 128) for c in range(8)]
            dense(Wd['w_in'][l], 16, lambda kc: (hT[:, kc, :N], ("hT", kc)), mt, cons, N, "lru", ckey=("lru", l))
            if mode == "p" and ti == NT - 1:
                store_fm(SPo['lru_h'][l], lrs[:, l, :], 4, [("lrs", l)])
                for w_ in range(3):
                    store_fm(SPo['lru_conv'][l, w_], lrh[:, l, :, w_], 4, [("lrh", l)])
            if mode == "s":
                sample_out(SSo['lru_h'][l], lambda j, w: hb[j][:w, :], 512, lambda j: ("ar_lh", j))
                sample_out(SSo['lru_conv'][l, :, 2, :], lambda j, w: xe[j][:w, 0:NS], 512, lambda j: ("ar_lxe", j))
            barrier("x")

        from_mix["lru"] = mix_lru

    stage_x = arena[:, 10240:10240 + D]

    def load_tile(src_rows, n):
        for r0 in range(0, n, 128):
            nr = min(128, n - r0)
            dma_rows("sp", stage_x[:nr, :], src_rows[r0:r0 + nr, :], D, w=["ar_stage"])
            to_fm(lambda j, w, r0=r0, nr=nr: xT[:w, j, r0:r0 + nr], stage_x[:nr, :], nr, D, ["ar_stage"],
                  lambda j: ("xT", j))

    def final_out(dst_rows, n):
        b = next_bank(4, 8)
        for c in range(16):
            s = sqb[cnt["sq"] % 2]
            sk = ("sqb", cnt["sq"] % 2)
            cnt["sq"] += 1
            act(s[:, :n], xT[:, c, :n], AF.Square, [("xT", c)], [sk])
            mm(psum[:, b, :n], ones[:], s[:, :n], c == 0, c == 15, [sk, "ones"], [PSB(b)])
        ts("dve", rstd[:, :n], psum[:, b, :n], 1.0 / D, NORM_EPS, ALU.mult, ALU.add, [PSB(b)], ["rstd"])
        act(rstd[:, :n], rstd[:, :n], AF.Sqrt, ["rstd"], ["rstd"])
        P.op("dve", lambda e: e.reciprocal(rstd[:, :n], rstd[:, :n]), ["rstd"], ["rstd"])
        for c in range(16):
            tt("dve", xT[:, c, :n], xT[:, c, :n], rstd[:, :n], ALU.mult, [("xT", c), "rstd"], [("xT", c)])
            ts("pool", xT[:, c, :n], xT[:, c, :n], fg[:, c:c + 1], None, ALU.mult, None, [("xT", c), "fg"],
               [("xT", c)])
        for r0 in range(0, n, 128):
            nr = min(128, n - r0)
            st = arena[:, 12288 + (r0 // 128 % 2) * D:12288 + (r0 // 128 % 2) * D + D]
            sk = "ar_ost%d" % (r0 // 128 % 2)
            for j0 in range(0, 16, 4):
                b = next_bank(4, 8)
                for i in range(4):
                    o_ap = psum[:nr, b, i * 128:(i + 1) * 128]
                    i_ap = xT[:, j0 + i, r0:r0 + nr]
                    P.op("pe", lambda e, o_ap=o_ap, i_ap=i_ap: e.transpose(o_ap, i_ap, ident[:, :]),
                         [("xT", j0 + i), "ident"], [PSB(b)])
                copy(ev_eng(), st[:nr, j0 * 128:(j0 + 4) * 128], psum[:nr, b, :], [PSB(b)], [(sk, j0)])
            dma_rows("sp", dst_rows[r0:r0 + nr, :], st[:nr, :], D, r=[sk], dkey=sk)

    tiles = [("p", ti) for ti in range(NT)] + ([("s", 0)] if NS > 0 else [])
    if dbg == 1:
        tiles = []
    if dbg == 2:
        tiles = tiles[:1]
    if dbg == 3:
        tiles = tiles[-1:]
    if dbg in (5, 7, 8, 9):
        tiles = tiles[:1]
    if dbg == 10:
        tiles = []
        barrier("x")
        barrier("x")
    if dbg == 11:
        tiles = []
        barrier("x")
        P.dma("sp", stage_x[:128, :], xp[0:128, :], w=["ar_stage"])
        barrier("x")
    if dbg == 13:
        tiles = []
        barrier("x")
        P.dma("sp", xT[:, 0:4, :].rearrange("p a b -> p (a b)"), xp[0:128, :], w=["xT"])
        barrier("x")
    if dbg == 14:
        tiles = []
        barrier("x")
        P.dma("sp", stage_x[:16, :], xs_in[0:16, :], w=["ar_stage"])
        barrier("x")
    if dbg == 15:
        tiles = []
        barrier("x")
        P.dma("sp", hT[:, 0:8, :].rearrange("p a b -> p (a b)").bitcast(F32), xp[0:128, :], w=["hT"])
        barrier("x")
    if dbg == 16:
        tiles = []
        barrier("x")
        P.dma("sp", A_(10240, 2048), xp[0:128, :], w=["ar_stage"])
        barrier("x")
    if dbg == 18:
        tiles = []
        barrier("x")
        hst = hT[:, 0:8, :].rearrange("p a b -> p (a b)").bitcast(F32)
        P.dma("sp", hst, xp[0:128, :], w=["hT"])
        to_fm(lambda j, w: xT[:w, j, 0:128], hst, 128, D, ["hT"], lambda j: ("xT", j))
        barrier("x")
    if dbg == 12:
        tiles = []
        P.dma("sp", stage_x[:128, :], xp[0:128, :], w=["ar_stage"])
        to_fm(lambda j, w: xT[:w, j, 0:128], stage_x[:128, :], 128, D, ["ar_stage"], lambda j: ("xT", j))
    if dbg == 6:
        tiles = tiles[-1:]
    for mode, ti in tiles:
        N = TT if mode == "p" else NS
        barrier("arena_all")
        if dbg == 8:
            pass
        elif mode == "p":
            load_tile(xp[ti * TT:(ti + 1) * TT, :], TT)
        else:
            load_tile(xs_in, NS)
        for l in range(DEPTH if dbg < 4 else 0):
            layer(l, N, mode, ti)
        barrier("arena_all")
        if dbg == 7:
            continue
        if mode == "p":
            final_out(yp[ti * TT:(ti + 1) * TT, :], TT)
        else:
            final_out(ys, NS)
    nc_out = P.finish()
    return nc_out, P


_CACHE = {}


def kernel(**inputs):
    ncores = 8
    NS = NSAMP // ncores
    if "nc" not in _CACHE:
        _CACHE["nc"] = build()[0]
    nc = _CACHE["nc"]
    f32 = lambda a: np.ascontiguousarray(np.asarray(a, dtype=np.float32))
    in_maps = []
    for c in range(ncores):
        sq = c % NSEQ_P
        m = {"xp": f32(inputs["x_prompt"][sq]),
             "xs": f32(inputs["x_sample"][c * NS:(c + 1) * NS, 0, :]),
             "cc": f32(np.concatenate([inputs["c_prompt"][sq:sq + 1], inputs["c_sample"][c * NS:(c + 1) * NS]], axis=0))}
        for n in W_NAMES:
            m[n] = f32(inputs[n])
        for n in STATE_NAMES:
            m["si_" + n] = f32(inputs["state_" + n][:, c * NS:(c + 1) * NS])
        in_maps.append(m)
    res = run_bass_kernel_spmd(nc, in_maps, core_ids=list(range(ncores)))
    R = res.results
    y_prompt = np.stack([R[s]["yp"] for s in range(NSEQ_P)], axis=0)
    y_sample = np.concatenate([R[c]["ys"] for c in range(ncores)], axis=0)[:, None, :]
    outs = [y_prompt, y_sample]
    for n in STATE_NAMES:
        outs.append(np.stack([R[s]["sp_" + n] for s in range(NSEQ_P)], axis=1))
    for n in STATE_NAMES:
        outs.append(np.concatenate([R[c]["ss_" + n] for c in range(ncores)], axis=1))
    return tuple(np.ascontiguousarray(o, dtype=np.float32) for o in outs)
```

```python
import numpy as np
from contextlib import ExitStack
import concourse.bass as bass
import concourse.mybir as mybir
from concourse.bass_utils import run_bass_kernel_spmd

F32 = mybir.dt.float32
BF16 = mybir.dt.bfloat16
AF = mybir.ActivationFunctionType
ALU = mybir.AluOpType
AX = mybir.AxisListType

EPOCH = 4000
DMA_EPOCH = 1000


class Op:
    __slots__ = ("eng", "fn", "deps", "is_dma", "signal", "sem", "val", "idx", "dkey")

    def __init__(self, eng, fn, is_dma=False, dkey=None):
        self.eng = eng
        self.fn = fn
        self.deps = []
        self.is_dma = is_dma
        self.signal = is_dma
        self.sem = None
        self.val = 0
        self.dkey = dkey


class Prog:
    ENGS = ("pe", "act", "dve", "pool", "sp")

    def __init__(self):
        self.nc = bass.Bass("TRN2", target_bir_lowering=False)
        self.st = ExitStack()
        self.ops = []
        self.state = {}
        self.last_eng = {}
        self.dmas_since = []

    def full_barrier(self):
        c = Op("pool", lambda e: e.memset(self._bar[:, 0:1], 0.0))
        c.deps = [o for o in self.last_eng.values()] + list(self.dmas_since)
        c.idx = len(self.ops)
        self.ops.append(c)
        self.dmas_since = []
        self.last_eng = {"pool": c}
        for en in ("pe", "act", "dve"):
            o = Op(en, None)
            o.deps = [c]
            o.idx = len(self.ops)
            self.ops.append(o)

    def sb(self, name, shape, dt=F32):
        return self.st.enter_context(self.nc.sbuf_tensor(name, list(shape), dt))

    def ps(self, name, shape, dt=F32):
        return self.st.enter_context(self.nc.psum_tensor(name, list(shape), dt))

    def dram(self, name, shape, dt=F32, kind="Internal"):
        return self.nc.dram_tensor(name, list(shape), dt, kind=kind)

    @staticmethod
    def _norm(k):
        if isinstance(k, tuple):
            return k[0], tuple(k[1:])
        return k, ()

    def _entries(self, name, sub):
        d = self.state.setdefault(name, {})
        out = []
        n = len(sub)
        for s2, e in d.items():
            m = min(n, len(s2))
            if s2[:m] == sub[:m]:
                out.append(e)
        return out

    def _add(self, op, reads, writes):
        deps = set()
        for k in reads:
            name, sub = self._norm(k)
            for e in self._entries(name, sub):
                if e[0] is not None:
                    deps.add(e[0])
                if name == "ps":
                    for r in e[1]:
                        if r.eng != op.eng:
                            deps.add(r)
        for k in writes:
            name, sub = self._norm(k)
            for e in self._entries(name, sub):
                if e[0] is not None:
                    deps.add(e[0])
                for r in e[1]:
                    deps.add(r)
        for k in reads:
            name, sub = self._norm(k)
            d = self.state.setdefault(name, {})
            e = d.setdefault(sub, [None, []])
            if not op.is_dma:
                e[1] = [x for x in e[1] if x.is_dma or x.eng != op.eng]
            e[1].append(op)
        for k in writes:
            name, sub = self._norm(k)
            d = self.state.setdefault(name, {})
            n = len(sub)
            for s2 in [s2 for s2 in d if len(s2) >= n and s2[:n] == sub]:
                del d[s2]
            d[sub] = [op, []]
        deps.discard(op)
        op.deps = list(deps)
        op.idx = len(self.ops)
        self.ops.append(op)
        if op.is_dma:
            self.dmas_since.append(op)
        else:
            self.last_eng[op.eng] = op
        return op

    def op(self, eng, fn, r=(), w=()):
        return self._add(Op(eng, fn), r, w)

    def dma(self, q, out, in_, r=(), w=(), dkey=None, prefetch=False, **kw):
        if q == "sp" and not prefetch:
            q = "act"
        if dkey is None:
            dkey = (w[0] if len(w) else r[0])
        dkey = ("dma",) + (dkey if isinstance(dkey, tuple) else (dkey,))
        o = Op(q, lambda e: e.dma_start(out=out, in_=in_, **kw), is_dma=True, dkey=dkey)
        return self._add(o, r, w)

    def finish(self):
        nc = self.nc
        ops = self.ops
        for o in ops:
            for d in o.deps:
                if d.eng == o.eng and o.eng == "pe" and not d.is_dma and not o.is_dma:
                    continue
                d.signal = True
        sems = {}

        def getsem(key):
            if key not in sems:
                sems[key] = self.st.enter_context(nc.semaphore("s%d" % len(sems)))
            return sems[key]

        ecount = {e: 0 for e in self.ENGS}
        dcount = {}
        for o in ops:
            if o.is_dma:
                c = dcount.get(o.dkey, 0)
                dcount[o.dkey] = c + 1
                o.sem = getsem((o.dkey, c // DMA_EPOCH))
                o.val = 16 * (c % DMA_EPOCH + 1)
            elif o.signal:
                c = ecount[o.eng]
                ecount[o.eng] = c + 1
                o.sem = getsem((o.eng, c // EPOCH))
                o.val = c % EPOCH + 1
        last = {}
        for o in ops:
            if o.is_dma:
                last[id(o.sem)] = o
        per = {e: [] for e in self.ENGS}
        for o in ops:
            per[o.eng].append(o)
        self.n_sems = len(sems)
        self.counts = {e: len(per[e]) for e in self.ENGS}
        known = {e: {} for e in self.ENGS}
        blk = self.st.enter_context(nc.Block())

        def emit(engname):
            def body(eng):
                kn = known[engname]
                for o in per[engname]:
                    need = {}
                    for d in o.deps:
                        if d.eng == engname and engname == "pe" and not d.is_dma:
                            continue
                        k = id(d.sem)
                        if kn.get(k, 0) >= d.val:
                            continue
                        if k not in need or need[k][1] < d.val:
                            need[k] = (d.sem, d.val)
                    for k, (s, v) in need.items():
                        eng.wait_ge(s, v)
                        kn[k] = v
                    if o.fn is None:
                        continue
                    ins = o.fn(eng)
                    if o.signal:
                        ins.then_inc(o.sem, 16 if o.is_dma else 1)
                if engname == "sp":
                    for o in last.values():
                        eng.wait_ge(o.sem, o.val)
            return body

        blk.tensor(emit("pe"))
        blk.scalar(emit("act"))
        blk.vector(emit("dve"))
        blk.gpsimd(emit("pool"))
        blk.sync(emit("sp"))
        self.st.close()
        return nc


D = 2048
DEPTH = 2
NSEQ_P = 4
LSEQ = 2048
NSAMP = 128
RW_D, RW_H, RW_HD = 512, 8, 64
RW_PROJ = 1984
S5_D, S5_G, S5_N, S5_CH = 512, 32, 64, 16
GD_D, GD_H, GD_HD = 512, 4, 128
GD_PROJ = 2056
LR_D = 512
N_IN = 5576
OFF_S5 = 1984
OFF_GD = 2496
OFF_LR = 4552
D_FF = 5632
NORM_EPS = 1e-6

W_NAMES = ['ada_w', 'ada_b', 'norm1_g', 'norm2_g', 'final_g', 'w_in', 'w_out',
           'rwkv_mu', 'rwkv_w0', 'rwkv_w_up', 'rwkv_a0', 'rwkv_a_up', 'rwkv_g_up', 'rwkv_k_k', 'rwkv_k_a',
           'rwkv_r_k', 'rwkv_ln_w', 'rwkv_ln_b',
           's5_lambda_re', 's5_lambda_im', 's5_log_dt', 's5_b_re', 's5_b_im', 's5_c_re', 's5_c_im', 's5_d',
           's5_glu_w', 's5_glu_b',
           'gdn_conv_w', 'gdn_a_log', 'gdn_dt_bias', 'gdn_norm_w',
           'lru_conv_w', 'lru_conv_b', 'lru_wr', 'lru_br', 'lru_wi', 'lru_bi', 'lru_lambda',
           'ffn_w_up', 'ffn_conv_w', 'ffn_conv_b', 'ffn_w_down']
W_SHAPES = {
    'ada_w': (2, 2048, 12288), 'ada_b': (2, 12288), 'norm1_g': (2, 2048), 'norm2_g': (2, 2048), 'final_g': (2048,),
    'w_in': (2, 2048, 5576), 'w_out': (2, 2048, 2048), 'rwkv_mu': (2, 1984), 'rwkv_w0': (2, 512),
    'rwkv_w_up': (2, 96, 512), 'rwkv_a0': (2, 512), 'rwkv_a_up': (2, 96, 512), 'rwkv_g_up': (2, 256, 512),
    'rwkv_k_k': (2, 512), 'rwkv_k_a': (2, 512), 'rwkv_r_k': (2, 8, 64), 'rwkv_ln_w': (2, 512), 'rwkv_ln_b': (2, 512),
    's5_lambda_re': (2, 32, 64), 's5_lambda_im': (2, 32, 64), 's5_log_dt': (2, 32), 's5_b_re': (2, 32, 64, 16),
    's5_b_im': (2, 32, 64, 16), 's5_c_re': (2, 32, 16, 64), 's5_c_im': (2, 32, 16, 64), 's5_d': (2, 512),
    's5_glu_w': (2, 512, 512), 's5_glu_b': (2, 512), 'gdn_conv_w': (2, 4, 1536), 'gdn_a_log': (2, 4),
    'gdn_dt_bias': (2, 4), 'gdn_norm_w': (2, 128), 'lru_conv_w': (2, 4, 512), 'lru_conv_b': (2, 512),
    'lru_wr': (2, 8, 64, 64), 'lru_br': (2, 512), 'lru_wi': (2, 8, 64, 64), 'lru_bi': (2, 512), 'lru_lambda': (2, 512),
    'ffn_w_up': (2, 2048, 11264), 'ffn_conv_w': (2, 3, 11264), 'ffn_conv_b': (2, 11264), 'ffn_w_down': (2, 5632, 2048),
}
STATE_SHAPES = {
    'rwkv_wkv': (8, 64, 64), 'rwkv_shift': (1984,), 's5_re': (32, 64), 's5_im': (32, 64), 'gdn': (4, 128, 128),
    'gdn_conv': (3, 1536), 'lru_h': (512,), 'lru_conv': (3, 512), 'ffn_conv': (2, 11264),
}
STATE_NAMES = ('rwkv_wkv', 'rwkv_shift', 's5_re', 's5_im', 'gdn', 'gdn_conv', 'lru_h', 'lru_conv', 'ffn_conv')


def build(LP=2048, NS=16, TT=512, mixers=("lru", "s5", "gdn", "rwkv"), dbg=0):
    P = Prog()
    nc = P.nc
    NT = LP // TT
    xp = P.dram("xp", [LP, D], F32, kind="ExternalInput").ap()
    xs_in = P.dram("xs", [NS, D], F32, kind="ExternalInput").ap()
    cc_in = P.dram("cc", [1 + NS, D], F32, kind="ExternalInput").ap()
    Wd = {n: P.dram(n, list(W_SHAPES[n]), F32, kind="ExternalInput").ap() for n in W_NAMES}
    Sin = {n: P.dram("si_" + n, [DEPTH, NS] + list(STATE_SHAPES[n]), F32, kind="ExternalInput").ap() for n in STATE_NAMES}
    yp = P.dram("yp", [LP, D], F32, kind="ExternalOutput").ap()
    ys = P.dram("ys", [NS, D], F32, kind="ExternalOutput").ap()
    SPo = {n: P.dram("sp_" + n, [DEPTH] + list(STATE_SHAPES[n]), F32, kind="ExternalOutput").ap() for n in STATE_NAMES}
    SSo = {n: P.dram("ss_" + n, [DEPTH, NS] + list(STATE_SHAPES[n]), F32, kind="ExternalOutput").ap() for n in STATE_NAMES}

    xT = P.sb("xT", [128, 16, TT], F32)
    hT = P.sb("hT", [128, 16, TT], BF16)
    ycat = P.sb("ycat", [128, 16, TT], BF16)
    NWB = 2
    WBE = 6144
    wbuf = [P.sb("wbuf%d" % i, [128, WBE], BF16) for i in range(NWB)]
    ARW = 19456
    arena = P.sb("arena", [128, ARW], F32)
    ident = P.sb("ident", [128, 128], F32)
    ones = P.sb("ones", [128, 128], F32)
    adab = P.sb("adab", [128, DEPTH, 96], F32)
    ng = P.sb("ng", [128, DEPTH, 2, 16], F32)
    fg = P.sb("fg", [128, 16], F32)
    coef = P.sb("coef", [128, DEPTH, 6, 16, 1 + NS], F32)
    rstd = P.sb("rstd", [128, TT], F32)
    sqb = [P.sb("sqb%d" % i, [128, TT], F32) for i in range(2)]
    ffh = P.sb("ffh", [128, DEPTH, 88, 2], F32)
    ffw = P.sb("ffw", [128, DEPTH, 88, 4], F32)
    psum = P.ps("psum", [128, 8, 512], F32)

    def PSB(b):
        return ("ps", b)

    ABASE = 0

    def A_(off, n):
        assert ABASE + off + n <= ARW, (off, n)
        return arena[:, ABASE + off:ABASE + off + n]

    cnt = {"ps": 0, "wb": 0, "sq": 0, "ev": 0, "vs": 0}

    def next_bank(lo=0, hi=4):
        b = lo + cnt["ps"] % (hi - lo)
        cnt["ps"] += 1
        return b

    def ev_eng():
        cnt["ev"] += 1
        return "dve" if cnt["ev"] % 2 else "act"

    def copy(eng, out, in_, r, w):
        if eng == "act":
            P.op("act", lambda e: e.copy(out, in_), r, w)
        else:
            P.op(eng, lambda e: e.tensor_copy(out, in_), r, w)

    def tt(eng, out, a, b, op, r, w):
        eng = "dve" if eng == "pool" else eng
        P.op(eng, lambda e: e.tensor_tensor(out, a, b, op), r, w)

    def ts(eng, out, a, s1, s2, op0, op1, r, w):
        eng = "dve" if eng == "pool" else eng
        if op1 is None:
            P.op(eng, lambda e: e.tensor_scalar(out, a, s1, None, op0), r, w)
        else:
            P.op(eng, lambda e: e.tensor_scalar(out, a, s1, s2, op0, op1), r, w)

    def stt(eng, out, in0, sc, in1, op0, op1, r, w):
        P.op(eng, lambda e: e.scalar_tensor_tensor(out, in0, sc, in1, op0, op1), r, w)

    def act(out, in_, func, r, w, bias=None, scale=None):
        kw = {}
        if bias is not None:
            kw["bias"] = bias
        if scale is not None:
            kw["scale"] = scale
        P.op("act", lambda e: e.activation(out, in_, func, **kw), r, w)

    def mm(out, lhsT, rhs, start, stop, r, w):
        P.op("pe", lambda e: e.matmul(out, lhsT, rhs, start=start, stop=stop), r, w)

    def transp(out, in_, n, r, w):
        P.op("pe", lambda e: e.transpose(out, in_, ident[:n, :n]), list(r) + ["ident"], w)

    P._bar = P.sb("barbuf", [128, 8], F32)

    def dma_rows(q, out, in_, ncols, r=(), w=(), dkey=None):
        base = dkey if dkey is not None else (w[0] if len(w) else r[0])
        base = base if isinstance(base, tuple) else (base,)
        for ci, c0 in enumerate(range(0, ncols, 512)):
            c1 = min(ncols, c0 + 512)
            P.dma(q, out[:, c0:c1], in_[:, c0:c1], r=r, w=w, dkey=base + ("c%d" % (ci % 4),))

    def barrier(name):
        P.full_barrier()

    P.op("pool", lambda e: e.memset(ident[:], 1.0), w=["ident"])
    P.op("pool", lambda e: e.affine_select(ident[:], ident[:], pattern=[[-1, 128]], compare_op=ALU.is_equal,
                                           fill=0.0, base=0, channel_multiplier=1), r=["ident"], w=["ident"])
    P.op("pool", lambda e: e.memset(ones[:], 1.0), w=["ones"])
    P.op("dve", lambda e: e.memset(ffh[:], 0.0), w=["ffh"])

    vst = [P.sb("vst%d" % i, [128, 128], F32) for i in range(2)]

    def load_fm(dst, src1d, nchunk, key, q="sp"):
        i = cnt["vs"] % 2
        cnt["vs"] += 1
        P.dma(q, vst[i][:nchunk, :], src1d.rearrange("(c p) -> c p", p=128), w=[("vst", i)])
        b = next_bank(4, 8)
        transp(psum[:, b, :nchunk], vst[i][:nchunk, :], nchunk, [("vst", i)], [PSB(b)])
        copy("dve", dst, psum[:, b, :nchunk], [PSB(b)], [key])

    def store_fm(dst1d, src, nchunk, rkeys):
        i = cnt["vs"] % 2
        cnt["vs"] += 1
        b = next_bank(4, 8)
        P.op("pe", lambda e: e.transpose(psum[:nchunk, b, :128], src, ident[:, :]), list(rkeys) + ["ident"], [PSB(b)])
        copy("dve", vst[i][:nchunk, :], psum[:nchunk, b, :128], [PSB(b)], [("vst", i)])
        P.dma("sp", dst1d.rearrange("(c p) -> c p", p=128), vst[i][:nchunk, :], r=[("vst", i)])

    for l in range(DEPTH):
        load_fm(adab[:, l, :], Wd['ada_b'][l], 96, ("adab", l))
        load_fm(ng[:, l, 0, :], Wd['norm1_g'][l], 16, ("ng", l, 0))
        load_fm(ng[:, l, 1, :], Wd['norm2_g'][l], 16, ("ng", l, 1))
        for j in range(3):
            load_fm(ffw[:, l, :, j], Wd['ffn_conv_w'][l, j], 88, ("ffw", l, j))
        load_fm(ffw[:, l, :, 3], Wd['ffn_conv_b'][l], 88, ("ffw", l, 3))
    load_fm(fg[:], Wd['final_g'], 16, "fg")

    def to_fm(dst_fn, src_sb, n, F, rkeys, wkey_fn, bank_lo=4, bank_hi=8):
        nblk = (F + 127) // 128
        per = max(1, 512 // n)
        j = 0
        while j < nblk:
            b = next_bank(bank_lo, bank_hi)
            g = min(per, nblk - j)
            for i in range(g):
                wdt = min(128, F - (j + i) * 128)
                transp(psum[:wdt, b, i * n:(i + 1) * n], src_sb[:, (j + i) * 128:(j + i) * 128 + wdt], n,
                       rkeys, [PSB(b)])
            for i in range(g):
                wdt = min(128, F - (j + i) * 128)
                copy(ev_eng(), dst_fn(j + i, wdt), psum[:wdt, b, i * n:(i + 1) * n], [PSB(b)], [wkey_fn(j + i)])
            j += g

    WSC_TOTAL = 800000
    wsc = P.dram("wsc", [128, WSC_TOTAL], BF16).ap()
    wcache = {}
    wsc_off = [0]

    def dense(W2d, KC, rhs_fn, mtiles, consume, N, tag, ckey=None):
        maxc = WBE // KC
        blocks = []
        cur = []
        for i, (c0, ncol) in enumerate(mtiles):
            if cur and (c0 + ncol - mtiles[cur[0]][0] > maxc or mtiles[cur[-1]][0] + mtiles[cur[-1]][1] != c0):
                blocks.append(cur)
                cur = []
            cur.append(i)
        if cur:
            blocks.append(cur)
        Wv = W2d.rearrange("(kc p) m -> p kc m", p=128)
        for bi_, blk in enumerate(blocks):
            c0 = mtiles[blk[0]][0]
            c1 = mtiles[blk[-1]][0] + mtiles[blk[-1]][1]
            bw = c1 - c0
            wi = cnt["wb"] % NWB
            cnt["wb"] += 1
            wv = wbuf[wi][:, 0:KC * bw].rearrange("p (kc m) -> p kc m", kc=KC)
            ck = (ckey + (bi_,)) if ckey is not None else None
            if ck is not None and ck in wcache:
                P.dma("sp", wbuf[wi][:, 0:KC * bw], wcache[ck], r=[("wsc",) + ck], w=[("wbuf", wi)], prefetch=True)
            else:
                P.dma("pool", wv, Wv[:, :, c0:c1], w=[("wbuf", wi)])
                if ck is not None:
                    off = wsc_off[0]
                    wsc_off[0] += KC * bw
                    assert wsc_off[0] <= WSC_TOTAL
                    wcache[ck] = wsc[:, off:off + KC * bw]
                    P.dma("act", wcache[ck], wbuf[wi][:, 0:KC * bw], r=[("wbuf", wi)], w=[("wsc",) + ck],
                          dkey=("wsc_o", wi))
            for i in blk:
                m0, ncol = mtiles[i]
                b = next_bank(0, 4)
                for kc in range(KC):
                    rap, rkey = rhs_fn(kc)
                    mm(psum[:ncol, b, 0:N], wv[:, kc, m0 - c0:m0 - c0 + ncol], rap, kc == 0, kc == KC - 1,
                       [("wbuf", wi), rkey], [PSB(b)])
                consume(i, psum[:ncol, b, 0:N], PSB(b))

    NSQ = 1 + NS
    ccs = A_(0, D)[:NSQ, :]
    scT = A_(D, 16 * NSQ).rearrange("p (c n) -> p c n", c=16)
    scTb = hT[:, :, 0:NSQ]
    modT = A_(4096, DEPTH * 96 * NSQ).rearrange("p (l c n) -> p l c n", l=DEPTH, c=96)
    dma_rows("sp", ccs, cc_in, D, w=["ar_cc"])
    to_fm(lambda j, w: scT[:w, j, :], ccs, NSQ, D, ["ar_cc"], lambda j: ("ar_scT", j))
    act(scTb, scT, AF.Silu, ["ar_scT"], ["hT"])
    for l in range(DEPTH):
        def cons_mod(i, ps, pk, l=l):
            ts("dve", modT[:, l, i, :], ps, adab[:, l, i:i + 1], None, ALU.add, None,
               [pk, ("adab", l)], [("modT", l, i)])
        dense(Wd['ada_w'][l], 16, lambda kc: (scTb[:, kc, :], "hT"), [(i * 128, 128) for i in range(96)],
              cons_mod, NSQ, "ada")
        for which, (sh_i, sc_i, g_i) in enumerate([(0, 1, 2), (3, 4, 5)]):
            base = which * 3
            ts("dve", coef[:, l, base + 0], modT[:, l, sc_i * 16:(sc_i + 1) * 16, :], 1.0, None, ALU.add, None,
               [("modT", l)], [("coef", l, base + 0)])
            tt("dve", coef[:, l, base + 0], coef[:, l, base + 0],
               ng[:, l, which, :].unsqueeze(2).to_broadcast([128, 16, NSQ]), ALU.mult,
               [("coef", l, base + 0), ("ng", l, which)], [("coef", l, base + 0)])
            copy("dve", coef[:, l, base + 1], modT[:, l, sh_i * 16:(sh_i + 1) * 16, :], [("modT", l)],
                 [("coef", l, base + 1)])
            copy("dve", coef[:, l, base + 2], modT[:, l, g_i * 16:(g_i + 1) * 16, :], [("modT", l)],
                 [("coef", l, base + 2)])
    barrier("arena_all")

    def cf(l, which, c, mode, N):
        if mode == "p":
            return coef[:, l, which, c, 0:1].to_broadcast([128, N])
        return coef[:, l, which, c, 1:1 + NS]

    def rmsnorm_mod(l, which, N, mode):
        b = next_bank(4, 8)
        for c in range(16):
            s = sqb[cnt["sq"] % 2]
            sk = ("sqb", cnt["sq"] % 2)
            cnt["sq"] += 1
            act(s[:, :N], xT[:, c, :N], AF.Square, [("xT", c)], [sk])
            mm(psum[:, b, :N], ones[:], s[:, :N], c == 0, c == 15, [sk, "ones"], [PSB(b)])
        ts("dve", rstd[:, :N], psum[:, b, :N], 1.0 / D, NORM_EPS, ALU.mult, ALU.add, [PSB(b)], ["rstd"])
        act(rstd[:, :N], rstd[:, :N], AF.Sqrt, ["rstd"], ["rstd"])
        P.op("dve", lambda e: e.reciprocal(rstd[:, :N], rstd[:, :N]), ["rstd"], ["rstd"])
        for c in range(16):
            s = sqb[cnt["sq"] % 2]
            sk = ("sqb", cnt["sq"] % 2)
            cnt["sq"] += 1
            if which is None:
                raise RuntimeError
            tt("dve", s[:, :N], xT[:, c, :N], rstd[:, :N], ALU.mult, [("xT", c), "rstd"], [sk])
            tt("pool", s[:, :N], s[:, :N], cf(l, which * 3 + 0, c, mode, N), ALU.mult, [sk, ("coef", l)], [sk])
            tt("dve", hT[:, c, :N], s[:, :N], cf(l, which * 3 + 1, c, mode, N), ALU.add, [sk, ("coef", l)],
               [("hT", c)])

    def resid_add(l, which, c, ps, pk, N, mode):
        s = sqb[cnt["sq"] % 2]
        sk = ("sqb", cnt["sq"] % 2)
        cnt["sq"] += 1
        tt("dve", s[:, :N], ps, cf(l, which * 3 + 2, c, mode, N), ALU.mult, [pk, ("coef", l)], [sk])
        tt("pool", xT[:, c, :N], xT[:, c, :N], s[:, :N], ALU.add, [sk, ("xT", c)], [("xT", c)])

    def zero_ycat(N):
        P.op("pool", lambda e: e.memset(ycat[:, :, :N], 0.0), w=["ycat"])

    from_mix = {}

    def layer(l, N, mode, ti):
        last_tile = (ti == NT - 1)
        rmsnorm_mod(l, 0, N, mode)
        barrier("arena_all")
        zero_ycat(N)
        for name in mixers:
            from_mix[name](l, N, mode, ti)
        dense(Wd['w_out'][l], 16, lambda kc: (ycat[:, kc, :N], ("ycat", kc)), [(i * 128, 128) for i in range(16)],
              lambda i, ps, pk: resid_add(l, 0, i, ps, pk, N, mode), N, "wout", ckey=("wout", l))
        rmsnorm_mod(l, 1, N, mode)
        barrier("arena_all")
        ffn(l, N, mode, ti)

    def ffn(l, N, mode, ti):
        NN = TT if mode == "p" else NS
        actT = A_(0, 44 * NN // 2).bitcast(BF16).rearrange("p (j n) -> p j n", j=44)
        o0 = 44 * NN // 2
        upx = [A_(o0 + i * (NN + 2), NN + 2) for i in range(4)]
        o1 = o0 + 4 * (NN + 2)
        cv = [A_(o1 + i * NN, NN) for i in range(4)]
        o2 = o1 + 4 * NN
        if mode == "s":
            hist = A_(o2, 88 * 2 * NS).rearrange("p (j w n) -> p j w n", j=88, w=2)
            stage = A_(o2 + 88 * 2 * NS, 11264)
            newrow = A_(o2 + 88 * 2 * NS + 11264, 88 * NS).rearrange("p (j n) -> p j n", j=88)
            for wi_ in range(2):
                dma_rows("sp", stage[:NS, :], Sin['ffn_conv'][l, :, wi_, :], 11264, w=["ar_stage"])
                to_fm(lambda j, w, wi_=wi_: hist[:w, j, wi_, :], stage[:NS, :], NS, 11264, ["ar_stage"],
                      lambda j, wi_=wi_: ("ar_hist", j, wi_))
            dma_rows("sp", SSo['ffn_conv'][l, :, 0, :], Sin['ffn_conv'][l, :, 1, :], 11264, r=[], w=["o_ffnc0"])
        gate_done = {}

        def cons_up(i, ps, pk):
            u = upx[i % 4]
            uk = ("ar_upx", i % 4)
            c = cv[i % 4]
            ck = ("ar_cv", i % 4)
            w0, w1, w2, bb = (ffw[:, l, i, k:k + 1] for k in range(4))
            if mode == "p":
                copy("act", u[:, 2:2 + N], ps, [pk], [uk])
                copy("pool", u[:, 0:2], ffh[:, l, i, :], [("ffh", l, i)], [uk])
                copy("pool", ffh[:, l, i, :], u[:, N:N + 2], [uk], [("ffh", l, i)])
                ts("dve", c[:, :N], u[:, 2:2 + N], w2, bb, ALU.mult, ALU.add, [uk, ("ffw", l)], [ck])
                stt("dve", c[:, :N], u[:, 1:1 + N], w1, c[:, :N], ALU.mult, ALU.add, [uk, ck, ("ffw", l)], [ck])
                stt("dve", c[:, :N], u[:, 0:N], w0, c[:, :N], ALU.mult, ALU.add, [uk, ck, ("ffw", l)], [ck])
            else:
                copy("act", u[:, 0:N], ps, [pk], [uk])
                copy("pool", newrow[:, i, :], u[:, 0:N], [uk], [("ar_newrow", i)])
                ts("dve", c[:, :N], u[:, 0:N], w2, bb, ALU.mult, ALU.add, [uk, ("ffw", l)], [ck])
                stt("dve", c[:, :N], hist[:, i, 1, :], w1, c[:, :N], ALU.mult, ALU.add,
                    [("ar_hist", i, 1), ck, ("ffw", l)], [ck])
                stt("dve", c[:, :N], hist[:, i, 0, :], w0, c[:, :N], ALU.mult, ALU.add,
                    [("ar_hist", i, 0), ck, ("ffw", l)], [ck])
            if i < 44:
                act(actT[:, i, :N], c[:, :N], AF.Silu, [ck], [("ar_actT", i)])
            else:
                tt("dve", actT[:, i - 44, :N], actT[:, i - 44, :N], c[:, :N], ALU.mult, [("ar_actT", i - 44), ck],
                   [("ar_actT", i - 44)])

        order = []
        for j0 in range(0, 44, 4):
            order += [(j, j * 128, 128) for j in range(j0, j0 + 4)]
            order += [(44 + j, (44 + j) * 128, 128) for j in range(j0, j0 + 4)]
        mt = [(c0, n) for (_, c0, n) in order]
        dense(Wd['ffn_w_up'][l], 16, lambda kc: (hT[:, kc, :N], ("hT", kc)), mt,
              lambda i, ps, pk: cons_up(order[i][0], ps, pk), N, "ffup", ckey=("ffup", l))
        if mode == "p" and ti == NT - 1:
            for w_ in range(2):
                store_fm(SPo['ffn_conv'][l, w_], ffh[:, l, :, w_], 88, [("ffh", l)])
        if mode == "s":
            to_tm(SSo['ffn_conv'][l, :, 1, :], lambda j, w: newrow[:w, j, :], NS, 11264,
                  lambda j: ("ar_newrow", j), stage, "ar_stage")
        dense(Wd['ffn_w_down'][l], 44, lambda kc: (actT[:, kc, :N], ("ar_actT", kc)),
              [(i * 128, 128) for i in range(16)],
              lambda i, ps, pk: resid_add(l, 1, i, ps, pk, N, mode), N, "ffdn", ckey=("ffdn", l))

    def to_tm(dst_hbm, src_fn, n, F, rkey_fn, stage, stage_key):
        nblk = (F + 127) // 128
        j = 0
        while j < nblk:
            b = next_bank(4, 8)
            g = min(4, nblk - j)
            tot = 0
            for i in range(g):
                wdt = min(128, F - (j + i) * 128)
                o_ap = psum[:n, b, i * 128:i * 128 + wdt]
                i_ap = src_fn(j + i, wdt)
                id_ap = ident[:wdt, :wdt]
                P.op("pe", lambda e, o_ap=o_ap, i_ap=i_ap, id_ap=id_ap: e.transpose(o_ap, i_ap, id_ap),
                     [rkey_fn(j + i), "ident"], [PSB(b)])
                tot += wdt
            copy(ev_eng(), stage[:n, j * 128:j * 128 + tot], psum[:n, b, 0:tot], [PSB(b)], [(stage_key, j)])
            j += g
        dma_rows("sp", dst_hbm, stage[:n, :F], F, r=[stage_key], dkey=stage_key)

    def mix_stub(l, N, mode, ti):
        pass

    for name in mixers:
        from_mix[name] = mix_stub

    def sample_in(dst3, src2d, F, key):
        stg = arena[:, ARW - 2048:ARW]
        for f0 in range(0, F, 2048):
            fw = min(2048, F - f0)
            dma_rows("sp", stg[:NS, :fw], src2d[:, f0:f0 + fw], fw, w=["ar_sin"])
            to_fm(lambda j, w, f0=f0: dst3[:w, f0 // 128 + j, :], stg[:NS, :fw], NS, fw, ["ar_sin"],
                  lambda j, f0=f0: (key, f0 // 128 + j))

    def sample_out(dst2d, src_fn, F, rkey_fn):
        stg = arena[:, ARW - 4096:ARW - 2048]
        for f0 in range(0, F, 2048):
            fw = min(2048, F - f0)
            to_tm(dst2d[:, f0:f0 + fw], lambda j, w, f0=f0: src_fn(f0 // 128 + j, w), NS, fw,
                  lambda j, f0=f0: rkey_fn(f0 // 128 + j), stg, "ar_sout")

    CH = 64
    if "gdn" in mixers or "rwkv" in mixers:
        msk = P.sb("msk", [64, 3, 4, 64], F32)
        P.op("pool", lambda e: e.memset(msk[:], 1.0), w=["msk"])
        for k_, (pat, cm_, base_) in enumerate([([[0, 4], [-1, 64]], 1, -1), ([[0, 4], [1, 64]], -1, -1),
                                                ([[0, 4], [1, 64]], -1, 0)]):
            P.op("pool", lambda e, k_=k_, pat=pat, cm_=cm_, base_=base_: e.affine_select(
                msk[:, k_], msk[:, k_], pattern=pat, compare_op=ALU.is_ge, fill=0.0, base=base_,
                channel_multiplier=cm_), r=["msk"], w=["msk"])

    def neumann(Q, QT, Q2, QT2, PT, H, C, kq):
        idb = ident[:C, :C].unsqueeze(1).to_broadcast([C, H, C])
        tt("dve", PT, Q, idb, ALU.add, [(kq, "Q"), "ident"], [(kq, "P")])
        L = 0
        while (1 << L) < C:
            L += 1
        cq, cqt, nq, nqt = Q, QT, Q2, QT2
        ckq, ckqt, nkq, nkqt = (kq, "Q"), (kq, "QT"), (kq, "Q2"), (kq, "QT2")
        for j in range(1, L):
            last = (j == L - 1)
            b = next_bank(4, 8)
            for h in range(H):
                mm(psum[:C, b, h * C:(h + 1) * C], cq[:, h, :], cqt[:, h, :], True, True, [ckq, ckqt], [PSB(b)])
            copy("act", nqt, psum[:C, b, 0:H * C].rearrange("p (h c) -> p h c", h=H), [PSB(b)], [nkqt])
            if not last:
                b2 = next_bank(4, 8)
                for h in range(H):
                    mm(psum[:C, b2, h * C:(h + 1) * C], cqt[:, h, :], cq[:, h, :], True, True, [ckq, ckqt], [PSB(b2)])
                copy("act", nq, psum[:C, b2, 0:H * C].rearrange("p (h c) -> p h c", h=H), [PSB(b2)], [nkq])
            b3 = next_bank(4, 8)
            for h in range(H):
                mm(psum[:C, b3, h * C:(h + 1) * C], nqt[:, h, :], PT[:, h, :], True, True, [nkqt, (kq, "P")], [PSB(b3)])
            tt("dve", PT, PT, psum[:C, b3, 0:H * C].rearrange("p (h c) -> p h c", h=H), ALU.add,
               [(kq, "P"), PSB(b3)], [(kq, "P")])
            cq, cqt, nq, nqt = nq, nqt, cq, cqt
            ckq, ckqt, nkq, nkqt = nkq, nkqt, ckq, ckqt

    if "gdn" in mixers:
        gst = P.sb("gst", [128, DEPTH, 4, 128], F32)
        gch = P.sb("gch", [128, DEPTH, 12, 3], F32)
        gcw = P.sb("gcw", [128, DEPTH, 12, 4], F32)
        gnw = P.sb("gnw", [128, DEPTH], F32)
        gp8 = P.sb("gp8", [8, DEPTH, 4], F32)
        P.op("dve", lambda e: e.memset(gst[:], 0.0), w=["gst"])
        P.op("dve", lambda e: e.memset(gch[:], 0.0), w=["gch"])
        P.op("dve", lambda e: e.memset(gp8[:], 0.0), w=["gp8"])
        mA = A_(0, 2)[:8, :]
        P.op("pool", lambda e: e.memset(mA, 1.0), w=["ar_mA"])
        P.op("pool", lambda e: e.affine_select(mA[:, 0:1], mA[:, 0:1], pattern=[[0, 1]], compare_op=ALU.is_ge, fill=0.0,
                                               base=3, channel_multiplier=-1), r=["ar_mA"], w=["ar_mA"])
        for l in range(DEPTH):
            for k in range(4):
                load_fm(gcw[:, l, :, k], Wd['gdn_conv_w'][l, k], 12, ("gcw", l, k))
            P.dma("sp", gnw[:, l:l + 1], Wd['gdn_norm_w'][l].rearrange("(p o) -> p o", o=1), w=[("gnw", l)])
            P.dma("sp", gp8[0:4, l, 0:1], Wd['gdn_dt_bias'][l].rearrange("(p o) -> p o", o=1), w=[("gp8", l, 0)])
            P.dma("sp", gp8[0:4, l, 1:2], Wd['gdn_a_log'][l].rearrange("(p o) -> p o", o=1), w=[("gp8", l, 1)])
            act(gp8[:, l, 1:2], gp8[:, l, 1:2], AF.Exp, [("gp8", l, 1)], [("gp8", l, 1)])
            ts("dve", gp8[:, l, 1:2], gp8[:, l, 1:2], mA[:, 0:1], -1.0, ALU.mult, ALU.mult, [("gp8", l, 1), "ar_mA"],
               [("gp8", l, 1)])
            ts("dve", gp8[:, l, 2:3], mA[:, 0:1], -1.0, 1.0, ALU.mult, ALU.add, ["ar_mA"], [("gp8", l, 2)])
        barrier("x")

        def mix_gdn(l, N, mode, ti):
            H = 4
            C = CH if mode == "p" else 1
            o = 0

            def AL(n, shape=None):
                nonlocal o
                a = A_(o, n)
                o += n
                return a

            QKV = AL(12 * N).rearrange("p (i n) -> p i n", i=12)
            ZS = AL(4 * N).rearrange("p (i n) -> p i n", i=4)
            xe = [AL(N + 3) for _ in range(2)]
            GB = AL(N)
            gt1 = AL(N)
            if mode == "s":
                hist = AL(36 * NS).rearrange("p (j n) -> p j n", j=36)
                xnew = AL(12 * NS).rearrange("p (j n) -> p j n", j=12)
                Ssb = [AL(512).rearrange("p (h v) -> p h v", h=4) for _ in range(2)]
                sample_in(hist, Sin['gdn_conv'][l].rearrange("b w f -> b (w f)"), 4608, "ar_ghist")
                for w_ in range(2):
                    dma_rows("sp", SSo['gdn_conv'][l, :, w_, :], Sin['gdn_conv'][l, :, w_ + 1, :], 1536, r=[],
                             w=[("o_gdnc", w_)])

            def cons(i, ps, pk):
                if i < 12:
                    u = xe[i % 2]
                    uk = ("ar_gxe", i % 2)
                    X = QKV[:, i, :N]
                    XK = ("ar_qkv", i)
                    w0, w1, w2, w3 = (gcw[:, l, i, k:k + 1] for k in range(4))
                    if mode == "p":
                        copy("act", u[:, 3:3 + N], ps, [pk], [uk])
                        copy("dve", u[:, 0:3], gch[:, l, i, :], [("gch", l, i)], [uk])
                        copy("dve", gch[:, l, i, :], u[:, N:N + 3], [uk], [("gch", l, i)])
                        ts("dve", X, u[:, 3:3 + N], w3, None, ALU.mult, None, [uk, ("gcw", l)], [XK])
                        stt("dve", X, u[:, 2:2 + N], w2, X, ALU.mult, ALU.add, [uk, XK, ("gcw", l)], [XK])
                        stt("dve", X, u[:, 1:1 + N], w1, X, ALU.mult, ALU.add, [uk, XK, ("gcw", l)], [XK])
                        stt("dve", X, u[:, 0:N], w0, X, ALU.mult, ALU.add, [uk, XK, ("gcw", l)], [XK])
                    else:
                        copy("act", xnew[:, i, :], ps, [pk], [("ar_gxn", i)])
                        ts("dve", X, xnew[:, i, :], w3, None, ALU.mult, None, [("ar_gxn", i), ("gcw", l)], [XK])
                        for w_, wv in ((2, w2), (1, w1), (0, w0)):
                            stt("dve", X, hist[:, w_ * 12 + i, :], wv, X, ALU.mult, ALU.add,
                                [("ar_ghist", w_ * 12 + i), XK, ("gcw", l)], [XK])
                    act(X, X, AF.Silu, [XK], [XK])
                elif i < 16:
                    act(ZS[:, i - 12, :N], ps, AF.Silu, [pk], [("ar_zs", i - 12)])
                else:
                    act(gt1[:8, :N], ps, AF.Exp, [pk, ("gp8", l)], ["ar_gt1"], bias=gp8[:, l, 0:1])
                    act(gt1[:8, :N], gt1[:8, :N], AF.Ln, ["ar_gt1"], ["ar_gt1"], bias=1.0)
                    act(GB[:8, :N], ps, AF.Sigmoid, [pk], ["ar_GB"])
                    ts("dve", GB[:8, :N], GB[:8, :N], gp8[:, l, 2:3], None, ALU.mult, None, ["ar_GB", ("gp8", l)], ["ar_GB"])
                    stt("dve", GB[:8, :N], gt1[:8, :N], gp8[:, l, 1:2], GB[:8, :N], ALU.mult, ALU.add,
                        ["ar_gt1", "ar_GB", ("gp8", l)], ["ar_GB"])

            mt = [(OFF_GD + c * 128, 128) for c in range(16)] + [(OFF_GD + 2048, 8)]
            dense(Wd['w_in'][l], 16, lambda kc: (hT[:, kc, :N], ("hT", kc)), mt, cons, N, "gdn", ckey=("gdn", l))
            for i in range(8):
                sq = sqb[i % 2]
                sk = ("sqb", i % 2)
                tt("dve", sq[:, :N], QKV[:, i, :N], QKV[:, i, :N], ALU.mult, [("ar_qkv", i)], [sk])
                b = next_bank(4, 8)
                mm(psum[:, b, :N], ones[:], sq[:, :N], True, True, [sk, "ones"], [PSB(b)])
                ts("dve", sq[:, :N], psum[:, b, :N], 1e-6, None, ALU.add, None, [PSB(b)], [sk])
                act(sq[:, :N], sq[:, :N], AF.Sqrt, [sk], [sk])
                P.op("dve", lambda e, sq=sq: e.reciprocal(sq[:, :N], sq[:, :N]), [sk], [sk])
                if i < 4:
                    stt("dve", QKV[:, i, :N], QKV[:, i, :N], float(GD_HD) ** -0.5, sq[:, :N], ALU.mult, ALU.mult,
                        [("ar_qkv", i), sk], [("ar_qkv", i)])
                else:
                    tt("dve", QKV[:, i, :N], QKV[:, i, :N], sq[:, :N], ALU.mult, [("ar_qkv", i), sk], [("ar_qkv", i)])
            def galloc():
                d = {}
                d["Gt"] = AL(8)
                d["gc"] = AL(4)
                d["sm"] = AL(16)
                d["GB3"] = AL(512).rearrange("p (h a) -> p h a", h=4)
                for nm in ("Dm", "E1", "E2", "EG"):
                    d[nm] = AL(256).rearrange("p (h c) -> p h c", h=4)
                d["gtot"] = AL(4)
                for nm in ("kbg", "ktl", "vb"):
                    d[nm] = AL(512).rearrange("p (h a) -> p h a", h=4)
                for nm in ("Qm", "QTm", "Q2m", "QT2m", "PTm", "ATm", "wkT", "qdT"):
                    d[nm] = AL(256).rearrange("p (h c) -> p h c", h=4)
                for nm in ("usb", "vnew", "osb", "osq"):
                    d[nm] = AL(512).rearrange("p (h a) -> p h a", h=4)
                d["ss"] = AL(8)
                return d

            gsets = [galloc() for _ in range(2 if mode == "s" else 1)]
            nchunks = N // C
            for ci in range(nchunks):
                c0 = ci * C
                si_ = ci % len(gsets)
                T_ = gsets[si_]
                Gt, gc, sm, GB3, Dm, E1, E2, EG, gtot, kbg, ktl, vb = (T_[k] for k in (
                    "Gt", "gc", "sm", "GB3", "Dm", "E1", "E2", "EG", "gtot", "kbg", "ktl", "vb"))
                Qm, QTm, Q2m, QT2m, PTm, ATm, wkT, qdT, usb, vnew, osb, osq, ss = (T_[k] for k in (
                    "Qm", "QTm", "Q2m", "QT2m", "PTm", "ATm", "wkT", "qdT", "usb", "vnew", "osb", "osq", "ss"))
                K_ = lambda nm, si_=si_: ("ar_g_" + nm, si_)
                GN = "ar_gn%d" % si_
                if mode == "p":
                    S = gst[:, l]
                    SK = ("gst", l)
                else:
                    S = Ssb[ci % 2]
                    SK = ("ar_gS", ci % 2)
                    P.dma("sp", S, Sin['gdn'][l, ci].rearrange("h k v -> k h v"), w=[SK])
                b = next_bank(4, 8)
                o_ap, i_ap, id_ap = psum[:C, b, 0:8], GB[:8, c0:c0 + C], ident[:8, :8]
                P.op("pe", lambda e, o_ap=o_ap, i_ap=i_ap, id_ap=id_ap: e.transpose(o_ap, i_ap, id_ap),
                     ["ar_GB", "ident"], [PSB(b)])
                copy("dve", Gt[:C, :], psum[:C, b, 0:8], [PSB(b)], [K_("Gt")])
                b = next_bank(4, 8)
                mm(psum[:C, b, 0:4], msk[:C, 2, 0, :C], Gt[:C, 0:4], True, True, ["msk", K_("Gt")], [PSB(b)])
                copy("dve", gc[:C, :], psum[:C, b, 0:4], [PSB(b)], [K_("gc")])
                tt("dve", GB3[:C], Gt[:C, 0:4].unsqueeze(2).to_broadcast([C, 4, 128]),
                   ones[:C, :].unsqueeze(1).to_broadcast([C, 4, 128]), ALU.mult, [K_("Gt"), "ones"], [K_("GB3")])
                bR = next_bank(4, 8)
                for h in range(H):
                    mm(psum[:, bR, h * C:(h + 1) * C], GB3[:C, h, :], msk[:C, 2, 0, :C], True, True,
                       [K_("GB3"), "msk"], [PSB(bR)])
                Rv = psum[:, bR, 0:H * C].rearrange("p (h c) -> p h c", h=H)
                act(EG[:, :, :C], Rv, AF.Exp, [PSB(bR)], [K_("EG")])
                act(gtot[:, :], Rv[:, :, C - 1], AF.Exp, [PSB(bR)], [K_("gtot")])
                tt("dve", Dm[:C, :, :C], gc[:C, :].unsqueeze(2).to_broadcast([C, 4, C]), Rv[:C], ALU.subtract,
                   [K_("gc"), PSB(bR)], [K_("D")])
                tt("dve", sm[:C, 8:12], Rv[:C, :, C - 1], gc[:C, :], ALU.subtract, [PSB(bR), K_("gc")], [K_("sm")])
                act(sm[:C, 8:12], sm[:C, 8:12], AF.Exp, [K_("sm")], [K_("sm")])
                act(sm[:C, 0:4], gc[:C, :], AF.Exp, [K_("gc")], [K_("sm")])
                tt("dve", sm[:C, 4:8], sm[:C, 0:4], Gt[:C, 4:8], ALU.mult, [K_("sm"), K_("Gt")], [K_("sm")])
                ts("dve", sm[:C, 12:16], Gt[:C, 4:8], -1.0, None, ALU.mult, None, [K_("Gt")], [K_("sm")])
                ts("dve", E1[:C, :, :C], Dm[:C, :, :C], 0.0, None, ALU.min, None, [K_("D")], [K_("E1")])
                act(E1[:C, :, :C], E1[:C, :, :C], AF.Exp, [K_("E1")], [K_("E1")])
                tt("dve", E1[:C, :, :C], E1[:C, :, :C], msk[:C, 0, 0:4, :C], ALU.mult, [K_("E1"), "msk"], [K_("E1")])
                ts("dve", E2[:C, :, :C], Dm[:C, :, :C], -1.0, 0.0, ALU.mult, ALU.min, [K_("D")], [K_("E2")])
                act(E2[:C, :, :C], E2[:C, :, :C], AF.Exp, [K_("E2")], [K_("E2")])
                tt("dve", E2[:C, :, :C], E2[:C, :, :C], msk[:C, 2, 0:4, :C], ALU.mult, [K_("E2"), "msk"], [K_("E2")])
                bK = next_bank(4, 8)
                for h in range(H):
                    o_ap, i_ap = psum[:C, bK, h * 128:(h + 1) * 128], QKV[:, 4 + h, c0:c0 + C]
                    P.op("pe", lambda e, o_ap=o_ap, i_ap=i_ap: e.transpose(o_ap, i_ap, ident[:, :]),
                         [("ar_qkv", 4 + h), "ident"], [PSB(bK)])
                Kt = psum[:C, bK, :].rearrange("p (h a) -> p h a", h=H)
                tt("dve", kbg[:C], Kt, sm[:C, 4:8].unsqueeze(2).to_broadcast([C, 4, 128]), ALU.mult,
                   [PSB(bK), K_("sm")], [K_("kbg")])
                tt("dve", ktl[:C], Kt, sm[:C, 8:12].unsqueeze(2).to_broadcast([C, 4, 128]), ALU.mult,
                   [PSB(bK), K_("sm")], [K_("ktl")])
                bV = next_bank(4, 8)
                for h in range(H):
                    o_ap, i_ap = psum[:C, bV, h * 128:(h + 1) * 128], QKV[:, 8 + h, c0:c0 + C]
                    P.op("pe", lambda e, o_ap=o_ap, i_ap=i_ap: e.transpose(o_ap, i_ap, ident[:, :]),
                         [("ar_qkv", 8 + h), "ident"], [PSB(bV)])
                tt("dve", vb[:C], psum[:C, bV, :].rearrange("p (h a) -> p h a", h=H),
                   Gt[:C, 4:8].unsqueeze(2).to_broadcast([C, 4, 128]), ALU.mult, [PSB(bV), K_("Gt")], [K_("vb")])
                bKK = next_bank(4, 8)
                for h in range(H):
                    mm(psum[:C, bKK, h * C:(h + 1) * C], QKV[:, 4 + h, c0:c0 + C], QKV[:, 4 + h, c0:c0 + C], True, True,
                       [("ar_qkv", 4 + h)], [PSB(bKK)])
                KKv = psum[:C, bKK, 0:H * C].rearrange("p (h c) -> p h c", h=H)
                tt("dve", QTm[:C, :, :C], KKv, E1[:C, :, :C], ALU.mult, [PSB(bKK), K_("E1")], [(GN, "QT")])
                tt("dve", QTm[:C, :, :C], QTm[:C, :, :C], sm[:C, 12:16].unsqueeze(2).to_broadcast([C, 4, C]), ALU.mult,
                   [(GN, "QT"), K_("sm")], [(GN, "QT")])
                bT = next_bank(4, 8)
                for h in range(H):
                    o_ap, i_ap, id_ap = psum[:C, bT, h * C:(h + 1) * C], QTm[:C, h, :C], ident[:C, :C]
                    P.op("pe", lambda e, o_ap=o_ap, i_ap=i_ap, id_ap=id_ap: e.transpose(o_ap, i_ap, id_ap),
                         [(GN, "QT"), "ident"], [PSB(bT)])
                copy("act", Qm[:C, :, :C], psum[:C, bT, 0:H * C].rearrange("p (h c) -> p h c", h=H), [PSB(bT)],
                     [(GN, "Q")])
                bQK = next_bank(4, 8)
                for h in range(H):
                    mm(psum[:C, bQK, h * C:(h + 1) * C], QKV[:, 4 + h, c0:c0 + C], QKV[:, h, c0:c0 + C], True, True,
                       [("ar_qkv", 4 + h), ("ar_qkv", h)], [PSB(bQK)])
                tt("dve", ATm[:C, :, :C], psum[:C, bQK, 0:H * C].rearrange("p (h c) -> p h c", h=H), E2[:C, :, :C],
                   ALU.mult, [PSB(bQK), K_("E2")], [K_("AT")])
                neumann(Qm[:C, :, :C], QTm[:C, :, :C], Q2m[:C, :, :C], QT2m[:C, :, :C], PTm[:C, :, :C], H, C, GN)
                bW = next_bank(4, 8)
                for h in range(H):
                    mm(psum[:, bW, h * C:(h + 1) * C], kbg[:C, h, :], PTm[:C, h, :C], True, True,
                       [K_("kbg"), (GN, "P")], [PSB(bW)])
                copy("act", wkT[:, :, :C], psum[:, bW, 0:H * C].rearrange("p (h c) -> p h c", h=H), [PSB(bW)], [K_("wkT")])
                bU = next_bank(4, 8)
                for h in range(H):
                    mm(psum[:C, bU, h * 128:(h + 1) * 128], PTm[:C, h, :C], vb[:C, h, :], True, True,
                       [(GN, "P"), K_("vb")], [PSB(bU)])
                copy("act", usb[:C], psum[:C, bU, :].rearrange("p (h a) -> p h a", h=H), [PSB(bU)], [K_("usb")])
                bWS = next_bank(4, 8)
                for h in range(H):
                    mm(psum[:C, bWS, h * 128:(h + 1) * 128], wkT[:, h, :C], S[:, h, :], True, True, [K_("wkT"), SK],
                       [PSB(bWS)])
                tt("dve", vnew[:C], usb[:C], psum[:C, bWS, :].rearrange("p (h a) -> p h a", h=H), ALU.subtract,
                   [K_("usb"), PSB(bWS)], [K_("vnew")])
                tt("dve", qdT[:, :, :C], QKV[:, 0:4, c0:c0 + C], EG[:, :, :C], ALU.mult, [("ar_qkv",), K_("EG")], [K_("qdT")])
                bO = next_bank(4, 8)
                for h in range(H):
                    mm(psum[:C, bO, h * 128:(h + 1) * 128], qdT[:, h, :C], S[:, h, :], True, False, [K_("qdT"), SK], [PSB(bO)])
                    mm(psum[:C, bO, h * 128:(h + 1) * 128], ATm[:C, h, :C], vnew[:C, h, :], False, True,
                       [K_("AT"), K_("vnew")], [PSB(bO)])
                copy("act", osb[:C], psum[:C, bO, :].rearrange("p (h a) -> p h a", h=H), [PSB(bO)], [K_("osb")])
                bS = next_bank(4, 8)
                for h in range(H):
                    mm(psum[:, bS, h * 128:(h + 1) * 128], ktl[:C, h, :], vnew[:C, h, :], True, True,
                       [K_("ktl"), K_("vnew")], [PSB(bS)])
                for h in range(H):
                    stt("dve", S[:, h, :], S[:, h, :], gtot[:, h:h + 1], psum[:, bS, h * 128:(h + 1) * 128], ALU.mult,
                        ALU.add, [SK, K_("gtot"), PSB(bS)], [SK])
                if mode == "s":
                    P.dma("sp", SSo['gdn'][l, ci].rearrange("h k v -> k h v"), S, r=[SK], dkey=("gS_o", ci % 2))
                tt("dve", osq[:C], osb[:C], osb[:C], ALU.mult, [K_("osb")], [K_("osq")])
                P.op("dve", lambda e, a=ss[:C, 0:4], b_=osq[:C]: e.tensor_reduce(a, b_, AX.X, ALU.add), [K_("osq")], [K_("ss")])
                ts("dve", ss[:C, 0:4], ss[:C, 0:4], 1.0 / GD_HD, NORM_EPS, ALU.mult, ALU.add, [K_("ss")], [K_("ss")])
                act(ss[:C, 0:4], ss[:C, 0:4], AF.Sqrt, [K_("ss")], [K_("ss")])
                P.op("dve", lambda e, a=ss[:C, 0:4]: e.reciprocal(a, a), [K_("ss")], [K_("ss")])
                tt("dve", osb[:C], osb[:C], ss[:C, 0:4].unsqueeze(2).to_broadcast([C, 4, 128]), ALU.mult,
                   [K_("osb"), K_("ss")], [K_("osb")])
                bF = next_bank(4, 8)
                for h in range(H):
                    o_ap, i_ap, id_ap = psum[:, bF, h * C:(h + 1) * C], osb[:C, h, :], ident[:C, :C]
                    P.op("pe", lambda e, o_ap=o_ap, i_ap=i_ap, id_ap=id_ap: e.transpose(o_ap, i_ap, id_ap),
                         [K_("osb"), "ident"], [PSB(bF)])
                for h in range(H):
                    stt("dve", ycat[:, 8 + h, c0:c0 + C], psum[:, bF, h * C:(h + 1) * C], gnw[:, l:l + 1],
                        ZS[:, h, c0:c0 + C], ALU.mult, ALU.mult, [PSB(bF), ("gnw", l), ("ar_zs", h)], [("ycat", 8 + h)])
            if mode == "p" and ti == NT - 1:
                P.dma("sp", SPo['gdn'][l].rearrange("h k v -> k h v"), gst[:, l], r=[("gst", l)], dkey=("gst_o",))
                for w_ in range(3):
                    store_fm(SPo['gdn_conv'][l, w_], gch[:, l, :, w_], 12, [("gch", l)])
            if mode == "s":
                sample_out(SSo['gdn_conv'][l, :, 2, :], lambda j, w: xnew[:w, j, :], 1536, lambda j: ("ar_gxn", j))
            barrier("x")

        from_mix["gdn"] = mix_gdn

    if "rwkv" in mixers:
        RSEG = [(i * 128, 128) for i in range(12)] + [(1536, 96), (1632, 96), (1728, 128), (1856, 128)]
        rst = P.sb("rst", [128, DEPTH, 4, 64], F32)
        rsh = P.sb("rsh", [128, DEPTH, 16], F32)
        rmu = P.sb("rmu", [128, DEPTH, 16], F32)
        rpp = P.sb("rpp", [128, DEPTH, 7, 4], F32)
        blk = P.sb("blk", [128, 128], F32)
        P.op("dve", lambda e: e.memset(rst[:], 0.0), w=["rst"])
        P.op("dve", lambda e: e.memset(rsh[:], 0.0), w=["rsh"])
        P.op("dve", lambda e: e.memset(rmu[:], 0.0), w=["rmu"])
        P.op("dve", lambda e: e.memset(blk[:], 0.0), w=["blk"])
        P.op("dve", lambda e: e.memset(blk[0:64, 0:64], 1.0), w=["blk"])
        P.op("dve", lambda e: e.memset(blk[64:128, 64:128], 1.0), w=["blk"])
        for l in range(DEPTH):
            for i, (f0, n) in enumerate(RSEG):
                P.dma("sp", rmu[:n, l, i:i + 1], Wd['rwkv_mu'][l, f0:f0 + n].rearrange("(p o) -> p o", o=1),
                      w=[("rmu", l, i)], dkey=("rmu", i % 4))
            for k, nm in enumerate(['rwkv_w0', 'rwkv_a0', 'rwkv_k_k', 'rwkv_k_a', None, 'rwkv_ln_w', 'rwkv_ln_b']):
                src = Wd['rwkv_r_k'][l].rearrange("h d -> (h d)") if nm is None else Wd[nm][l]
                load_fm(rpp[:, l, k, :], src, 4, ("rpp", l, k))
        barrier("x")

        def mix_rwkv(l, N, mode, ti):
            C = CH if mode == "p" else 1
            o = 0

            def AL(n):
                nonlocal o
                a = A_(o, n)
                o += n
                return a

            V4 = lambda a, k: a.rearrange("p (i n) -> p i n", i=k)
            Rr, Kx, Vv = V4(AL(4 * N), 4), V4(AL(4 * N), 4), V4(AL(4 * N), 4)
            TW, XA = AL(N), AL(N)
            SG = V4(AL(2 * N), 2)
            WU = AL(512)
            AU = AL(512)
            GU = V4(AL(1024), 2)
            P.dma("sp", WU[:96, :], Wd['rwkv_w_up'][l], w=["ar_r_WU"])
            P.dma("sp", AU[:96, :], Wd['rwkv_a_up'][l], w=["ar_r_AU"])
            P.dma("sp", GU, Wd['rwkv_g_up'][l].rearrange("(kc p) m -> p kc m", p=128), w=["ar_r_GU"])
            if mode == "s":
                sh0 = V4(AL(16 * NS), 16)
                pnew = V4(AL(16 * NS), 16)
                Ssb = [V4(AL(256), 4) for _ in range(2)]
                stg = arena[:, ARW - 2048:ARW]
                dma_rows("sp", stg[:NS, :1984], Sin['rwkv_shift'][l], 1984, w=["ar_sin"])
                for i, (f0, n) in enumerate(RSEG):
                    to_fm(lambda j, w, i=i: sh0[:w, i, :], stg[:NS, f0:f0 + n], NS, n, ["ar_sin"],
                          lambda j, i=i: ("ar_r_sh0", i))

            def cons(i, ps, pk):
                n = RSEG[i][1]
                pb = sqb[i % 2]
                pbk = ("sqb", i % 2)
                if i < 4:
                    dst, dk = Rr[:, i, :N], ("ar_r_R", i)
                elif i < 8:
                    dst, dk = Kx[:, i - 4, :N], ("ar_r_K", i - 4)
                elif i < 12:
                    dst, dk = Vv[:, i - 8, :N], ("ar_r_V", i - 8)
                elif i == 12:
                    dst, dk = TW[:n, :N], ("ar_r_TW",)
                elif i == 13:
                    dst, dk = XA[:n, :N], ("ar_r_XA",)
                else:
                    dst, dk = SG[:, i - 14, :N], ("ar_r_SG", i - 14)
                mu = rmu[:n, l, i:i + 1]
                if mode == "p":
                    copy("act", pb[:n, 0:N], ps, [pk], [pbk])
                    tt("dve", dst[:, 1:N], pb[:n, 0:N - 1], pb[:n, 1:N], ALU.subtract, [pbk], [dk])
                    tt("dve", dst[:, 0:1], rsh[:n, l, i:i + 1], pb[:n, 0:1], ALU.subtract, [pbk, ("rsh", l, i)], [dk])
                    copy("dve", rsh[:n, l, i:i + 1], pb[:n, N - 1:N], [pbk, dk], [("rsh", l, i)])
                    stt("dve", dst, dst, mu, pb[:n, 0:N], ALU.mult, ALU.add, [dk, pbk, ("rmu", l, i)], [dk])
                else:
                    copy("act", pnew[:n, i, :], ps, [pk], [("ar_r_pn", i)])
                    tt("dve", dst, sh0[:n, i, :], pnew[:n, i, :], ALU.subtract, [("ar_r_sh0", i), ("ar_r_pn", i)], [dk])
                    stt("dve", dst, dst, mu, pnew[:n, i, :], ALU.mult, ALU.add, [dk, ("ar_r_pn", i), ("rmu", l, i)], [dk])
                if i == 12:
                    act(dst, dst, AF.Tanh, [dk], [dk])
                if i >= 14:
                    act(dst, dst, AF.Sigmoid, [dk], [dk])

            dense(Wd['w_in'][l], 16, lambda kc: (hT[:, kc, :N], ("hT", kc)), RSEG, cons, N, "rwkv", ckey=("rwkv", l))
            H = 4
            obase = o
            for half in range(2):
                o = obase
                V2 = lambda a: a.rearrange("p (i n) -> p i n", i=2)
                LW, Aa, KKb, LC, Eb = (V2(AL(2 * N)) for _ in range(5))
                BON = LW
                KH = ("ar_r_h",)
                for q in range(2):
                    hp = 2 * half + q
                    cols = slice(hp * 128, (hp + 1) * 128)
                    pw0, pa0, pkk, pka, prk, plw, plb = (rpp[:, l, k, hp:hp + 1] for k in range(7))
                    b = next_bank(0, 8)
                    mm(psum[:, b, :N], WU[:96, cols], TW[:96, :N], True, True, ["ar_r_WU", ("ar_r_TW",)], [PSB(b)])
                    act(LW[:, q, :N], psum[:, b, :N], AF.Sigmoid, [PSB(b), ("rpp", l)], [("ar_r_LW", q)], bias=pw0)
                    ts("dve", LW[:, q, :N], LW[:, q, :N], -0.6065306597126334, None, ALU.mult, None, [("ar_r_LW", q)],
                       [("ar_r_LW", q)])
                    b = next_bank(0, 8)
                    mm(psum[:, b, :N], AU[:96, cols], XA[:96, :N], True, True, ["ar_r_AU", ("ar_r_XA",)], [PSB(b)])
                    act(Aa[:, q, :N], psum[:, b, :N], AF.Sigmoid, [PSB(b), ("rpp", l)], [("ar_r_A", q)], bias=pa0)
                    ts("dve", KKb[:, q, :N], Kx[:, hp, :N], pkk, None, ALU.mult, None, [("ar_r_K", hp), ("rpp", l)],
                       [("ar_r_KK", q)])
                    sq = sqb[q]
                    sk = ("sqb", q)
                    tt("dve", sq[:, :N], KKb[:, q, :N], KKb[:, q, :N], ALU.mult, [("ar_r_KK", q)], [sk])
                    b = next_bank(0, 8)
                    mm(psum[:, b, :N], blk[:], sq[:, :N], True, True, [sk, "blk"], [PSB(b)])
                    ts("dve", sq[:, :N], psum[:, b, :N], 1e-12, None, ALU.add, None, [PSB(b)], [sk])
                    act(sq[:, :N], sq[:, :N], AF.Sqrt, [sk], [sk])
                    P.op("dve", lambda e, sq=sq: e.reciprocal(sq[:, :N], sq[:, :N]), [sk], [sk])
                    tt("dve", KKb[:, q, :N], KKb[:, q, :N], sq[:, :N], ALU.mult, [("ar_r_KK", q), sk], [("ar_r_KK", q)])
                    ts("dve", sq[:, :N], Aa[:, q, :N], -1.0, pka, ALU.add, ALU.mult, [("ar_r_A", q), ("rpp", l)], [sk])
                    stt("dve", Kx[:, hp, :N], sq[:, :N], 1.0, Kx[:, hp, :N], ALU.add, ALU.mult, [sk, ("ar_r_K", hp)],
                        [("ar_r_K", hp)])
                    tt("dve", Aa[:, q, :N], Aa[:, q, :N], KKb[:, q, :N], ALU.mult, [("ar_r_A", q), ("ar_r_KK", q)],
                       [("ar_r_A", q)])
                    if mode == "p":
                        for ci in range(N // C):
                            c0 = ci * C
                            o_ap, d1 = LC[:, q, c0:c0 + C], LW[:, q, c0:c0 + C]
                            P.op("dve", lambda e, o_ap=o_ap, d1=d1: e.tensor_tensor_scan(
                                o_ap, ones[:, 0:C], d1, 0.0, ALU.mult, ALU.add), [("ar_r_LW", q), "ones"], [("ar_r_LC", q)])
                    else:
                        copy("dve", LC[:, q, :N], LW[:, q, :N], [("ar_r_LW", q)], [("ar_r_LC", q)])
                    tt("dve", Eb[:, q, :N], LC[:, q, :N], LW[:, q, :N], ALU.subtract, [("ar_r_LC", q), ("ar_r_LW", q)],
                       [("ar_r_E", q)])
                    act(Eb[:, q, :N], Eb[:, q, :N], AF.Exp, [("ar_r_E", q)], [("ar_r_E", q)])
                    tt("dve", KKb[:, q, :N], KKb[:, q, :N], Eb[:, q, :N], ALU.mult, [("ar_r_KK", q), ("ar_r_E", q)],
                       [("ar_r_KK", q)])
                    tt("dve", sq[:, :N], Rr[:, hp, :N], Kx[:, hp, :N], ALU.mult, [("ar_r_R", hp), ("ar_r_K", hp)], [sk])
                    ts("dve", sq[:, :N], sq[:, :N], prk, None, ALU.mult, None, [sk, ("rpp", l)], [sk])
                    b = next_bank(0, 8)
                    mm(psum[:, b, :N], blk[:], sq[:, :N], True, True, [sk, "blk"], [PSB(b)])
                    tt("dve", BON[:, q, :N], psum[:, b, :N], Vv[:, hp, :N], ALU.mult, [PSB(b), ("ar_r_V", hp)],
                       [("ar_r_LW", q)])
                    act(Eb[:, q, :N], LC[:, q, :N], AF.Exp, [("ar_r_LC", q), ("ar_r_E", q)], [("ar_r_E", q)], scale=-1.0)
                    tt("dve", Aa[:, q, :N], Aa[:, q, :N], Eb[:, q, :N], ALU.mult, [("ar_r_A", q), ("ar_r_E", q)],
                       [("ar_r_A", q)])
                    tt("dve", Kx[:, hp, :N], Kx[:, hp, :N], Eb[:, q, :N], ALU.mult, [("ar_r_K", hp), ("ar_r_E", q)],
                       [("ar_r_K", hp)])
                    act(LC[:, q, :N], LC[:, q, :N], AF.Exp, [("ar_r_LC", q)], [("ar_r_LC", q)])
                    tt("dve", Rr[:, hp, :N], Rr[:, hp, :N], LC[:, q, :N], ALU.mult, [("ar_r_R", hp), ("ar_r_LC", q)],
                       [("ar_r_R", hp)])
                def ralloc():
                    d = {}
                    M4 = lambda: AL(H * C).rearrange("p (h c) -> p h c", h=H)
                    for nm in ("AakT", "BqkT", "BqaT", "Qm", "QTm", "Q2m", "QT2m", "PTm"):
                        d[nm] = M4()
                    T64 = lambda: AL(H * 64).rearrange("p (h c) -> p h c", h=H)
                    for nm in ("Vtok", "KTtok", "ATtok", "Rsb", "Usb"):
                        d[nm] = T64()
                    d["Ysq"] = d["Rsb"]
                    d["Ysb"] = d["AakT"] if C == 64 else T64()
                    d["MK"] = AL(8 * C).rearrange("p (k q h c) -> p k q h c", k=2, q=2, h=2)
                    d["st8"] = AL(8)
                    return d

                rsets = [ralloc() for _ in range(2 if mode == "s" else 1)]
                msl, msu, miu = msk[:C, 0, 0:H, :C], msk[:C, 1, 0:H, :C], msk[:C, 2, 0:H, :C]
                for ci in range(N // C):
                    c0 = ci * C
                    cs = slice(c0, c0 + C)
                    si_ = ci % len(rsets)
                    T_ = rsets[si_]
                    AakT, BqkT, BqaT, Qm, QTm, Q2m, QT2m, PTm = (T_[k] for k in (
                        "AakT", "BqkT", "BqaT", "Qm", "QTm", "Q2m", "QT2m", "PTm"))
                    Vtok, KTtok, ATtok, Rsb, Usb, Ysq, Ysb, MK, st8 = (T_[k] for k in (
                        "Vtok", "KTtok", "ATtok", "Rsb", "Usb", "Ysq", "Ysb", "MK", "st8"))
                    K_ = lambda nm, si_=si_: ("ar_r_c_" + nm, si_)
                    RN = "ar_rn%d" % si_
                    if mode == "p":
                        S = rst[:, l]
                        SK = ("rst", l)
                    else:
                        S = Ssb[ci % 2]
                        SK = ("ar_r_S", ci % 2, half)
                        if half == 0:
                            P.dma("sp", S, Sin['rwkv_wkv'][l, ci].rearrange("(hp h2) k v -> (h2 k) hp v", h2=2),
                                  w=[("ar_r_S", ci % 2)])
                            SK = ("ar_r_S", ci % 2)
                        else:
                            SK = ("ar_r_S", ci % 2)
                    if mode == "s" and half == 1:
                        P.dma("sp", S, Sin['rwkv_wkv'][l, ci].rearrange("(hp h2) k v -> (h2 k) hp v", h2=2),
                              w=[("ar_r_S", ci % 2)])
                    banks = [next_bank(0, 8) for _ in range(5)]
                    for q in range(2):
                        hp = 2 * half + q
                        for h2 in range(2):
                            hmask = blk[:, 64 * h2:64 * h2 + 1]
                            ts("dve", MK[:, 0, q, h2, :C], KKb[:, q, cs], hmask, None, ALU.mult, None,
                               [("ar_r_KK", q), "blk"], [K_("MK")])
                            ts("dve", MK[:, 1, q, h2, :C], Rr[:, hp, cs], hmask, None, ALU.mult, None,
                               [("ar_r_R", hp), "blk"], [K_("MK")])
                    for q in range(2):
                        hp = 2 * half + q
                        for h2 in range(2):
                            hl = q * 2 + h2
                            kt, at = Kx[:, hp, cs], Aa[:, q, cs]
                            kp, qh = MK[:, 0, q, h2, :C], MK[:, 1, q, h2, :C]
                            rk = [("ar_r_K", hp), ("ar_r_A", q), K_("MK")]
                            oc = slice(hl * C, (hl + 1) * C)
                            mm(psum[:C, banks[0], oc], kt, kp, True, True, rk, [PSB(banks[0])])
                            mm(psum[:C, banks[1], oc], kt, qh, True, True, rk, [PSB(banks[1])])
                            mm(psum[:C, banks[2], oc], at, kp, True, True, rk, [PSB(banks[2])])
                            mm(psum[:C, banks[3], oc], at, qh, True, True, rk, [PSB(banks[3])])
                            mm(psum[:C, banks[4], oc], kp, at, True, True, rk, [PSB(banks[4])])
                    pv = lambda bi: psum[:C, banks[bi], 0:H * C].rearrange("p (h c) -> p h c", h=H)
                    tt("dve", AakT[:C], pv(0), msu, ALU.mult, [PSB(banks[0]), "msk"], [K_("AakT")])
                    tt("dve", BqkT[:C], pv(1), miu, ALU.mult, [PSB(banks[1]), "msk"], [K_("BqkT")])
                    stt("dve", Qm[:C], pv(2), -1.0, msu, ALU.mult, ALU.mult, [PSB(banks[2]), "msk"], [(RN, "Q")])
                    tt("dve", BqaT[:C], pv(3), miu, ALU.mult, [PSB(banks[3]), "msk"], [K_("BqaT")])
                    stt("dve", QTm[:C], pv(4), -1.0, msl, ALU.mult, ALU.mult, [PSB(banks[4]), "msk"], [(RN, "QT")])
                    neumann(Qm[:C], QTm[:C], Q2m[:C], QT2m[:C], PTm[:C], H, C, RN)
                    for (srcfn, dstt, nm) in ((lambda q: Vv[:, 2 * half + q, cs], Vtok, "Vtok"),
                                              (lambda q: Kx[:, 2 * half + q, cs], KTtok, "KTtok"),
                                              (lambda q: Aa[:, q, cs], ATtok, "ATtok")):
                        b = next_bank(0, 8)
                        for q in range(2):
                            o_ap, i_ap = psum[:C, b, q * 128:(q + 1) * 128], srcfn(q)
                            P.op("pe", lambda e, o_ap=o_ap, i_ap=i_ap: e.transpose(o_ap, i_ap, ident[:, :]),
                                 [("ar_r_V",), ("ar_r_K",), ("ar_r_A",), "ident"], [PSB(b)])
                        copy("act", dstt[:C], psum[:C, b, 0:256].rearrange("p (h c) -> p h c", h=H), [PSB(b)], [K_(nm)])
                    b = next_bank(0, 8)
                    for q in range(2):
                        hp = 2 * half + q
                        for h2 in range(2):
                            hl = q * 2 + h2
                            rows = slice(64 * h2, 64 * h2 + 64)
                            oc = slice(hl * 64, (hl + 1) * 64)
                            mm(psum[:C, b, oc], MK[:, 0, q, h2, :C], S[:, hp, :], True, False, [K_("MK"), SK], [PSB(b)])
                            mm(psum[:C, b, oc], AakT[:C, hl, :], Vtok[:C, hl, :], False, True, [K_("AakT"), K_("Vtok")],
                               [PSB(b)])
                    ts("dve", Rsb[:C], psum[:C, b, 0:256].rearrange("p (h c) -> p h c", h=H), -1.0, None, ALU.mult, None,
                       [PSB(b)], [K_("Rsb")])
                    b = next_bank(0, 8)
                    for hl in range(H):
                        mm(psum[:C, b, hl * 64:(hl + 1) * 64], PTm[:C, hl, :], Rsb[:C, hl, :], True, True,
                           [(RN, "P"), K_("Rsb")], [PSB(b)])
                    copy("act", Usb[:C], psum[:C, b, 0:256].rearrange("p (h c) -> p h c", h=H), [PSB(b)], [K_("Usb")])
                    b = next_bank(0, 8)
                    for q in range(2):
                        hp = 2 * half + q
                        for h2 in range(2):
                            hl = q * 2 + h2
                            rows = slice(64 * h2, 64 * h2 + 64)
                            oc = slice(hl * 64, (hl + 1) * 64)
                            mm(psum[:C, b, oc], MK[:, 1, q, h2, :C], S[:, hp, :], True, False, [K_("MK"), SK], [PSB(b)])
                            mm(psum[:C, b, oc], BqkT[:C, hl, :], Vtok[:C, hl, :], False, False, [K_("BqkT"), K_("Vtok")],
                               [PSB(b)])
                            mm(psum[:C, b, oc], BqaT[:C, hl, :], Usb[:C, hl, :], False, True, [K_("BqaT"), K_("Usb")],
                               [PSB(b)])
                    copy("act", Ysb[:C], psum[:C, b, 0:256].rearrange("p (h c) -> p h c", h=H), [PSB(b)], [K_("AakT")])
                    b = next_bank(0, 8)
                    for q in range(2):
                        for h2 in range(2):
                            hl = q * 2 + h2
                            oc = slice(hl * 64, (hl + 1) * 64)
                            mm(psum[:, b, oc], KTtok[:C, 2 * q:2 * q + 2, :].rearrange("p h c -> p (h c)"), Vtok[:C, hl, :],
                               True, False, [K_("KTtok"), K_("Vtok")], [PSB(b)])
                            mm(psum[:, b, oc], ATtok[:C, 2 * q:2 * q + 2, :].rearrange("p h c -> p (h c)"), Usb[:C, hl, :],
                               False, True, [K_("ATtok"), K_("Usb")], [PSB(b)])
                    for q in range(2):
                        hp = 2 * half + q
                        for h2 in range(2):
                            hl = q * 2 + h2
                            rows = slice(64 * h2, 64 * h2 + 64)
                            tt("dve", S[rows, hp, :], S[rows, hp, :], psum[rows, b, hl * 64:(hl + 1) * 64], ALU.add,
                               [SK, PSB(b)], [SK])
                            ts("dve", S[rows, hp, :], S[rows, hp, :], LC[rows, q, c0 + C - 1:c0 + C], None, ALU.mult, None,
                               [SK, ("ar_r_LC", q)], [SK])
                    if mode == "s":
                        for q in range(2):
                            hp = 2 * half + q
                            P.dma("sp", SSo['rwkv_wkv'][l, ci, 2 * hp:2 * hp + 2].rearrange("h2 k v -> (h2 k) v"),
                                  S[:, hp, :], r=[SK], dkey=("rS_o", ci % 2, q))
                    P.op("dve", lambda e, a=st8[:C, 0:4], b_=Ysb[:C]: e.tensor_reduce(a, b_, AX.X, ALU.add), [K_("AakT")],
                         [K_("st")])
                    ts("dve", st8[:C, 0:4], st8[:C, 0:4], -1.0 / 64, None, ALU.mult, None, [K_("st")], [K_("st")])
                    tt("dve", Ysb[:C], Ysb[:C], st8[:C, 0:4].unsqueeze(2).to_broadcast([C, H, 64]), ALU.add,
                       [K_("AakT"), K_("st")], [K_("AakT")])
                    tt("dve", Ysq[:C], Ysb[:C], Ysb[:C], ALU.mult, [K_("AakT")], [K_("Rsb")])
                    P.op("dve", lambda e, a=st8[:C, 4:8], b_=Ysq[:C]: e.tensor_reduce(a, b_, AX.X, ALU.add), [K_("Rsb")],
                         [K_("st")])
                    ts("dve", st8[:C, 4:8], st8[:C, 4:8], 1.0 / 64, 64e-5, ALU.mult, ALU.add, [K_("st")], [K_("st")])
                    act(st8[:C, 4:8], st8[:C, 4:8], AF.Sqrt, [K_("st")], [K_("st")])
                    P.op("dve", lambda e, a=st8[:C, 4:8]: e.reciprocal(a, a), [K_("st")], [K_("st")])
                    tt("dve", Ysb[:C], Ysb[:C], st8[:C, 4:8].unsqueeze(2).to_broadcast([C, H, 64]), ALU.mult,
                       [K_("AakT"), K_("st")], [K_("AakT")])
                    bF = next_bank(0, 8)
                    for q in range(2):
                        o_ap, i_ap, id_ap = psum[:, bF, q * C:(q + 1) * C], Ysb[:C, 2 * q:2 * q + 2, :].rearrange(
                            "p h c -> p (h c)"), ident[:C, :C]
                        P.op("pe", lambda e, o_ap=o_ap, i_ap=i_ap, id_ap=id_ap: e.transpose(o_ap, i_ap, id_ap),
                             [K_("AakT"), "ident"], [PSB(bF)])
                    bG = next_bank(0, 8)
                    for q in range(2):
                        hp = 2 * half + q
                        for kc in range(2):
                            mm(psum[:, bG, q * C:(q + 1) * C], GU[:, kc, hp * 128:(hp + 1) * 128], SG[:, kc, cs], kc == 0,
                               kc == 1, ["ar_r_GU", ("ar_r_SG", kc)], [PSB(bG)])
                    for q in range(2):
                        hp = 2 * half + q
                        yf = Eb[:, q, cs]
                        ts("dve", yf, psum[:, bF, q * C:(q + 1) * C], rpp[:, l, 5, hp:hp + 1], rpp[:, l, 6, hp:hp + 1],
                           ALU.mult, ALU.add, [PSB(bF), ("rpp", l)], [("ar_r_E", q)])
                        tt("dve", yf, yf, BON[:, q, cs], ALU.add, [("ar_r_E", q), ("ar_r_LW", q)], [("ar_r_E", q)])
                        tt("dve", ycat[:, hp, cs], yf, psum[:, bG, q * C:(q + 1) * C], ALU.mult, [("ar_r_E", q), PSB(bG)],
                           [("ycat", hp)])
                barrier("x")
            if mode == "p" and ti == NT - 1:
                P.dma("sp", SPo['rwkv_wkv'][l].rearrange("(hp h2) k v -> (h2 k) hp v", h2=2), rst[:, l], r=[("rst", l)],
                      dkey=("rst_o",))
                for i, (f0, n) in enumerate(RSEG):
                    P.dma("sp", SPo['rwkv_shift'][l, f0:f0 + n].rearrange("(p o) -> p o", o=1), rsh[:n, l, i:i + 1],
                          r=[("rsh", l, i)], dkey=("rsh_o", i % 4))
            if mode == "s":
                stg2 = arena[:, ARW - 4096:ARW - 2048]
                for i, (f0, n) in enumerate(RSEG):
                    b = next_bank(0, 8)
                    o_ap, i_ap, id_ap = psum[:NS, b, 0:n], pnew[:n, i, :], ident[:n, :n]
                    P.op("pe", lambda e, o_ap=o_ap, i_ap=i_ap, id_ap=id_ap: e.transpose(o_ap, i_ap, id_ap),
                         [("ar_r_pn", i), "ident"], [PSB(b)])
                    copy("dve", stg2[:NS, f0:f0 + n], psum[:NS, b, 0:n], [PSB(b)], [("ar_r_stg2", i)])
                dma_rows("sp", SSo['rwkv_shift'][l], stg2[:NS, :1984], 1984, r=["ar_r_stg2"], dkey="ar_r_stg2")
            barrier("x")

        from_mix["rwkv"] = mix_rwkv

    if "s5" in mixers:
        import math
        PI = math.pi
        NLEV = 9
        s5m = P.dram("s5m", [DEPTH, 4, 128, 16 * 128], F32).ap()
        s5c = P.sb("s5c", [128, DEPTH, 12, 16], F32)
        s5pw = P.sb("s5pw", [128, DEPTH, 16, NLEV, 3], F32)
        s5h = P.sb("s5h", [128, DEPTH, 16, 2], F32)
        s5d = P.sb("s5d", [128, DEPTH, 2, 4], F32)
        par = P.sb("par", [128, 4], F32)
        P.op("dve", lambda e: e.memset(s5h[:], 0.0), w=["s5h"])
        G8 = A_(0, 8)
        P.op("pool", lambda e: e.memset(G8, 1.0), w=["ar_g8"])
        P.op("pool", lambda e: e.affine_select(G8, G8, pattern=[[-16, 8]], compare_op=ALU.is_ge, fill=0.0, base=0,
                                               channel_multiplier=1), r=["ar_g8"], w=["ar_g8"])
        P.op("pool", lambda e: e.affine_select(G8, G8, pattern=[[16, 8]], compare_op=ALU.is_ge, fill=0.0, base=15,
                                               channel_multiplier=-1), r=["ar_g8"], w=["ar_g8"])
        G8v = G8.rearrange("p (i two) -> p two i", two=2)
        P.op("dve", lambda e: e.tensor_reduce(par[:, 0:2], G8v, AX.X, ALU.add), r=["ar_g8"], w=["par"])
        ts("dve", par[:, 2:4], par[:, 0:2], -1.0, None, ALU.mult, None, ["par"], ["par"])
        XZ = [A_(64 + m * 128, 128) for m in range(4)]
        for m in range(4):
            P.op("dve", lambda e, m=m: e.memset(XZ[m], 0.0), w=[("ar_xz", m)])
        for l in range(DEPTH):
            C_ = lambda k: s5c[:, l, k, :]
            CK = lambda k: ("s5c", l, k)
            load_fm(C_(0), Wd['s5_lambda_re'][l].rearrange("g n -> (g n)"), 16, CK(0))
            load_fm(C_(1), Wd['s5_lambda_im'][l].rearrange("g n -> (g n)"), 16, CK(1))
            load_fm(s5d[:, l, 0, :], Wd['s5_d'][l], 4, ("s5d", l, 0))
            load_fm(s5d[:, l, 1, :], Wd['s5_glu_b'][l], 4, ("s5d", l, 1))
            ldt = A_(32, 32)
            P.dma("sp", ldt[0:1, :], Wd['s5_log_dt'][l:l + 1, :], w=["ar_ldt"])
            b = next_bank(4, 8)
            mm(psum[:, b, 0:32], ones[0:1, :], ldt[0:1, :], True, True, ["ar_ldt", "ones"], [PSB(b)])
            dtb = A_(576, 32)
            act(dtb, psum[:, b, 0:32], AF.Exp, [PSB(b)], ["ar_dtb"])
            dv = dtb.rearrange("p (j g) -> p j g", g=2)
            copy("dve", s5c[0:64, l, 2, :], dv[0:64, :, 0], ["ar_dtb"], [CK(2)])
            copy("dve", s5c[64:128, l, 2, :], dv[64:128, :, 1], ["ar_dtb"], [CK(2)])
            tt("dve", C_(7), C_(0), C_(2), ALU.mult, [CK(0), CK(2)], [CK(7)])
            act(C_(7), C_(7), AF.Exp, [CK(7)], [CK(7)])
            tt("dve", C_(8), C_(1), C_(2), ALU.mult, [CK(1), CK(2)], [CK(8)])
            for slot, shift in ((4, 0.0), (3, PI / 2)):
                ts("dve", C_(9), C_(8), shift, None, ALU.add, None, [CK(8)], [CK(9)])
                copy("dve", C_(10), C_(9), [CK(9)], [CK(10)])
                for k in range(1, 7):
                    ts("dve", C_(11), C_(9), (2 * k - 1) * PI, -2 * PI, ALU.is_ge, ALU.mult, [CK(9)], [CK(11)])
                    tt("dve", C_(10), C_(10), C_(11), ALU.add, [CK(10), CK(11)], [CK(10)])
                act(C_(slot), C_(10), AF.Sin, [CK(10)], [CK(slot)])
                tt("dve", C_(slot), C_(slot), C_(7), ALU.mult, [CK(slot), CK(7)], [CK(slot)])
            tt("dve", C_(9), C_(0), C_(0), ALU.mult, [CK(0)], [CK(9)])
            tt("dve", C_(10), C_(1), C_(1), ALU.mult, [CK(1)], [CK(10)])
            tt("dve", C_(9), C_(9), C_(10), ALU.add, [CK(9), CK(10)], [CK(9)])
            P.op("dve", lambda e, l=l: e.reciprocal(s5c[:, l, 9, :], s5c[:, l, 9, :]), [CK(9)], [CK(9)])
            ts("dve", C_(10), C_(3), -1.0, None, ALU.add, None, [CK(3)], [CK(10)])
            tt("dve", C_(5), C_(10), C_(0), ALU.mult, [CK(10), CK(0)], [CK(5)])
            tt("dve", C_(11), C_(4), C_(1), ALU.mult, [CK(4), CK(1)], [CK(11)])
            tt("dve", C_(5), C_(5), C_(11), ALU.add, [CK(5), CK(11)], [CK(5)])
            tt("dve", C_(5), C_(5), C_(9), ALU.mult, [CK(5), CK(9)], [CK(5)])
            tt("dve", C_(6), C_(4), C_(0), ALU.mult, [CK(4), CK(0)], [CK(6)])
            tt("dve", C_(11), C_(10), C_(1), ALU.mult, [CK(10), CK(1)], [CK(11)])
            tt("dve", C_(6), C_(6), C_(11), ALU.subtract, [CK(6), CK(11)], [CK(6)])
            tt("dve", C_(6), C_(6), C_(9), ALU.mult, [CK(6), CK(9)], [CK(6)])
            copy("dve", s5pw[:, l, :, 0, 0], C_(3), [CK(3)], [("s5pw", l)])
            copy("dve", s5pw[:, l, :, 0, 1], C_(4), [CK(4)], [("s5pw", l)])
            for lev in range(1, NLEV):
                pr, pi_ = s5pw[:, l, :, lev - 1, 0], s5pw[:, l, :, lev - 1, 1]
                tt("dve", C_(10), pr, pr, ALU.mult, [("s5pw", l)], [CK(10)])
                tt("dve", C_(11), pi_, pi_, ALU.mult, [("s5pw", l)], [CK(11)])
                tt("dve", s5pw[:, l, :, lev, 0], C_(10), C_(11), ALU.subtract, [CK(10), CK(11)], [("s5pw", l)])
                tt("dve", C_(10), pr, pi_, ALU.mult, [("s5pw", l)], [CK(10)])
                ts("dve", s5pw[:, l, :, lev, 1], C_(10), 2.0, None, ALU.mult, None, [CK(10)], [("s5pw", l)])
            ts("dve", s5pw[:, l, :, :, 2], s5pw[:, l, :, :, 1], -1.0, None, ALU.mult, None, [("s5pw", l)], [("s5pw", l)])
            braw = [A_(1024 + k * 256, 256).rearrange("p (j c) -> p j c", j=16) for k in range(4)]
            for k, nm in enumerate(['s5_b_re', 's5_b_im']):
                P.dma("sp", braw[k], Wd[nm][l].rearrange("(j gl) n c -> (gl n) j c", gl=2), w=[("ar_braw", k)])
            fre = s5c[:, l, 5, :].unsqueeze(2).to_broadcast([128, 16, 16])
            fim = s5c[:, l, 6, :].unsqueeze(2).to_broadcast([128, 16, 16])
            tmpb = A_(2048, 256).rearrange("p (j c) -> p j c", j=16)
            tt("dve", braw[2], braw[0], fre, ALU.mult, [("ar_braw", 0), CK(5)], [("ar_braw", 2)])
            tt("dve", tmpb, braw[1], fim, ALU.mult, [("ar_braw", 1), CK(6)], ["ar_tmpb"])
            tt("dve", braw[2], braw[2], tmpb, ALU.subtract, [("ar_braw", 2), "ar_tmpb"], [("ar_braw", 2)])
            tt("dve", braw[3], braw[1], fre, ALU.mult, [("ar_braw", 1), CK(5)], [("ar_braw", 3)])
            tt("dve", tmpb, braw[0], fim, ALU.mult, [("ar_braw", 0), CK(6)], ["ar_tmpb"])
            tt("dve", braw[3], braw[3], tmpb, ALU.add, [("ar_braw", 3), "ar_tmpb"], [("ar_braw", 3)])
            MT = A_(4096, 2048).rearrange("p (j m) -> p j m", j=16)
            for kind in range(2):
                for j in range(16):
                    m = j % 4
                    X = XZ[m]
                    copy("dve", X[0:64, 32 * m:32 * m + 16], braw[2 + kind][0:64, j, :], [("ar_braw", 2 + kind)],
                         [("ar_xz", m)])
                    copy("dve", X[64:128, 32 * m + 16:32 * m + 32], braw[2 + kind][64:128, j, :],
                         [("ar_braw", 2 + kind)], [("ar_xz", m)])
                    b = next_bank(4, 8)
                    transp(psum[:, b, 0:128], X, 128, [("ar_xz", m)], [PSB(b)])
                    copy("act", MT[:, j, :], psum[:, b, 0:128], [PSB(b)], [("ar_mt", j)])
                P.dma("sp", s5m[l, kind], MT.rearrange("p j m -> p (j m)"), r=["ar_mt"], w=[("s5m", l, kind)],
                      dkey=("s5m_o",))
            for kind, nm in enumerate(['s5_c_re', 's5_c_im']):
                P.op("dve", lambda e: e.memset(MT, 0.0), w=["ar_mt"])
                for ct in range(4):
                    ctile = A_(3072, 64)
                    Z = A_(3200, 128)
                    P.dma("sp", ctile, Wd[nm][l].rearrange("g c n -> (g c) n")[ct * 128:(ct + 1) * 128, :],
                          w=["ar_ctile"])
                    ts("dve", Z[:, 0:64], ctile, par[:, 2 * kind:2 * kind + 1], None, ALU.mult, None,
                       ["ar_ctile", "par"], ["ar_z"])
                    ts("dve", Z[:, 64:128], ctile, par[:, 2 * kind + 1:2 * kind + 2], None, ALU.mult, None,
                       ["ar_ctile", "par"], ["ar_z"])
                    b = next_bank(4, 8)
                    transp(psum[:, b, 0:128], Z, 128, ["ar_z"], [PSB(b)])
                    for m in range(4):
                        copy("act", MT[:, 4 * ct + m, 32 * m:32 * m + 32], psum[:, b, 32 * m:32 * m + 32], [PSB(b)],
                             [("ar_mt", 4 * ct + m)])
                P.dma("sp", s5m[l, 2 + kind], MT.rearrange("p j m -> p (j m)"), r=["ar_mt"], w=[("s5m", l, 2 + kind)],
                      dkey=("s5m_o",))
        barrier("x")

        def mix_s5(l, N, mode, ti):
            M = A_(0, 8192).rearrange("p (k j m) -> p k j m", k=4, j=16)
            for k in range(4):
                P.dma("sp", M[:, k].rearrange("p j m -> p (j m)"), s5m[l, k], r=[("s5m", l, k)], w=[("ar_M", k)])
            o = 8192
            U = A_(o, 4 * N).rearrange("p (c n) -> p c n", c=4)
            o += 4 * N
            HB = [[[A_(o + ((s_ * 2 + pp) * 2 + ri) * N, N) for ri in range(2)] for pp in range(2)] for s_ in range(2)]
            o += 8 * N
            ZF = A_(o, 4 * N).rearrange("p (c n) -> p c n", c=4)
            o += 4 * N
            ZB = A_(o, 2 * N).bitcast(BF16).rearrange("p (c n) -> p c n", c=4)
            o += 2 * N
            if mode == "s":
                H0 = A_(o, 32 * NS).rearrange("p (r j n) -> p r j n", r=2, j=16)
                o += 32 * NS
                HN = A_(o, 32 * NS).rearrange("p (r j n) -> p r j n", r=2, j=16)
                o += 32 * NS
                sample_in(H0[:, 0], Sin['s5_re'][l].rearrange("b g n -> b (g n)"), 2048, "ar_h0r")
                sample_in(H0[:, 1], Sin['s5_im'][l].rearrange("b g n -> b (g n)"), 2048, "ar_h0i")

            def cons_u(i, ps, pk):
                copy("act", U[:, i, :N], ps, [pk], [("ar_U", i)])

            dense(Wd['w_in'][l], 16, lambda kc: (hT[:, kc, :N], ("hT", kc)),
                  [(OFF_S5 + c * 128, 128) for c in range(4)], cons_u, N, "s5u", ckey=("s5u", l))
            for ct in range(4):
                by = 4 + ct % 2
                for m in range(4):
                    j = 4 * ct + m
                    set_ = HB[j % 2]
                    sk = ("ar_hb", j % 2)
                    cur = 0
                    for ri in range(2):
                        b = next_bank(6, 8)
                        mm(psum[:, b, :N], M[:, ri, j, :], U[:, ct, :N], True, True, [("ar_M", ri), ("ar_U", ct)], [PSB(b)])
                        copy("act" if ri else "dve", set_[0][ri], psum[:, b, :N], [PSB(b)], [sk])
                    ar, ai, nai = (s5pw[:, l, j, 0, k:k + 1] for k in range(3))
                    if mode == "p":
                        h0r, h0i = s5h[:, l, j, 0:1], s5h[:, l, j, 1:2]
                        br0, bi0 = set_[0][0][:, 0:1], set_[0][1][:, 0:1]
                        stt("dve", br0, h0r, ar, br0, ALU.mult, ALU.add, [("s5h", l, j), ("s5pw", l), sk], [sk])
                        stt("dve", br0, h0i, nai, br0, ALU.mult, ALU.add, [("s5h", l, j), ("s5pw", l), sk], [sk])
                        stt("dve", bi0, h0i, ar, bi0, ALU.mult, ALU.add, [("s5h", l, j), ("s5pw", l), sk], [sk])
                        stt("dve", bi0, h0r, ai, bi0, ALU.mult, ALU.add, [("s5h", l, j), ("s5pw", l), sk], [sk])
                        lev = 0
                        d = 1
                        while d < N:
                            src, dst = set_[cur], set_[1 - cur]
                            pr, pi_, npi = (s5pw[:, l, j, lev, k:k + 1] for k in range(3))
                            for ri in range(2):
                                copy("dve", dst[ri][:, 0:d], src[ri][:, 0:d], [sk], [sk])
                            stt("dve", dst[0][:, d:N], src[1][:, 0:N - d], npi, src[0][:, d:N], ALU.mult, ALU.add,
                                [sk, ("s5pw", l)], [sk])
                            stt("dve", dst[0][:, d:N], src[0][:, 0:N - d], pr, dst[0][:, d:N], ALU.mult, ALU.add,
                                [sk, ("s5pw", l)], [sk])
                            stt("dve", dst[1][:, d:N], src[0][:, 0:N - d], pi_, src[1][:, d:N], ALU.mult, ALU.add,
                                [sk, ("s5pw", l)], [sk])
                            stt("dve", dst[1][:, d:N], src[1][:, 0:N - d], pr, dst[1][:, d:N], ALU.mult, ALU.add,
                                [sk, ("s5pw", l)], [sk])
                            cur = 1 - cur
                            d *= 2
                            lev += 1
                        fin = set_[cur]
                        copy("dve", s5h[:, l, j, 0:1], fin[0][:, N - 1:N], [sk], [("s5h", l, j)])
                        copy("dve", s5h[:, l, j, 1:2], fin[1][:, N - 1:N], [sk], [("s5h", l, j)])
                    else:
                        fin = [HN[:, 0, j, :], HN[:, 1, j, :]]
                        fk = ("ar_hn", j)
                        h0r, h0i = H0[:, 0, j, :], H0[:, 1, j, :]
                        stt("dve", fin[0], h0r, ar, set_[0][0], ALU.mult, ALU.add, [("ar_h0r", j), sk, ("s5pw", l)], [fk])
                        stt("dve", fin[0], h0i, nai, fin[0], ALU.mult, ALU.add, [("ar_h0i", j), fk, ("s5pw", l)], [fk])
                        stt("dve", fin[1], h0i, ar, set_[0][1], ALU.mult, ALU.add, [("ar_h0i", j), sk, ("s5pw", l)], [fk])
                        stt("dve", fin[1], h0r, ai, fin[1], ALU.mult, ALU.add, [("ar_h0r", j), fk, ("s5pw", l)], [fk])
                        sk = fk
                    mm(psum[:, by, :N], M[:, 2, j, :], fin[0], m == 0, False, [("ar_M", 2), sk], [PSB(by)])
                    mm(psum[:, by, :N], M[:, 3, j, :], fin[1], False, m == 3, [("ar_M", 3), sk], [PSB(by)])
                stt("dve", ZF[:, ct, :N], U[:, ct, :N], s5d[:, l, 0, ct:ct + 1], psum[:, by, :N], ALU.mult, ALU.add,
                    [("ar_U", ct), ("s5d", l), PSB(by)], [("ar_zf", ct)])
                act(ZF[:, ct, :N], ZF[:, ct, :N], AF.Gelu_apprx_tanh, [("ar_zf", ct)], [("ar_zf", ct)])
                copy("dve", ZB[:, ct, :N], ZF[:, ct, :N], [("ar_zf", ct)], [("ar_zb", ct)])

            def cons_glu(i, ps, pk):
                sg = HB[0][0][0]
                act(sg[:, :N], ps, AF.Sigmoid, [pk, ("s5d", l)], [("ar_hb", 0)], bias=s5d[:, l, 1, i:i + 1])
                tt("dve", ycat[:, 4 + i, :N], ZF[:, i, :N], sg[:, :N], ALU.mult, [("ar_zf", i), ("ar_hb", 0)],
                   [("ycat", 4 + i)])

            dense(Wd['s5_glu_w'][l], 4, lambda kc: (ZB[:, kc, :N], ("ar_zb", kc)), [(c * 128, 128) for c in range(4)],
                  cons_glu, N, "s5glu", ckey=("s5glu", l))
            if mode == "p" and ti == NT - 1:
                store_fm(SPo['s5_re'][l].rearrange("g n -> (g n)"), s5h[:, l, :, 0], 16, [("s5h", l)])
                store_fm(SPo['s5_im'][l].rearrange("g n -> (g n)"), s5h[:, l, :, 1], 16, [("s5h", l)])
            if mode == "s":
                sample_out(SSo['s5_re'][l].rearrange("b g n -> b (g n)"), lambda j, w: HN[:w, 0, j, :], 2048,
                           lambda j: ("ar_hn", j))
                sample_out(SSo['s5_im'][l].rearrange("b g n -> b (g n)"), lambda j, w: HN[:w, 1, j, :], 2048,
                           lambda j: ("ar_hn", j))
            barrier("x")

        from_mix["s5"] = mix_s5

    if "lru" in mixers:
        lrh = P.sb("lrh", [128, DEPTH, 4, 3], F32)
        lrs = P.sb("lrs", [128, DEPTH, 4], F32)
        lrp = P.sb("lrp", [128, DEPTH, 4, 8], F32)
        P.op("dve", lambda e: e.memset(lrh[:], 0.0), w=["lrh"])
        P.op("dve", lambda e: e.memset(lrs[:], 0.0), w=["lrs"])
        for l in range(DEPTH):
            for k in range(4):
                load_fm(lrp[:, l, :, k], Wd['lru_conv_w'][l, k], 4, ("lrp", l, k))
            for k, nm in enumerate(['lru_conv_b', 'lru_br', 'lru_bi', 'lru_lambda']):
                load_fm(lrp[:, l, :, 4 + k], Wd[nm][l], 4, ("lrp", l, 4 + k))
            act(lrp[:, l, :, 7], lrp[:, l, :, 7], AF.Exp, [("lrp", l, 7)], [("lrp", l, 7)], scale=-1.0)
            act(lrp[:, l, :, 7], lrp[:, l, :, 7], AF.Ln, [("lrp", l, 7)], [("lrp", l, 7)], bias=1.0)
            ts("dve", lrp[:, l, :, 7], lrp[:, l, :, 7], -8.0, None, ALU.mult, None, [("lrp", l, 7)], [("lrp", l, 7)])

        def mix_lru(l, N, mode, ti):
            o = 0
            xe = [A_(o + c * (N + 3), N + 3) for c in range(4)]
            o += 4 * (N + 3)
            xc = [A_(o + c * N, N) for c in range(4)]
            o += 4 * N
            hb = [A_(o + c * N, N) for c in range(4)]
            o += 4 * N
            tmp = [A_(o + k * N, N) for k in range(8)]
            o += 8 * N
            lrwa = A_(o, 1024).rearrange("p (w c m) -> p w c m", w=2, c=4)
            o += 1024
            P.op("dve", lambda e: e.memset(lrwa, 0.0), w=["ar_lrw"])
            for wi_, nm in enumerate(['lru_wr', 'lru_wi']):
                for c in range(4):
                    for h2 in range(2):
                        P.dma("sp", lrwa[64 * h2:64 * h2 + 64, wi_, c, 64 * h2:64 * h2 + 64], Wd[nm][l, 2 * c + h2],
                              r=[], w=[("ar_lrw", wi_, c)], dkey=("lrw", h2))
            if mode == "s":
                hist = A_(o, 12 * NS).rearrange("p (j n) -> p j n", j=12)
                o += 12 * NS
                h0 = A_(o, 4 * NS).rearrange("p (j n) -> p j n", j=4)
                o += 4 * NS
                sample_in(hist, Sin['lru_conv'][l].rearrange("b w f -> b (w f)"), 1536, "ar_lhist")
                sample_in(h0, Sin['lru_h'][l], 512, "ar_lh0")
                for w_ in range(2):
                    P.dma("sp", SSo['lru_conv'][l, :, w_, :], Sin['lru_conv'][l, :, w_ + 1, :], r=[],
                          w=[("o_lruc", w_)])

            def cons(i, ps, pk):
                if i < 4:
                    c = i
                    w0, w1, w2, w3, cb, br, bi, cc_ = (lrp[:, l, c, k:k + 1] for k in range(8))
                    X, XK = xc[c], ("ar_lxc", c)
                    if mode == "p":
                        u, uk = xe[c], ("ar_lxe", c)
                        copy("act", u[:, 3:3 + N], ps, [pk], [uk])
                        copy("dve", u[:, 0:3], lrh[:, l, c, :], [("lrh", l, c)], [uk])
                        copy("dve", lrh[:, l, c, :], u[:, N:N + 3], [uk], [("lrh", l, c)])
                        ts("dve", X, u[:, 3:3 + N], w3, cb, ALU.mult, ALU.add, [uk, ("lrp", l)], [XK])
                        stt("dve", X, u[:, 2:2 + N], w2, X, ALU.mult, ALU.add, [uk, XK, ("lrp", l)], [XK])
                        stt("dve", X, u[:, 1:1 + N], w1, X, ALU.mult, ALU.add, [uk, XK, ("lrp", l)], [XK])
                        stt("dve", X, u[:, 0:N], w0, X, ALU.mult, ALU.add, [uk, XK, ("lrp", l)], [XK])
                    else:
                        u, uk = xe[c], ("ar_lxe", c)
                        copy("act", u[:, 0:N], ps, [pk], [uk])
                        ts("dve", X, u[:, 0:N], w3, cb, ALU.mult, ALU.add, [uk, ("lrp", l)], [XK])
                        stt("dve", X, hist[:, 8 + c, :], w2, X, ALU.mult, ALU.add, [("ar_lhist", 8 + c), XK, ("lrp", l)], [XK])
                        stt("dve", X, hist[:, 4 + c, :], w1, X, ALU.mult, ALU.add, [("ar_lhist", 4 + c), XK, ("lrp", l)], [XK])
                        stt("dve", X, hist[:, c, :], w0, X, ALU.mult, ALU.add, [("ar_lhist", c), XK, ("lrp", l)], [XK])
                    b1 = next_bank(4, 8)
                    mm(psum[:, b1, :N], lrwa[:, 0, c, :], X, True, True, [("ar_lrw", 0, c), XK], [PSB(b1)])
                    r_, i_, a_, q_ = tmp[0], tmp[1], tmp[2], tmp[3]
                    act(r_, psum[:, b1, :N], AF.Sigmoid, [PSB(b1)], [("ar_ltmp", 0)], bias=br)
                    b2 = next_bank(4, 8)
                    mm(psum[:, b2, :N], lrwa[:, 1, c, :], X, True, True, [("ar_lrw", 1, c), XK], [PSB(b2)])
                    act(i_, psum[:, b2, :N], AF.Sigmoid, [PSB(b2)], [("ar_ltmp", 1)], bias=bi)
                    act(a_, r_, AF.Exp, [("ar_ltmp", 0), ("lrp", l)], [("ar_ltmp", 2)], scale=cc_)
                    tt("dve", q_, a_, a_, ALU.mult, [("ar_ltmp", 2)], [("ar_ltmp", 3)])
                    ts("dve", q_, q_, -1.0, 1.0, ALU.mult, ALU.add, [("ar_ltmp", 3)], [("ar_ltmp", 3)])
                    act(q_, q_, AF.Sqrt, [("ar_ltmp", 3)], [("ar_ltmp", 3)])
                    tt("dve", i_, i_, X, ALU.mult, [("ar_ltmp", 1), XK], [("ar_ltmp", 1)])
                    tt("dve", q_, q_, i_, ALU.mult, [("ar_ltmp", 3), ("ar_ltmp", 1)], [("ar_ltmp", 3)])
                    H, HK = hb[c], ("ar_lh", c)
                    if mode == "p":
                        P.op("dve", lambda e: e.tensor_tensor_scan(H, a_, q_, lrs[:, l, c:c + 1], ALU.mult, ALU.add),
                             [("ar_ltmp", 2), ("ar_ltmp", 3), ("lrs", l, c)], [HK])
                        copy("dve", lrs[:, l, c:c + 1], H[:, N - 1:N], [HK], [("lrs", l, c)])
                    else:
                        tt("dve", H, a_, h0[:, c, :], ALU.mult, [("ar_ltmp", 2), ("ar_lh0", c)], [HK])
                        tt("dve", H, H, q_, ALU.add, [HK, ("ar_ltmp", 3)], [HK])
                else:
                    c = i - 4
                    g_ = tmp[4 + c % 2]
                    gk = ("ar_ltmp", 4 + c % 2)
                    act(g_, ps, AF.Gelu_apprx_tanh, [pk], [gk])
                    tt("dve", ycat[:, 12 + c, :N], hb[c], g_, ALU.mult, [("ar_lh", c), gk], [("ycat", 12 + c)])

            mt = [(OFF_LR + c * 128, 128) for c in range(8)]
            dense(Wd['w_in'][l], 16, lambda kc: (hT[:, kc, :N], ("hT", kc)), mt, cons, N, "lru", ckey=("lru", l))
            if mode == "p" and ti == NT - 1:
                store_fm(SPo['lru_h'][l], lrs[:, l, :], 4, [("lrs", l)])
                for w_ in range(3):
                    store_fm(SPo['lru_conv'][l, w_], lrh[:, l, :, w_], 4, [("lrh", l)])
            if mode == "s":
                sample_out(SSo['lru_h'][l], lambda j, w: hb[j][:w, :], 512, lambda j: ("ar_lh", j))
                sample_out(SSo['lru_conv'][l, :, 2, :], lambda j, w: xe[j][:w, 0:NS], 512, lambda j: ("ar_lxe", j))
            barrier("x")

        from_mix["lru"] = mix_lru

    stage_x = arena[:, 10240:10240 + D]

    def load_tile(src_rows, n):
        for r0 in range(0, n, 128):
            nr = min(128, n - r0)
            dma_rows("sp", stage_x[:nr, :], src_rows[r0:r0 + nr, :], D, w=["ar_stage"])
            to_fm(lambda j, w, r0=r0, nr=nr: xT[:w, j, r0:r0 + nr], stage_x[:nr, :], nr, D, ["ar_stage"],
                  lambda j: ("xT", j))

    def final_out(dst_rows, n):
        b = next_bank(4, 8)
        for c in range(16):
            s = sqb[cnt["sq"] % 2]
            sk = ("sqb", cnt["sq"] % 2)
            cnt["sq"] += 1
            act(s[:, :n], xT[:, c, :n], AF.Square, [("xT", c)], [sk])
            mm(psum[:, b, :n], ones[:], s[:, :n], c == 0, c == 15, [sk, "ones"], [PSB(b)])
        ts("dve", rstd[:, :n], psum[:, b, :n], 1.0 / D, NORM_EPS, ALU.mult, ALU.add, [PSB(b)], ["rstd"])
        act(rstd[:, :n], rstd[:, :n], AF.Sqrt, ["rstd"], ["rstd"])
        P.op("dve", lambda e: e.reciprocal(rstd[:, :n], rstd[:, :n]), ["rstd"], ["rstd"])
        for c in range(16):
            tt("dve", xT[:, c, :n], xT[:, c, :n], rstd[:, :n], ALU.mult, [("xT", c), "rstd"], [("xT", c)])
            ts("pool", xT[:, c, :n], xT[:, c, :n], fg[:, c:c + 1], None, ALU.mult, None, [("xT", c), "fg"],
               [("xT", c)])
        for r0 in range(0, n, 128):
            nr = min(128, n - r0)
            st = arena[:, 12288 + (r0 // 128 % 2) * D:12288 + (r0 // 128 % 2) * D + D]
            sk = "ar_ost%d" % (r0 // 128 % 2)
            for j0 in range(0, 16, 4):
                b = next_bank(4, 8)
                for i in range(4):
                    o_ap = psum[:nr, b, i * 128:(i + 1) * 128]
                    i_ap = xT[:, j0 + i, r0:r0 + nr]
                    P.op("pe", lambda e, o_ap=o_ap, i_ap=i_ap: e.transpose(o_ap, i_ap, ident[:, :]),
                         [("xT", j0 + i), "ident"], [PSB(b)])
                copy(ev_eng(), st[:nr, j0 * 128:(j0 + 4) * 128], psum[:nr, b, :], [PSB(b)], [(sk, j0)])
            dma_rows("sp", dst_rows[r0:r0 + nr, :], st[:nr, :], D, r=[sk], dkey=sk)

    tiles = [("p", ti) for ti in range(NT)] + ([("s", 0)] if NS > 0 else [])
    if dbg == 1:
        tiles = []
    if dbg == 2:
        tiles = tiles[:1]
    if dbg == 3:
        tiles = tiles[-1:]
    if dbg in (5, 7, 8, 9):
        tiles = tiles[:1]
    if dbg == 10:
        tiles = []
        barrier("x")
        barrier("x")
    if dbg == 11:
        tiles = []
        barrier("x")
        P.dma("sp", stage_x[:128, :], xp[0:128, :], w=["ar_stage"])
        barrier("x")
    if dbg == 13:
        tiles = []
        barrier("x")
        P.dma("sp", xT[:, 0:4, :].rearrange("p a b -> p (a b)"), xp[0:128, :], w=["xT"])
        barrier("x")
    if dbg == 14:
        tiles = []
        barrier("x")
        P.dma("sp", stage_x[:16, :], xs_in[0:16, :], w=["ar_stage"])
        barrier("x")
    if dbg == 15:
        tiles = []
        barrier("x")
        P.dma("sp", hT[:, 0:8, :].rearrange("p a b -> p (a b)").bitcast(F32), xp[0:128, :], w=["hT"])
        barrier("x")
    if dbg == 16:
        tiles = []
        barrier("x")
        P.dma("sp", A_(10240, 2048), xp[0:128, :], w=["ar_stage"])
        barrier("x")
    if dbg == 18:
        tiles = []
        barrier("x")
        hst = hT[:, 0:8, :].rearrange("p a b -> p (a b)").bitcast(F32)
        P.dma("sp", hst, xp[0:128, :], w=["hT"])
        to_fm(lambda j, w: xT[:w, j, 0:128], hst, 128, D, ["hT"], lambda j: ("xT", j))
        barrier("x")
    if dbg == 12:
        tiles = []
        P.dma("sp", stage_x[:128, :], xp[0:128, :], w=["ar_stage"])
        to_fm(lambda j, w: xT[:w, j, 0:128], stage_x[:128, :], 128, D, ["ar_stage"], lambda j: ("xT", j))
    if dbg == 6:
        tiles = tiles[-1:]
    for mode, ti in tiles:
        N = TT if mode == "p" else NS
        barrier("arena_all")
        if dbg == 8:
            pass
        elif mode == "p":
            load_tile(xp[ti * TT:(ti + 1) * TT, :], TT)
        else:
            load_tile(xs_in, NS)
        for l in range(DEPTH if dbg < 4 else 0):
            layer(l, N, mode, ti)
        barrier("arena_all")
        if dbg == 7:
            continue
        if mode == "p":
            final_out(yp[ti * TT:(ti + 1) * TT, :], TT)
        else:
            final_out(ys, NS)
    nc_out = P.finish()
    return nc_out, P


_CACHE = {}


def kernel(**inputs):
    ncores = 8
    NS = NSAMP // ncores
    if "nc" not in _CACHE:
        _CACHE["nc"] = build()[0]
    nc = _CACHE["nc"]
    f32 = lambda a: np.ascontiguousarray(np.asarray(a, dtype=np.float32))
    in_maps = []
    hot = [0, 1, 4, 5]
    zx = np.zeros((LSEQ, D), np.float32)
    zc = np.zeros((1, D), np.float32)
    for c in range(ncores):
        if c in hot:
            sq = hot.index(c)
            xpc, cpc = f32(inputs["x_prompt"][sq]), np.asarray(inputs["c_prompt"][sq:sq + 1], np.float32)
        else:
            xpc, cpc = zx, zc
        m = {"xp": xpc,
             "xs": f32(inputs["x_sample"][c * NS:(c + 1) * NS, 0, :]),
             "cc": f32(np.concatenate([cpc, inputs["c_sample"][c * NS:(c + 1) * NS]], axis=0))}
        for n in W_NAMES:
            m[n] = f32(inputs[n])
        for n in STATE_NAMES:
            m["si_" + n] = f32(inputs["state_" + n][:, c * NS:(c + 1) * NS])
        in_maps.append(m)
    res = run_bass_kernel_spmd(nc, in_maps, core_ids=list(range(ncores)))
    R = res.results
    y_prompt = np.stack([R[hot[s]]["yp"] for s in range(NSEQ_P)], axis=0)
    y_sample = np.concatenate([R[c]["ys"] for c in range(ncores)], axis=0)[:, None, :]
    outs = [y_prompt, y_sample]
    for n in STATE_NAMES:
        outs.append(np.stack([R[hot[s]]["sp_" + n] for s in range(NSEQ_P)], axis=1))
    for n in STATE_NAMES:
        outs.append(np.concatenate([R[c]["ss_" + n] for c in range(ncores)], axis=1))
    return tuple(np.ascontiguousarray(o, dtype=np.float32) for o in outs)
```
